# Optimizing a Trainium2 kernel written in Bass

```python
import math
import jax, jax.numpy as jnp
from jax import lax
import numpy as np

D_MODEL = 1024
BATCH = 4
SEQ = 8192
DEPTH = 1

BRANCH_W = D_MODEL // 2
POOL_WINDOWS = (2, 4, 8, 16)
N_POOL_GROUPS = len(POOL_WINDOWS)
POOL_GROUP_W = BRANCH_W // N_POOL_GROUPS
DILATED_GROUPS = ((128, 1), (512, 4), (2048, 16))
N_ATT_GROUPS = len(DILATED_GROUPS)
HEAD_DIM = 64
HEADS_PER_GROUP = BRANCH_W // HEAD_DIM
N_ATT_HEADS = N_ATT_GROUPS * HEADS_PER_GROUP
QKV_W = N_ATT_HEADS * HEAD_DIM
Q_BLOCK = 128
NUM_BUCKETS = 32
MAX_DISTANCE = 1024
IN_W = 2 * BRANCH_W + 3 * QKV_W + BRANCH_W + 2 * D_MODEL
RMS_EPS = 1e-6
NEG_INF = -1e30

kernel_name = "hybrid_pool_dilated_attn_gated_block"


def _rmsnorm(x, g):
    xf = x.astype(jnp.float32)
    r = lax.rsqrt(jnp.mean(xf * xf, axis=-1, keepdims=True) + RMS_EPS)
    return (xf * r).astype(x.dtype) * g


def _t5_bucket(rel):
    nb = NUM_BUCKETS // 2
    ret = (rel > 0).astype(jnp.int32) * nb
    n = jnp.abs(rel)
    max_exact = nb // 2
    nf = jnp.maximum(n, 1).astype(jnp.float32)
    large = max_exact + (jnp.log(nf / max_exact) / math.log(MAX_DISTANCE / max_exact)
                         * (nb - max_exact)).astype(jnp.int32)
    large = jnp.minimum(large, nb - 1)
    return ret + jnp.where(n < max_exact, n, large)


def _multiscale_pool(u, w_pool, pool_scale):
    B, S, _ = u.shape
    ug = u.reshape(B, S, N_POOL_GROUPS, POOL_GROUP_W)
    cs = jnp.cumsum(ug.astype(jnp.float32), axis=1)
    cs = jnp.concatenate([jnp.zeros_like(cs[:, :1]), cs], axis=1)
    pos = jnp.arange(S, dtype=jnp.int32)[:, None]
    half = jnp.array([w // 2 for w in POOL_WINDOWS], dtype=jnp.int32)[None, :]
    lo = jnp.maximum(pos - half, 0)
    hi = jnp.minimum(pos + half - 1, S - 1)
    gidx = jnp.arange(N_POOL_GROUPS, dtype=jnp.int32)[None, :]
    win_sum = cs[:, hi + 1, gidx] - cs[:, lo, gidx]
    count = (hi - lo + 1).astype(jnp.float32)[None, :, :, None]
    pooled = (win_sum / count).astype(u.dtype) - ug
    mixed = jnp.einsum('bsgc,gcd->bsgd', pooled, w_pool)
    return mixed.reshape(B, S, BRANCH_W) * pool_scale


def _dilated_attention(q, k, v, rel_bias):
    B, S = q.shape[0], q.shape[1]
    n_blocks = S // Q_BLOCK
    scale = 1.0 / math.sqrt(HEAD_DIM)
    cfg = []
    for g, (window, dil) in enumerate(DILATED_GROUPS):
        n_side = window // (2 * dil)
        offs = jnp.arange(-n_side, n_side + 1, dtype=jnp.int32) * dil
        bias = rel_bias[_t5_bucket(offs)][:, g * HEADS_PER_GROUP:(g + 1) * HEADS_PER_GROUP]
        cfg.append((offs, bias.T.astype(jnp.float32), k[:, :, g], v[:, :, g]))

    def block(i):
        s0 = i * Q_BLOCK
        qpos = s0 + jnp.arange(Q_BLOCK, dtype=jnp.int32)
        qb = lax.dynamic_slice_in_dim(q, s0, Q_BLOCK, axis=1).astype(jnp.float32) * scale
        lses, outs = [], []
        for g, (offs, bias, kg_all, vg_all) in enumerate(cfg):
            nk = offs.shape[0]
            kpos = qpos[:, None] + offs[None, :]
            valid = (kpos >= 0) & (kpos < S)
            kidx = jnp.clip(kpos, 0, S - 1).reshape(-1)
            kg = jnp.take(kg_all, kidx, axis=1).reshape(B, Q_BLOCK, nk, HEADS_PER_GROUP, HEAD_DIM)
            vg = jnp.take(vg_all, kidx, axis=1).reshape(B, Q_BLOCK, nk, HEADS_PER_GROUP, HEAD_DIM)
            s = jnp.einsum('bqhd,bqkhd->bhqk', qb[:, :, g], kg.astype(jnp.float32))
            s = s + bias[None, :, None, :]
            s = jnp.where(valid[None, None], s, NEG_INF)
            lse = jax.nn.logsumexp(s, axis=-1)
            p = jnp.exp(s - lse[..., None])
            outs.append(jnp.einsum('bhqk,bqkhd->bqhd', p, vg.astype(jnp.float32)))
            lses.append(lse)
        wts = jax.nn.softmax(jnp.stack(lses, axis=0), axis=0)
        wts = jnp.transpose(wts, (0, 1, 3, 2))[..., None]
        out = jnp.sum(wts * jnp.stack(outs, axis=0), axis=0)
        return out.astype(q.dtype)

    o = lax.map(block, jnp.arange(n_blocks, dtype=jnp.int32))
    o = jnp.transpose(o, (1, 0, 2, 3, 4))
    return o.reshape(B, S, HEADS_PER_GROUP * HEAD_DIM)


def setup_inputs(seed: int = 0) -> dict:
    key = jax.random.key(seed)
    ks = jax.random.split(key, 12)
    f32 = jnp.float32
    x = jax.random.normal(ks[0], (BATCH, SEQ, D_MODEL), f32)
    norm_gain = 1.0 + 0.05 * jax.random.normal(ks[1], (DEPTH, D_MODEL), f32)
    w_in = jax.random.normal(ks[2], (DEPTH, D_MODEL, IN_W), f32) * D_MODEL ** -0.5
    b_gate = 0.01 * jax.random.normal(ks[3], (DEPTH, 2, D_MODEL), f32)
    rel_bias = 0.5 * jax.random.normal(ks[4], (NUM_BUCKETS, N_ATT_HEADS), f32)
    w_pool = jax.random.normal(ks[5], (DEPTH, N_POOL_GROUPS, POOL_GROUP_W, POOL_GROUP_W), f32) * POOL_GROUP_W ** -0.5
    pool_scale = 1.0 + 0.1 * jax.random.normal(ks[6], (DEPTH, BRANCH_W), f32)
    w_proj_a = jax.random.normal(ks[7], (DEPTH, BRANCH_W, D_MODEL), f32) * BRANCH_W ** -0.5
    w_proj_b = jax.random.normal(ks[8], (DEPTH, BRANCH_W, D_MODEL), f32) * BRANCH_W ** -0.5
    w_out = jax.random.normal(ks[9], (DEPTH, D_MODEL, D_MODEL), f32) * D_MODEL ** -0.5
    final_gain = 1.0 + 0.05 * jax.random.normal(ks[10], (D_MODEL,), f32)
    return {"x": x, "norm_gain": norm_gain, "w_in": w_in, "b_gate": b_gate,
            "rel_bias": rel_bias, "w_pool": w_pool, "pool_scale": pool_scale,
            "w_proj_a": w_proj_a, "w_proj_b": w_proj_b, "w_out": w_out,
            "final_gain": final_gain}


def reference(x, norm_gain, w_in, b_gate, rel_bias, w_pool, pool_scale,
              w_proj_a, w_proj_b, w_out, final_gain):
    B, S, D = x.shape
    splits = np.cumsum([BRANCH_W, BRANCH_W, QKV_W, QKV_W, QKV_W, BRANCH_W]).tolist()
    for l in range(DEPTH):
        h = _rmsnorm(x, norm_gain[l])
        z = h @ w_in[l]
        a_in, a_gate, q, k, v, b_gp, g_logits = jnp.split(z, splits, axis=-1)
        y_a = (_multiscale_pool(a_in, w_pool[l], pool_scale[l]) * jax.nn.silu(a_gate)) @ w_proj_a[l]
        shp = (B, S, N_ATT_GROUPS, HEADS_PER_GROUP, HEAD_DIM)
        att = _dilated_attention(q.reshape(shp), k.reshape(shp), v.reshape(shp), rel_bias)
        y_b = (att * jax.nn.silu(b_gp)) @ w_proj_b[l]
        gates = jax.nn.sigmoid(g_logits.reshape(B, S, 2, D) + b_gate[l])
        merged = gates[:, :, 0] * y_a + gates[:, :, 1] * y_b
        x = x + merged @ w_out[l]
    return _rmsnorm(x, final_gain)
```

```python
import math
from contextlib import ExitStack

import numpy as np
import ml_dtypes

import concourse.bass as bass
import concourse.mybir as mybir
from concourse.bass_utils import run_bass_kernel_spmd

F32 = mybir.dt.float32
BF16 = mybir.dt.bfloat16
ALU = mybir.AluOpType
AF = mybir.ActivationFunctionType

D = 1024
SEQ = 8192
NCORES = 8
T = 512
NCH = 12
EXT = NCH * T
MAIN0, MAIN1 = 2, 10
COL_AIN, COL_AGATE, COL_Q, COL_K, COL_V, COL_BGP, COL_G = 0, 512, 1024, 2560, 4096, 5632, 6144
DIL = (1, 4, 16)
NEG = -30000.0
EPS = 1e-6

BLK = {}
_blocks = []


def _add(name, segs):
    BLK[name] = len(_blocks)
    _blocks.append(segs)


for g in range(3):
    for hb in range(2):
        _add(f"K{g}{hb}", [("win", COL_K + 512 * g + 256 * hb), ("win", COL_K + 512 * g + 256 * hb + 128)])
        _add(f"V{g}{hb}", [("win", COL_V + 512 * g + 256 * hb), ("win", COL_V + 512 * g + 256 * hb + 128)])
for pr in range(4):
    _add(f"QA{pr}", [("win", COL_Q + 128 * pr), ("win", COL_Q + 512 + 128 * pr)])
    _add(f"QB{pr}", [("win", COL_Q + 1024 + 128 * pr), ("win", COL_BGP + 128 * pr)])
for hb in range(2):
    _add(f"AI{hb}", [("win", COL_AIN + 256 * hb), ("win", COL_AIN + 256 * hb + 128)])
    _add(f"AG{hb}", [("win", COL_AGATE + 256 * hb), ("win", COL_AGATE + 256 * hb + 128)])
for q in range(4):
    _add(f"GA{q}", [("win", COL_G + 256 * q), ("win", COL_G + 256 * q + 128)])
    _add(f"GB{q}", [("win", COL_G + 1024 + 256 * q), ("win", COL_G + 1024 + 256 * q + 128)])
for q in range(4):
    _add(f"PAB{q}", [("pab", 256 * q)])
    _add(f"WO{q}", [("wout", 256 * q)])
NBLK = len(_blocks)


class Sem:
    def __init__(self, h):
        self.h = h
        self.val = 0


class Eng:
    def __init__(self, name, h, sem, inorder=False):
        self.name, self.h, self.sem, self.inorder = name, h, sem, inorder
        self.waited = {}


class Buf:
    __slots__ = ("w", "r", "name")

    def __init__(self, name=""):
        self.w = None
        self.r = {}
        self.name = name


class Prog:
    def __init__(self, nc, es):
        self.nc, self.es = nc, es
        self.nsem = 0
        self.PE = Eng("pe", nc.tensor, self.sem("pe"), inorder=True)
        self.ACT = Eng("act", nc.scalar, self.sem("act"))
        self.DVE = Eng("dve", nc.vector, self.sem("dve"))
        self.POOL = Eng("pool", nc.gpsimd, self.sem("pool"))
        self.SP = Eng("sp", nc.sync, self.sem("sp"))
        self.engs = [self.PE, self.ACT, self.DVE, self.POOL, self.SP]
        self.all_sems = []

    def sem(self, name):
        self.nsem += 1
        s = Sem(self.es.enter_context(self.nc.semaphore(f"{name}_{self.nsem}")))
        if not hasattr(self, "_sems"):
            self._sems = []
        self._sems.append(s)
        return s

    def sb(self, name, shape, dt):
        return self.es.enter_context(self.nc.sbuf_tensor("s_" + name, list(shape), dt))

    def ps(self, name, shape, dt):
        return self.es.enter_context(self.nc.psum_tensor("p_" + name, list(shape), dt))

    def _waits(self, E, reads, writes):
        deps = {}

        def need(ev):
            if ev is None:
                return
            s, v = ev
            if deps.get(s, 0) < v:
                deps[s] = v

        for b in reads:
            need(b.w)
        for b in writes:
            need(b.w)
            for s, v in b.r.items():
                need((s, v))
        for s, v in deps.items():
            if s is E.sem and E.inorder:
                continue
            if E.waited.get(s, 0) < v:
                E.h.wait_ge(s.h, v)
                E.waited[s] = v

    def _record(self, ev, reads, writes):
        s, v = ev
        for b in reads:
            if b.r.get(s, 0) < v:
                b.r[s] = v
        for b in writes:
            b.w = ev
            b.r = {}

    def op(self, E, fn, reads=(), writes=(), signal=True):
        self._waits(E, reads, writes)
        ins = fn()
        if signal:
            E.sem.val += 1
            ins.then_inc(E.sem.h, 1)
            ev = (E.sem, E.sem.val)
        else:
            ev = (E.sem, E.sem.val + 1)
        self._record(ev, reads, writes)
        return ins

    def dma(self, E, out, in_, sem, reads=(), writes=(), **kw):
        self._waits(E, reads, writes)
        ins = E.h.dma_start(out=out, in_=in_, **kw)
        sem.val += 16
        ins.then_inc(sem.h, 16)
        self._record((sem, sem.val), reads, writes)
        return ins

    def barrier(self):
        for E in self.engs:
            for s in self._sems:
                if s.val > 0 and E.waited.get(s, 0) < s.val:
                    E.h.wait_ge(s.h, s.val)
                    E.waited[s] = s.val


def build_program(debug=False):
    nc = bass.Bass("TRN2", target_bir_lowering=False)
    dram = lambda n, s, dt, k="ExternalInput": nc.dram_tensor(n, list(s), dt, kind=k)
    x_d = dram("x", [EXT, D], F32)
    win_d = dram("w_in", [D, 8192], F32)
    wpa_d = dram("w_proj_a", [512, D], F32)
    wpb_d = dram("w_proj_b", [512, D], F32)
    wout_d = dram("w_out", [D, D], F32)
    wpool_d = dram("w_pool", [4, 128, 128], F32)
    ng_d = dram("norm_gain", [D], F32)
    fg_d = dram("final_gain", [D], F32)
    bg_d = dram("b_gate", [2, D], F32)
    psc_d = dram("pool_scale", [512], F32)
    rb_d = dram("rel_bias", [32, 24], F32)
    ident_d = dram("ident", [128, 128], BF16)
    onehot_d = dram("onehot", [33, 7 * 512], F32)
    band_d = dram("band", [128, 12 * 144], BF16)
    valid_d = dram("valid", [128, NCH], F32)
    sel2_d = dram("sel2", [128, 256], F32)
    out_d = dram("out", [8 * T, D], F32, "ExternalOutput")
    wb_d = dram("wb_scr", [NBLK, 128, 8 * 256], BF16, "Internal")
    pd_d = dram("pd_scr", [56, 128 * 512], BF16, "Internal")
    mk_d = dram("mk_scr", [8, 128, 256 + 384 + 640], BF16, "Internal")

    es = ExitStack()
    P = Prog(nc, es)
    PE, ACT, DVE, POOL, SP = P.PE, P.ACT, P.DVE, P.POOL, P.SP
    dbg_sem = P.sem("dbg")
    DBG_CHUNK = 2

    def dump(name, ap, reads, dt=None):
        if not debug:
            return
        shape = list(ap.shape)
        t = nc.dram_tensor("dbg_" + name, shape, dt or ap.dtype, kind="ExternalOutput")
        P.dma(SP, t.ap(), ap, dbg_sem, reads=reads)

    pj = [P.ps(f"pj{i}", [128, 512], F32) for i in range(2)]
    pjb = [Buf(f"pj{i}") for i in range(2)]
    st = [P.ps(f"st{i}", [128, 512], F32) for i in range(2)]
    stb = [Buf(f"st{i}") for i in range(2)]
    o0s = [P.ps(f"o0_{i}", [128, 512], F32) for i in range(2)]
    o0bs = [Buf() for _ in range(2)]
    o12s = [P.ps(f"o12_{i}", [128, 512], F32) for i in range(2)]
    o12bs = [Buf() for _ in range(2)]
    rot = {"pj": 0, "st": 0, "wide": True}
    wide_banks = [(pj[0], pjb[0]), (pj[1], pjb[1]), (st[0], stb[0]), (st[1], stb[1]),
                  (o0s[0], o0bs[0]), (o0s[1], o0bs[1]), (o12s[0], o12bs[0]), (o12s[1], o12bs[1])]

    def next_pj():
        lst = wide_banks if rot["wide"] else wide_banks[0:2]
        i = rot["pj"] % len(lst)
        rot["pj"] = i + 1
        return lst[i]

    st4 = [(st[0], stb[0]), (st[1], stb[1]), (pj[0], pjb[0]), (pj[1], pjb[1])]

    def next_st():
        i = rot["st"]
        rot["st"] = (i + 1) % 4
        return st4[i]

    with ExitStack() as es0:
        NST = 4
        stg = [es0.enter_context(nc.sbuf_tensor(f"stg{i}", [128, 8, 256], F32)) for i in range(NST)]
        stgb = [Buf() for _ in range(NST)]
        cvt = [es0.enter_context(nc.sbuf_tensor(f"cvt{i}", [128, 8 * 256], BF16)) for i in range(NST)]
        cvtb = [Buf() for _ in range(NST)]
        ld_sem = [P.sem("p0ld") for _ in range(NST)]
        st_sem = [P.sem("p0st") for _ in range(NST)]
        cv_engs = [DVE, ACT, POOL]
        for bi, segs in enumerate(_blocks):
            k = bi % NST
            if segs[0][0] == "win":
                for si, (_, c0) in enumerate(segs):
                    src = win_d.ap()[:, c0:c0 + 128].rearrange("(kc p) c -> p kc c", p=128)
                    P.dma(SP, stg[k][:, :, si * 128:(si + 1) * 128], src, ld_sem[k], writes=[stgb[k]] if si == 0 else [])
                stgb[k].w = (ld_sem[k], ld_sem[k].val)
            elif segs[0][0] == "pab":
                c0 = segs[0][1]
                P.dma(SP, stg[k][:, 0:4, :], wpa_d.ap()[:, c0:c0 + 256].rearrange("(kc p) c -> p kc c", p=128),
                      ld_sem[k], writes=[stgb[k]])
                P.dma(SP, stg[k][:, 4:8, :], wpb_d.ap()[:, c0:c0 + 256].rearrange("(kc p) c -> p kc c", p=128),
                      ld_sem[k], writes=[])
                stgb[k].w = (ld_sem[k], ld_sem[k].val)
            else:
                c0 = segs[0][1]
                P.dma(SP, stg[k][:, :, :], wout_d.ap()[:, c0:c0 + 256].rearrange("(kc p) c -> p kc c", p=128),
                      ld_sem[k], writes=[stgb[k]])
            E = cv_engs[bi % 3]
            src_ap = stg[k][:, :, :].rearrange("p a b -> p (a b)")
            if E is ACT:
                P.op(E, lambda s=src_ap, o=cvt[k]: nc.scalar.copy(out=o[:, :], in_=s), reads=[stgb[k]], writes=[cvtb[k]])
            else:
                P.op(E, lambda s=src_ap, o=cvt[k], e=E: e.h.tensor_copy(out=o[:, :], in_=s), reads=[stgb[k]], writes=[cvtb[k]])
            P.dma(POOL, wb_d.ap()[bi], cvt[k][:, :], st_sem[k], reads=[cvtb[k]])
        P.barrier()

    stat = P.sb("stat", [128, 16], F32)
    statb = Buf()
    ident = P.sb("ident", [128, 128], BF16)
    band = P.sb("band", [128, 12, 144], BF16)
    valid = P.sb("valid", [128, NCH], F32)
    gain = P.sb("gain", [128, 8], F32)
    hbias = P.sb("hbias", [128, 16], F32)
    psh = P.sb("psh", [128, 4], F32)
    fgb = P.sb("fgb", [128, D], F32)
    wpool = P.sb("wpool", [128, 4, 128], BF16)
    fstat = P.sb("fstat", [128, 8], F32)
    fstatb = Buf()
    sel2 = P.sb("sel2", [128, 256], F32)
    negh = P.sb("negh", [128, 4], F32)
    constb = Buf()
    MKW = 256 + 384 + 640

    su_sem = P.sem("setupA")
    su_semB = P.sem("setupB")
    su_semC = P.sem("setupC")
    su_semD = P.sem("setupD")
    with ExitStack() as es1:
        rb_aug = es1.enter_context(nc.sbuf_tensor("rb_aug", [64, 24], F32))
        onehot = es1.enter_context(nc.sbuf_tensor("s_onehot", [64, 7 * 512], F32))
        pv = es1.enter_context(nc.sbuf_tensor("pv", [32, 7, 512], BF16))
        mkall = es1.enter_context(nc.sbuf_tensor("mkall", [128, 8, MKW], BF16))
        wpool_f = es1.enter_context(nc.sbuf_tensor("wpool_f", [128, 4, 128], F32))
        bg_f = es1.enter_context(nc.sbuf_tensor("bg_f", [128, 16], F32))
        ps_f = es1.enter_context(nc.sbuf_tensor("ps_f", [128, 4], F32))
        sub = Buf()
        pvb = Buf()
        pdb = Buf()
        mkb = Buf()

        P.op(POOL, lambda: nc.gpsimd.memset(rb_aug[:, :], NEG), writes=[sub])
        P.op(POOL, lambda: nc.gpsimd.memset(stat[:, :], EPS), writes=[statb])
        P.op(POOL, lambda: nc.gpsimd.memset(negh[:, :], -0.5), writes=[constb])
        P.dma(SP, rb_aug[0:32, :], rb_d.ap(), su_sem, writes=[sub])
        P.dma(SP, onehot[0:33, :], onehot_d.ap(), su_sem, writes=[])
        P.dma(SP, ident[:, :], ident_d.ap(), su_sem, writes=[])
        P.dma(SP, band[:, :, :], band_d.ap().rearrange("p (a b) -> p a b", b=144), su_sem, writes=[])
        P.dma(SP, valid[:, :], valid_d.ap(), su_sem, writes=[])
        P.dma(SP, sel2[:, :], sel2_d.ap(), su_sem, writes=[])
        P.dma(SP, gain[:, :], ng_d.ap().rearrange("(dc p) -> p dc", p=128), su_sem, writes=[], allow_slow_non_contiguous=True)
        P.dma(SP, bg_f[:, :].rearrange("p (j oc) -> p j oc", j=2),
              bg_d.ap().rearrange("j (oc p) -> p j oc", p=128), su_sem, writes=[], allow_slow_non_contiguous=True)
        P.dma(SP, ps_f[:, :], psc_d.ap().rearrange("(g p) -> p g", p=128), su_sem, writes=[], allow_slow_non_contiguous=True)
        P.dma(SP, fgb[:, :], fg_d.ap().partition_broadcast(128), su_sem, writes=[])
        P.dma(SP, wpool_f[:, :, :], wpool_d.ap().rearrange("g c d -> c g d"), su_sem, writes=[])
        sub.w = (su_sem, su_sem.val)
        constb.w = (su_sem, su_sem.val)
        P.op(DVE, lambda: nc.vector.tensor_copy(out=wpool[:, :, :], in_=wpool_f[:, :, :]), reads=[constb], writes=[constb])
        P.op(DVE, lambda: nc.vector.tensor_scalar_mul(out=hbias[:, :], in0=bg_f[:, :], scalar1=0.5), reads=[constb], writes=[constb])
        P.op(DVE, lambda: nc.vector.tensor_scalar_mul(out=psh[:, :], in0=ps_f[:, :], scalar1=0.5), reads=[constb], writes=[constb])

        SETG = (0, 1, 2, 2, 2, 2, 2)
        for sidx in range(7):
            pbank, pbb = next_pj()
            P.op(PE, lambda sidx=sidx, pbank=pbank: nc.tensor.matmul(pbank[0:24, 0:512], lhsT=rb_aug[0:33, 0:24],
                                                                     rhs=onehot[0:33, sidx * 512:(sidx + 1) * 512],
                                                                     start=True, stop=True),
                 reads=[sub], writes=[pbb])
            P.op(ACT, lambda sidx=sidx, pbank=pbank: nc.scalar.activation(out=pv[0:24, sidx, :], in_=pbank[0:24, 0:512], func=AF.Exp),
                 reads=[pbb], writes=[pvb])
        for sidx in range(7):
            g = SETG[sidx]
            dst = pd_d.ap()[8 * sidx:8 * sidx + 8, :].rearrange("h (r i) -> h r i", i=512)
            src = pv[8 * g:8 * g + 8, sidx, :].unsqueeze(1).to_broadcast([8, 128, 512])
            P.dma(SP, dst, src, su_semB, reads=[pvb], writes=[])
        pdb.w = (su_semB, su_semB.val)

        def toeplitz_src(row, off, nrow, ncol):
            return bass.AP(pd_d, row * 128 * 512 + off, [[511, nrow], [1, ncol]])

        for h in range(8):
            P.dma(SP, mkall[:, h, 0:256], toeplitz_src(h, 64, 128, 256), su_semC, reads=[pdb], writes=[])
            P.dma(SP, mkall[:, h, 256:640], toeplitz_src(8 + h, 0, 128, 384), su_semC, reads=[pdb], writes=[])
            for di in range(5):
                P.dma(SP, mkall[:, h, 640 + 128 * di:640 + 128 * di + 128], toeplitz_src(8 * (2 + di) + h, 128, 128, 128),
                      su_semC, reads=[pdb], writes=[])
        mkb.w = (su_semC, su_semC.val)
        P.dma(SP, mk_d.ap().rearrange("h p c -> p h c"), mkall[:, :, :], su_semD, reads=[mkb])
        P.barrier()

    xb = [P.sb(f"xb{t}", [128, D], F32) for t in range(4)]
    xbb = [Buf() for _ in range(4)]
    xb_sem = [P.sem("xb") for _ in range(4)]
    xs = [P.sb(f"xs{t}", [128, D], BF16) for t in range(4)]
    xsb = [Buf() for _ in range(4)]
    hT = P.sb("hT", [128, 3, 8, T], BF16)
    hTb = [Buf() for _ in range(3)]
    kT = [P.sb("kT0", [128, 4, 3, T], BF16), P.sb("kT1", [128, 4, 3, T], BF16), P.sb("kT2", [128, 4, 5, T], BF16)]
    NSLOT = (3, 3, 5)
    kTb = [[Buf() for _ in range(NSLOT[g])] for g in range(3)]
    va = [P.sb(f"va{g}", [128, NSLOT[g], 4, 8, 65], BF16) for g in range(3)]
    vab = [[Buf() for _ in range(NSLOT[g])] for g in range(3)]
    qT = P.sb("qT", [128, 3, T], BF16)
    qTb = Buf()
    wbuf = [P.sb(f"wbuf{i}", [128, 8, 256], BF16) for i in range(4)]
    wbufb = [Buf() for _ in range(4)]
    wb_sem = [P.sem("wb") for _ in range(4)]
    mk = [P.sb(f"mk{i}", [128, MKW], BF16) for i in range(2)]
    mkbb = [Buf() for _ in range(2)]
    mk_sem = [P.sem("mk") for _ in range(2)]
    NU = 6
    uring = P.sb("uring", [128, NU, 512], BF16)
    uringb = [Buf() for _ in range(NU)]
    BT = P.sb("BT", [128, 4, T], BF16)
    BTb = [Buf() for _ in range(4)]
    AT = P.sb("AT", [128, 4, T], BF16)
    ATb = [Buf() for _ in range(4)]
    E0 = [P.sb(f"E0_{i}", [128, 512], BF16) for i in range(2)]
    E0hb = [[Buf(), Buf()] for _ in range(2)]
    Em = [P.sb(f"Em_{i}", [128, 512], BF16) for i in range(2)]
    Emhb = [[Buf(), Buf()] for _ in range(2)]
    Emb = [None, None]
    s12 = P.sb("s12", [128, T], F32)
    s12b = Buf()
    att = P.sb("att", [128, T], F32)
    attb = Buf()
    th = P.sb("th", [128, T], F32)
    thb = Buf()
    sgs = P.sb("sgs", [128, 4, T], BF16)
    sgsb = [Buf() for _ in range(4)]
    pooledT = Em[0]
    sgp = P.sb("sgp", [128, T], BF16)
    sgpb = Buf()
    den_sem = P.sem("den")
    merged = P.sb("merged", [128, 8, T], BF16)
    mergedb = [Buf() for _ in range(8)]
    gy, gyb = s12, s12b
    res = AT[:, :, :].rearrange("p a b -> p (a b)").bitcast(F32)
    xres0 = BT[:, :, :].rearrange("p a b -> p (a b)").bitcast(F32)
    xres1 = sgs[:, :, :].rearrange("p a b -> p (a b)").bitcast(F32)
    resb_l = ATb
    xres_bufs = [(xres0, BTb), (xres1, sgsb)]
    xres_sems = [P.sem("xres0"), P.sem("xres1")]

    def load_xres(c, ti):
        k = (ti + 1) % 2
        xr, xrb = xres_bufs[k]
        P.dma(POOL, xr[:, :], x_d.ap()[c * T + 128 * ti:c * T + 128 * ti + 128, :], xres_sems[k], writes=xrb)
    xres_sem = P.sem("xres")
    out_sems = [P.sem("out0"), P.sem("out1")]

    P.op(POOL, lambda: nc.gpsimd.memset(att[:, :], 1.0), writes=[attb])

    wstate = {"n": 0}

    def load_block(name):
        i = wstate["n"] % 4
        wstate["n"] += 1
        P.dma(SP, wbuf[i][:, :, :].rearrange("p a b -> p (a b)"), wb_d.ap()[BLK[name]], wb_sem[i], writes=[wbufb[i]])
        return wbuf[i], wbufb[i]

    def hslot(c):
        return c % 3

    def tok_ap(c, g, idx):
        s = hslot(c)
        if g == 0:
            return lambda dc: hT[:, s, dc, 128 * idx:128 * idx + 128]
        return lambda dc: hT[:, s, dc, :].rearrange("p (m r) -> p r m", r=4)[:, idx, :]

    def stageA1a_load(c):
        for t in range(4):
            P.dma(SP, xb[t][:, :], x_d.ap()[c * T + 128 * t:c * T + 128 * t + 128, :], xb_sem[t], writes=[xbb[t]])

    def stageA1a(c, load=True):
        if load:
            stageA1a_load(c)
        for t in range(4):
            P.op(ACT, lambda t=t: nc.scalar.activation(out=xs[t][:, :], in_=xb[t][:, :], func=AF.Square,
                                                       accum_out=stat[:, t:t + 1]),
                 reads=[xbb[t]], writes=[xsb[t], statb])
        P.op(DVE, lambda: nc.vector.tensor_scalar(out=stat[:, 4:8], in0=stat[:, 0:4], scalar1=1.0 / D, scalar2=EPS,
                                                  op0=ALU.mult, op1=ALU.add),
             reads=[statb], writes=[statb])
        P.op(POOL, lambda: nc.gpsimd.tensor_tensor(out=stat[:, 8:12], in0=stat[:, 4:8], in1=negh[:, 0:4], op=ALU.pow),
             reads=[statb, constb], writes=[statb])
        for t in range(4):
            P.op(POOL, lambda t=t: nc.gpsimd.tensor_tensor(out=xs[t][:, :], in0=xb[t][:, :],
                                                           in1=stat[:, 8 + t:9 + t].to_broadcast([128, D]), op=ALU.mult),
                 reads=[xbb[t], statb], writes=[xsb[t]])

    def evac_copy(i, out, in_, reads, writes):
        if i % 2 == 0:
            P.op(ACT, lambda: nc.scalar.copy(out=out, in_=in_), reads=reads, writes=writes)
        else:
            P.op(DVE, lambda: nc.vector.tensor_copy(out=out, in_=in_), reads=reads, writes=writes)

    def kv_stage(c, g):
        s = hslot(c)
        ks = c % NSLOT[g]
        for hb in range(2):
            wbk, wbb = load_block(f"K{g}{hb}")
            for seg in range(2):
                pair = 2 * hb + seg
                bank, bb = next_pj()
                for dc in range(8):
                    P.op(PE, lambda dc=dc, bank=bank, wbk=wbk, seg=seg: nc.tensor.matmul(
                        bank[:, 0:T], lhsT=wbk[:, dc, 128 * seg:128 * seg + 128], rhs=hT[:, s, dc, :],
                        start=(dc == 0), stop=(dc == 7)),
                         reads=[wbb, hTb[s]], writes=[bb], signal=(dc == 7))
                if g == 0:
                    src, dst = bank[:, 0:T], kT[0][:, pair, ks, :]
                else:
                    src = bank[:, 0:T].rearrange("p (m r) -> p r m", r=4)
                    dst = kT[g][:, pair, ks, :].rearrange("p (r m) -> p r m", r=4)
                evac_copy(pair, dst, src, [bb], [kTb[g][ks]])
        for hb in range(2):
            wbk, wbb = load_block(f"V{g}{hb}")
            for idx in range(4):
                bank, bb = next_pj()
                tf = tok_ap(c, g, idx)
                for dc in range(8):
                    P.op(PE, lambda dc=dc, bank=bank, wbk=wbk, tf=tf: nc.tensor.matmul(
                        bank[:, 0:256], lhsT=tf(dc), rhs=wbk[:, dc, :], start=(dc == 0), stop=(dc == 7)),
                         reads=[wbb, hTb[s]], writes=[bb], signal=(dc == 7))
                dst = va[g][:, ks, idx, 4 * hb:4 * hb + 4, 0:64]
                src = bank[:, 0:256].rearrange("p (h d) -> p h d", d=64)
                evac_copy(idx + 1, dst, src, [bb], [vab[g][ks]])
        P.op(POOL, lambda: nc.gpsimd.tensor_copy(out=va[g][:, ks, :, :, 64],
                                                 in_=valid[:, c:c + 1].unsqueeze(2).to_broadcast([128, 4, 8])),
             reads=[constb], writes=[vab[g][ks]])

    def stageA1b(c):
        s = hslot(c)
        for d2 in range(4):
            bank, bb = next_pj()
            bankh = bank[:, :].bitcast(BF16)
            for dd in range(2):
                dc = 2 * d2 + dd
                for t in range(4):
                    P.op(PE, lambda dc=dc, t=t, dd=dd, bankh=bankh: nc.tensor.transpose(
                        bankh[:, dd * 512 + 128 * t:dd * 512 + 128 * t + 128], xs[t][:, 128 * dc:128 * dc + 128], ident[:, :]),
                         reads=[xsb[t], constb], writes=[bb], signal=(dd == 1 and t == 3))
            for dd in range(2):
                dc = 2 * d2 + dd
                P.op(DVE, lambda dc=dc, dd=dd, bankh=bankh: nc.vector.tensor_scalar_mul(
                    out=hT[:, s, dc, :], in0=bankh[:, dd * 512:dd * 512 + 512], scalar1=gain[:, dc:dc + 1]),
                     reads=[bb, constb], writes=[hTb[s]])
        kv_stage(c, 2)

    def u_tile(tt):
        c, ti = tt // 4, tt % 4
        s = hslot(c)
        for hb in range(2):
            wbk, wbb = load_block(f"AI{hb}")
            bank, bb = next_pj()
            for dc in range(8):
                P.op(PE, lambda dc=dc, bank=bank, wbk=wbk: nc.tensor.matmul(
                    bank[:, 0:256], lhsT=hT[:, s, dc, 128 * ti:128 * ti + 128], rhs=wbk[:, dc, :],
                    start=(dc == 0), stop=(dc == 7)),
                     reads=[wbb, hTb[s]], writes=[bb], signal=(dc == 7))
            evac_copy(hb, uring[:, tt % NU, 256 * hb:256 * hb + 256], bank[:, 0:256], [bb], [uringb[tt % NU]])

    def u_tiles(tts):
        groups = {}
        for tt in tts:
            groups.setdefault(tt, None)
        for hb in range(2):
            wbk, wbb = load_block(f"AI{hb}")
            for tt in tts:
                c, ti = tt // 4, tt % 4
                s = hslot(c)
                bank, bb = next_pj()
                for dc in range(8):
                    P.op(PE, lambda dc=dc, bank=bank, wbk=wbk, s=s, ti=ti: nc.tensor.matmul(
                        bank[:, 0:256], lhsT=hT[:, s, dc, 128 * ti:128 * ti + 128], rhs=wbk[:, dc, :],
                        start=(dc == 0), stop=(dc == 7)),
                         reads=[wbb, hTb[s]], writes=[bb], signal=(dc == 7))
                evac_copy(tt + hb, uring[:, tt % NU, 256 * hb:256 * hb + 256], bank[:, 0:256], [bb], [uringb[tt % NU]])

    estate = {"n": 0}
    mstate = {"n": 0}

    def attention_pair(c, pr):
        hbanks = []
        for h in (2 * pr, 2 * pr + 1):
            banks = []
            pb = 64 * (h % 2)
            ob = h % 2
            o0, o0b, o12, o12b = o0s[ob], o0bs[ob], o12s[ob], o12bs[ob]
            mi = h % 2
            mkt, maskb = mk[mi], mkbb[mi]
            if mstate.get(mi) != h:
                P.dma(POOL, mkt[:, :], mk_d.ap()[h], mk_sem[mi], writes=[maskb])
                mstate[mi] = h
            tiles = [(c - 1, 3, 0, 64), (c, 0, 0, 192), (c, 1, 64, 320), (c, 2, 192, 448), (c, 3, 320, 512), (c + 1, 0, 448, 512)]
            first_o0 = True
            for half in range(2):
                rec = {"st": [], "mask": [], "pv": [], "maskb": maskb, "finish": None}
                col = 0
                for ti_, (cc, t_, q0, q1) in enumerate(tiles[3 * half:3 * half + 3]):
                    n = q1 - q0
                    ks = cc % 3
                    rec["st"].append((lambda stbank, col=col, n=n, ks=ks, t_=t_, q0=q0, q1=q1, pb=pb: nc.tensor.matmul(
                        stbank[:, col:col + n], lhsT=kT[0][pb:pb + 64, pr, ks, 128 * t_:128 * t_ + 128],
                        rhs=qT[pb:pb + 64, 0, q0:q1], start=True, stop=True), [kTb[0][ks], qTb]))
                    qq0 = (c * T + q0) - (cc * T + 128 * t_) + 128
                    rec["mask"].append((col, col + n, mkt[:, qq0 - 64:qq0 - 64 + n], None))
                    rec["pv"].append((lambda em, col=col, n=n, ks=ks, t_=t_, q0=q0, q1=q1, st_=first_o0, o0=o0, h=h: nc.tensor.matmul(
                        o0[0:65, q0:q1], lhsT=va[0][:, ks, t_, h, :], rhs=em[:, col:col + n],
                        start=st_, stop=False, skip_group_check=True), [vab[0][ks]], [o0b], col))
                    first_o0 = False
                    col += n
                rec["ncols"] = col
                banks.append(rec)
            first_o12 = True
            for g, dcs in ((1, (-1, 0, 1)), (2, (-2, -1, 0, 1, 2))):
                for di, dc_ in enumerate(dcs):
                    cc = c + dc_
                    ks = cc % NSLOT[g]
                    rec = {"st": [], "mask": [], "pv": [], "maskb": maskb, "finish": None, "ncols": 512}
                    for r4 in range(4):
                        rec["st"].append((lambda stbank, r4=r4, ks=ks, g=g, pb=pb: nc.tensor.matmul(
                            stbank[:, 128 * r4:128 * r4 + 128], lhsT=kT[g][pb:pb + 64, pr, ks, 128 * r4:128 * r4 + 128],
                            rhs=qT[pb:pb + 64, g, 128 * r4:128 * r4 + 128], start=True, stop=True), [kTb[g][ks], qTb]))
                        rec["pv"].append((lambda em, r4=r4, ks=ks, g=g, st_=first_o12, o12=o12, h=h: nc.tensor.matmul(
                            o12[0:65, 128 * r4:128 * r4 + 128], lhsT=va[g][:, ks, r4, h, :], rhs=em[:, 128 * r4:128 * r4 + 128],
                            start=st_, stop=False, skip_group_check=True), [vab[g][ks]], [o12b], 128 * r4))
                        first_o12 = False
                    moff = (256 + 128 - 128 * dc_) if g == 1 else (640 + 128 * di)
                    for hf_ in range(2):
                        rec["mask"].append((256 * hf_, 256 * hf_ + 256, mkt[:, moff:moff + 128].unsqueeze(1).to_broadcast([128, 2, 128]), 128))
                    banks.append(rec)

            def finish(h=h, pb=pb, o0=o0, o0b=o0b, o12=o12, o12b=o12b):
                P.op(ACT, lambda: nc.scalar.copy(out=s12[0:65, :].rearrange("p (m r) -> p r m", r=4),
                                                 in_=o12[0:65, :].rearrange("p (r m) -> p r m", r=4)),
                     reads=[o12b], writes=[s12b])
                if c == DBG_CHUNK:
                    dump(f"s12_h{h}", s12[0:65, :], [s12b])
                P.op(DVE, lambda: nc.vector.tensor_tensor(out=BT[pb:pb + 64, pr, :], in0=o0[0:64, :], in1=s12[0:64, :], op=ALU.add),
                     reads=[o0b, s12b], writes=[BTb[pr]])
                P.op(DVE, lambda: nc.vector.tensor_tensor(out=s12[64:65, :], in0=o0[64:65, :], in1=s12[64:65, :], op=ALU.add),
                     reads=[o0b, s12b], writes=[s12b])
                row = 32 * (h // 4) + (h % 4)
                P.dma(POOL, att[row:row + 1, :], s12[64:65, :], den_sem, reads=[s12b], writes=[attb])
                hn = (h + 2) % 8
                P.dma(POOL, mk[h % 2][:, :], mk_d.ap()[hn], mk_sem[h % 2], writes=[mkbb[h % 2]])
                mstate[h % 2] = hn
            banks[-1]["finish"] = finish
            hbanks.append(banks)

        def emit_st(stage):
            recs = [hbanks[0][stage], hbanks[1][stage]]
            for rec in recs:
                rec["stbank"], rec["stbb"] = next_st()
            nst = len(recs[0]["st"])
            for i in range(nst):
                for rec in recs:
                    fn, reads = rec["st"][i]
                    P.op(PE, lambda fn=fn, rec=rec: fn(rec["stbank"]), reads=reads, writes=[rec["stbb"]], signal=(i == nst - 1))

        def emit_rest(rec):
            k = estate["n"] % 2
            estate["n"] += 1
            stbank, stbb, ncols, maskb = rec["stbank"], rec["stbb"], rec["ncols"], rec["maskb"]
            for hf_ in range(2):
                lo, hi = 256 * hf_, min(256 * hf_ + 256, ncols)
                if hi <= lo:
                    continue
                P.op(ACT, lambda lo=lo, hi=hi: nc.scalar.activation(out=E0[k][:, lo:hi], in_=stbank[:, lo:hi], func=AF.Exp, scale=0.125),
                     reads=[stbb], writes=[E0hb[k][hf_]])
                for (c0, c1, m_ap, shape3) in rec["mask"]:
                    if not (lo <= c0 and c1 <= hi):
                        continue
                    if shape3 is None:
                        o_ap, i_ap = Em[k][:, c0:c1], E0[k][:, c0:c1]
                    else:
                        o_ap = Em[k][:, c0:c1].rearrange("p (a b) -> p a b", b=shape3)
                        i_ap = E0[k][:, c0:c1].rearrange("p (a b) -> p a b", b=shape3)
                    P.op(DVE, lambda o_ap=o_ap, i_ap=i_ap, m_ap=m_ap: nc.vector.tensor_tensor(out=o_ap, in0=i_ap, in1=m_ap, op=ALU.mult),
                         reads=[E0hb[k][hf_], maskb], writes=[Emhb[k][hf_]])
                pvs = [p for p in rec["pv"] if lo <= p[3] < hi]
                for i, (fn, reads, writes, _c0) in enumerate(pvs):
                    P.op(PE, lambda fn=fn: fn(Em[k]), reads=[Emhb[k][hf_]] + reads, writes=writes, signal=(i == len(pvs) - 1))
            if rec["finish"] is not None:
                rec["finish"]()

        nstage = len(hbanks[0])
        emit_st(0)
        for n in range(nstage):
            if n + 1 < nstage:
                emit_st(n + 1)
            emit_rest(hbanks[0][n])
            emit_rest(hbanks[1][n])

    def attention_recip(c):
        P.op(DVE, lambda: nc.vector.reciprocal(out=att[0:36, :], in_=att[0:36, :]), reads=[attb], writes=[attb])

    def attention_normalise(c):
        for pr in range(4):
            base = 32 * (pr // 2)
            a = pr % 2
            bcbank, bcb = next_pj()
            P.op(PE, lambda: nc.tensor.matmul(bcbank[:, 0:T], lhsT=sel2[base:base + 4, 128 * a:128 * a + 128],
                                              rhs=att[base:base + 4, :], start=True, stop=True),
                 reads=[attb, constb], writes=[bcb])
            P.op(DVE, lambda: nc.vector.scalar_tensor_tensor(out=th[:, :], in0=bcbank[:, 0:T], scalar=0.5,
                                                             in1=sgs[:, pr, :], op0=ALU.mult, op1=ALU.mult),
                 reads=[bcb, sgsb[pr]], writes=[thb])
            P.op(POOL, lambda: nc.gpsimd.tensor_tensor(out=BT[:, pr, :], in0=BT[:, pr, :], in1=th[:, :], op=ALU.mult),
                 reads=[thb, BTb[pr]], writes=[BTb[pr]])

    def silu_half(bank, bb, dst, dstb):
        P.op(ACT, lambda: nc.scalar.activation(out=th[:, :], in_=bank[:, 0:T], func=AF.Tanh, scale=0.5), reads=[bb], writes=[thb])
        P.op(DVE, lambda: nc.vector.scalar_tensor_tensor(out=dst, in0=th[:, :], scalar=1.0, in1=bank[:, 0:T],
                                                         op0=ALU.add, op1=ALU.mult),
             reads=[thb, bb], writes=[dstb])

    def proj_fm(wbk, wbb, seg, s):
        bank, bb = next_pj()
        for dc in range(8):
            P.op(PE, lambda dc=dc: nc.tensor.matmul(bank[:, 0:T], lhsT=wbk[:, dc, 128 * seg:128 * seg + 128],
                                                    rhs=hT[:, s, dc, :], start=(dc == 0), stop=(dc == 7)),
                 reads=[wbb, hTb[s]], writes=[bb], signal=(dc == 7))
        return bank, bb

    def stageB(c):
        s = hslot(c)
        rot["wide"] = False
        for pr in range(4):
            wa, wab = load_block(f"QA{pr}")
            wq, wqb = load_block(f"QB{pr}")
            for g in range(3):
                wbk, wbb, seg = (wa, wab, g) if g < 2 else (wq, wqb, 0)
                bank, bb = proj_fm(wbk, wbb, seg, s)
                if g == 0:
                    src, dst = bank[:, 0:T], qT[:, 0, :]
                else:
                    src = bank[:, 0:T].rearrange("p (m r) -> p r m", r=4)
                    dst = qT[:, g, :].rearrange("p (r m) -> p r m", r=4)
                evac_copy(g, dst, src, [bb], [qTb])
            bank, bb = proj_fm(wq, wqb, 1, s)
            silu_half(bank, bb, sgs[:, pr, :], sgsb[pr])
            if c == DBG_CHUNK:
                dump(f"qT_p{pr}", qT[:, :, :], [qTb])
                dump(f"sg_p{pr}", sgs[:, pr, :], [sgsb[pr]])
                if pr == 0:
                    dump("hT", hT[:, s, :, :], [hTb[s]])
                    for g in range(3):
                        dump(f"kT{g}", kT[g][:, :, :, :], kTb[g])
                        dump(f"va{g}", va[g][:, :, :, :, :].rearrange("p a b c d -> p (a b c d)"), vab[g])
            attention_pair(c, pr)
        rot["wide"] = True
        attention_recip(c)
        pbufs = [(Em[0], Emhb[0]), (Em[1], Emhb[1]), (E0[0], E0hb[0]), (E0[1], E0hb[1])]
        bands = []
        for g in range(4):
            bank, bb = next_pj()
            first = True
            for tp in range(-1, 5):
                tt = 4 * c + tp
                q0, q1 = max(0, 128 * tp - 8), min(T, 128 * tp + 136)
                qq0 = q0 - (128 * tp - 8)
                kind = 1
                if c == MAIN0 and tp == 0:
                    kind = 0
                if c == MAIN1 - 1 and tp == 3:
                    kind = 2
                P.op(PE, lambda tt=tt, q0=q0, q1=q1, qq0=qq0, kind=kind, g=g, first=first, bank=bank: nc.tensor.matmul(
                    bank[:, q0:q1], lhsT=uring[:, tt % NU, 128 * g:128 * g + 128],
                    rhs=band[:, 4 * kind + g, qq0:qq0 + (q1 - q0)], start=first, stop=False, skip_group_check=True),
                     reads=[uringb[tt % NU], constb], writes=[bb], signal=(tp == 4))
                first = False
            bands.append((bank, bb))
        for g in range(4):
            bank, bb = bands[g]
            evac_copy(g, pbufs[g][0][:, :], bank[:, 0:T], [bb], pbufs[g][1])
            if c == DBG_CHUNK:
                dump(f"pooledT_g{g}", pbufs[g][0][:, :], pbufs[g][1])
        mg = []
        for g in range(4):
            mbank, mbb = next_pj()
            P.op(PE, lambda g=g, mbank=mbank: nc.tensor.matmul(mbank[:, 0:T], lhsT=wpool[:, g, :], rhs=pbufs[g][0][:, :], start=True, stop=True),
                 reads=pbufs[g][1] + [constb], writes=[mbb])
            if g % 2 == 0:
                wg, wgb = load_block(f"AG{g // 2}")
            gbank, gbb = proj_fm(wg, wgb, g % 2, s)
            mg.append((mbank, mbb, gbank, gbb))
        for g in range(4):
            mbank, mbb, gbank, gbb = mg[g]
            silu_half(gbank, gbb, sgp[:, :], sgpb)
            P.op(DVE, lambda g=g, mbank=mbank: nc.vector.scalar_tensor_tensor(out=AT[:, g, :], in0=mbank[:, 0:T], scalar=psh[:, g:g + 1],
                                                                             in1=sgp[:, :], op0=ALU.mult, op1=ALU.mult),
                 reads=[mbb, sgpb, constb], writes=[ATb[g]])
        attention_normalise(c)
        if c == DBG_CHUNK:
            dump("BT", BT[:, :, :], BTb)
            dump("uring", uring[:, :, :], uringb)
        if c == DBG_CHUNK:
            dump("AT", AT[:, :, :], ATb)
        load_xres(c, 0)
        for q in range(4):
            wp, wpb_ = load_block(f"PAB{q}")
            wga, wgab = load_block(f"GA{q}")
            wgb_, wgbb = load_block(f"GB{q}")
            for seg in range(2):
                oc = 2 * q + seg
                for br in range(2):
                    ybank, ybb = next_pj()
                    src_t, src_b = (AT, ATb) if br == 0 else (BT, BTb)
                    for kc in range(4):
                        P.op(PE, lambda kc=kc, br=br, src_t=src_t: nc.tensor.matmul(
                            ybank[:, 0:T], lhsT=wp[:, 4 * br + kc, 128 * seg:128 * seg + 128], rhs=src_t[:, kc, :],
                            start=(kc == 0), stop=(kc == 3)),
                             reads=[wpb_, src_b[kc]], writes=[ybb], signal=(kc == 3))
                    gw, gwb = (wga, wgab) if br == 0 else (wgb_, wgbb)
                    gbank, gbb = proj_fm(gw, gwb, seg, s)
                    P.op(ACT, lambda br=br, oc=oc, gbank=gbank: nc.scalar.activation(
                        out=th[:, :], in_=gbank[:, 0:T], func=AF.Tanh, scale=0.5, bias=hbias[:, 8 * br + oc:8 * br + oc + 1]),
                         reads=[gbb, constb], writes=[thb])
                    if br == 0:
                        P.op(DVE, lambda ybank=ybank: nc.vector.scalar_tensor_tensor(
                            out=gy[:, :], in0=th[:, :], scalar=1.0, in1=ybank[:, 0:T], op0=ALU.add, op1=ALU.mult),
                             reads=[thb, ybb], writes=[gyb])
                    else:
                        P.op(DVE, lambda ybank=ybank: nc.vector.scalar_tensor_tensor(
                            out=att[:, :], in0=th[:, :], scalar=1.0, in1=ybank[:, 0:T], op0=ALU.add, op1=ALU.mult),
                             reads=[thb, ybb], writes=[attb])
                        P.op(POOL, lambda oc=oc: nc.gpsimd.tensor_tensor(out=merged[:, oc, :], in0=gy[:, :], in1=att[:, :], op=ALU.add),
                             reads=[gyb, attb], writes=[mergedb[oc]])
        if c == DBG_CHUNK:
            dump("merged", merged[:, :, :], mergedb)
        wo = [load_block(f"WO{q}") for q in range(4)]
        load_xres(c, 1)
        for ti in range(4):
            row0 = (c - MAIN0) * T + 128 * ti
            xr, xrb = xres_bufs[(ti + 1) % 2]
            for half in range(2):
                obank, obb = next_pj()
                first = True
                for qq in range(2):
                    wbk, wbb = wo[2 * half + qq]
                    for oc in range(8):
                        last = (qq == 1 and oc == 7)
                        P.op(PE, lambda oc=oc, qq=qq, wbk=wbk, first=first, obank=obank: nc.tensor.matmul(
                            obank[:, 256 * qq:256 * qq + 256], lhsT=merged[:, oc, 128 * ti:128 * ti + 128], rhs=wbk[:, oc, :],
                            start=first, stop=False, skip_group_check=True),
                             reads=[wbb, mergedb[oc]], writes=[obb], signal=last)
                        first = False
                P.op(DVE, lambda half=half, obank=obank, xr=xr: nc.vector.scalar_tensor_tensor(
                    out=res[:, 512 * half:512 * half + 512], in0=obank[:, 0:512], scalar=0.5,
                    in1=xr[:, 512 * half:512 * half + 512], op0=ALU.mult, op1=ALU.add),
                     reads=[obb] + xrb, writes=resb_l)
            P.op(ACT, lambda: nc.scalar.activation(out=th[:, :].bitcast(BF16), in_=res[:, :], func=AF.Square, accum_out=fstat[:, 0:1]),
                 reads=resb_l, writes=[thb, fstatb])
            P.op(DVE, lambda: nc.vector.tensor_scalar(out=fstat[:, 1:2], in0=fstat[:, 0:1], scalar1=1.0 / D, scalar2=EPS,
                                                      op0=ALU.mult, op1=ALU.add),
                 reads=[fstatb], writes=[fstatb])
            P.op(POOL, lambda: nc.gpsimd.tensor_tensor(out=fstat[:, 2:3], in0=fstat[:, 1:2], in1=negh[:, 0:1], op=ALU.pow),
                 reads=[fstatb, constb], writes=[fstatb])
            P.op(POOL, lambda xr=xr: nc.gpsimd.tensor_tensor(out=xr[:, :], in0=res[:, :], in1=fgb[:, :], op=ALU.mult),
                 reads=resb_l + [constb], writes=xrb)
            P.op(POOL, lambda xr=xr: nc.gpsimd.tensor_tensor(out=xr[:, :], in0=xr[:, :], in1=fstat[:, 2:3].to_broadcast([128, D]), op=ALU.mult),
                 reads=xrb + [fstatb], writes=xrb)
            P.dma(POOL, out_d.ap()[row0:row0 + 128, :], xr[:, :], out_sems[(ti + 1) % 2], reads=xrb)
            if ti + 2 < 4:
                load_xres(c, ti + 2)

    P.op(POOL, lambda: nc.gpsimd.memset(kT[2][:, :, 0:2, :], 0.0), writes=[kTb[2][0], kTb[2][1]])
    P.op(POOL, lambda: nc.gpsimd.memset(va[2][:, 0:2, :, :, :], 0.0), writes=[vab[2][0], vab[2][1]])
    for g in (0, 1):
        P.op(POOL, lambda g=g: nc.gpsimd.memset(kT[g][:, :, 1, :], 0.0), writes=[kTb[g][1]])
        P.op(POOL, lambda g=g: nc.gpsimd.memset(va[g][:, 1, :, :, :], 0.0), writes=[vab[g][1]])
    P.op(POOL, lambda: nc.gpsimd.memset(uring[:, 7 % NU, :], 0.0), writes=[uringb[7 % NU]])
    stageA1a(2); stageA1b(2)
    stageA1a(3); stageA1b(3)
    for g in (0, 1):
        kv_stage(2, g)
    u_tiles([8])
    for i in range(MAIN0, MAIN1):
        if i + 2 < NCH:
            stageA1a_load(i + 2)
        kv_stage(i + 1, 0)
        if i + 2 < NCH:
            stageA1a(i + 2, load=False)
        kv_stage(i + 1, 1)
        u_tiles([4 * i + 1, 4 * i + 2, 4 * i + 3, 4 * i + 4])
        if i + 2 < NCH:
            stageA1b(i + 2)
        stageB(i)
    for osem in out_sems:
        POOL.h.wait_ge(osem.h, osem.val)
        SP.h.wait_ge(osem.h, osem.val)
    if debug and dbg_sem.val:
        SP.h.wait_ge(dbg_sem.h, dbg_sem.val)
    es.close()
    return nc


def _t5_bucket_np(rel):
    nb = 16
    ret = (rel > 0).astype(np.int32) * nb
    n = np.abs(rel)
    max_exact = nb // 2
    nf = np.maximum(n, 1).astype(np.float32)
    large = max_exact + (np.log(nf / np.float32(max_exact)) / np.float32(math.log(1024 / max_exact))
                         * np.float32(nb - max_exact)).astype(np.int32)
    large = np.minimum(large, nb - 1)
    return ret + np.where(n < max_exact, n, large)


def _host_consts(hf):
    sign = 1 if hf == 0 else -1
    ident = np.eye(128, dtype=np.float32).astype(ml_dtypes.bfloat16)
    onehot = np.zeros((33, 7, 512), np.float32)
    idx = np.arange(512)
    for g in range(2):
        j = 128 - idx
        inwin = (idx >= 64) & (idx <= 192)
        b = _t5_bucket_np((sign * j * DIL[g]).astype(np.int32))
        b = np.where(inwin, b, 32)
        onehot[b, g, idx] = 1.0
    for di, dc_ in enumerate((-2, -1, 0, 1, 2)):
        delta = 128 - idx
        j = 32 * dc_ + delta // 4
        ok = (idx >= 1) & (idx <= 255) & (delta % 4 == 0) & (np.abs(j) <= 64)
        b = _t5_bucket_np((sign * j * DIL[2]).astype(np.int32))
        b = np.where(ok, b, 32)
        onehot[b, 2 + di, idx] = 1.0
    band = np.zeros((128, 3, 4, 144), np.float32)
    k = np.arange(128)[:, None]
    qp = (np.arange(144) - 8)[None, :]
    for g, w in enumerate((2, 4, 8, 16)):
        hw = w // 2
        if hf == 0:
            inw = ((k >= qp - hw) & (k <= qp + hw - 1)).astype(np.float32)
            cnt_first = (np.minimum(np.maximum(qp, 0), hw) + hw).astype(np.float32)
        else:
            inw = ((k >= qp - hw + 1) & (k <= qp + hw)).astype(np.float32)
            cnt_first = (hw + np.minimum(hw, np.maximum(qp, 0) + 1)).astype(np.float32)
        cnt_first = np.minimum(cnt_first, w)
        eye = (k == qp).astype(np.float32)
        cnt_mid = np.full((1, 144), float(w), np.float32)
        cnts = [cnt_first, cnt_mid, cnt_mid]
        for kind in range(3):
            band[:, kind, g, :] = inw / cnts[kind] - eye
    band = band.reshape(128, 12 * 144).astype(ml_dtypes.bfloat16)
    valid = np.ones((128, NCH), np.float32)
    valid[:, 0:2] = 0.0
    sel2 = np.zeros((128, 256), np.float32)
    for base in (0, 32):
        for r in range(4):
            for a in range(2):
                for m in range(128):
                    if r == 2 * a + m // 64:
                        sel2[base + r, 128 * a + m] = 1.0
    return ident, onehot.reshape(33, 7 * 512), band, valid, sel2


_CACHE = {}


def kernel(x, norm_gain, w_in, b_gate, rel_bias, w_pool, pool_scale, w_proj_a, w_proj_b, w_out, final_gain):
    x = np.asarray(x, np.float32)
    if "nc" not in _CACHE:
        _CACHE["nc"] = build_program()
    nc = _CACHE["nc"]
    common = {
        "w_in": np.ascontiguousarray(np.asarray(w_in, np.float32)[0]),
        "w_proj_a": np.ascontiguousarray(np.asarray(w_proj_a, np.float32)[0]),
        "w_proj_b": np.ascontiguousarray(np.asarray(w_proj_b, np.float32)[0]),
        "w_out": np.ascontiguousarray(np.asarray(w_out, np.float32)[0]),
        "w_pool": np.ascontiguousarray(np.asarray(w_pool, np.float32)[0]),
        "norm_gain": np.ascontiguousarray(np.asarray(norm_gain, np.float32)[0]),
        "final_gain": np.ascontiguousarray(np.asarray(final_gain, np.float32)),
        "b_gate": np.ascontiguousarray(np.asarray(b_gate, np.float32)[0]),
        "pool_scale": np.ascontiguousarray(np.asarray(pool_scale, np.float32)[0]),
        "rel_bias": np.ascontiguousarray(np.asarray(rel_bias, np.float32)),
    }
    in_maps = []
    for core in range(NCORES):
        b, hf = core // 2, core % 2
        xe = np.zeros((EXT, D), np.float32)
        if hf == 0:
            xe[1024:EXT] = x[b, 0:EXT - 1024]
        else:
            xe[1024:EXT] = x[b, SEQ - (EXT - 1024):SEQ][::-1]
        ident, onehot, band, valid, sel2 = _host_consts(hf)
        m = dict(common)
        m.update({"x": xe, "ident": ident, "onehot": onehot, "band": band, "valid": valid, "sel2": sel2})
        in_maps.append(m)
    res = run_bass_kernel_spmd(nc, in_maps, core_ids=list(range(NCORES)))
    out = np.empty((4, SEQ, D), np.float32)
    for core in range(NCORES):
        b, hf = core // 2, core % 2
        o = np.asarray(res.results[core]["out"], np.float32)
        out[b, hf * 4096:(hf + 1) * 4096] = o if hf == 0 else o[::-1]
    return out
```

```python
import math
from contextlib import ExitStack

import numpy as np
import ml_dtypes

import concourse.bass as bass
import concourse.mybir as mybir
from concourse.bass_utils import run_bass_kernel_spmd

F32 = mybir.dt.float32
BF16 = mybir.dt.bfloat16
ALU = mybir.AluOpType
AF = mybir.ActivationFunctionType

D = 1024
SEQ = 8192
NCORES = 8
T = 512
NCH = 12
EXT = NCH * T
MAIN0, MAIN1 = 2, 10
COL_AIN, COL_AGATE, COL_Q, COL_K, COL_V, COL_BGP, COL_G = 0, 512, 1024, 2560, 4096, 5632, 6144
DIL = (1, 4, 16)
NEG = -30000.0
EPS = 1e-6

BLK = {}
_blocks = []


def _add(name, segs):
    BLK[name] = len(_blocks)
    _blocks.append(segs)


for g in range(3):
    for hb in range(2):
        _add(f"K{g}{hb}", [("win", COL_K + 512 * g + 256 * hb), ("win", COL_K + 512 * g + 256 * hb + 128)])
        _add(f"V{g}{hb}", [("win", COL_V + 512 * g + 256 * hb), ("win", COL_V + 512 * g + 256 * hb + 128)])
for pr in range(4):
    _add(f"QA{pr}", [("win", COL_Q + 128 * pr), ("win", COL_Q + 512 + 128 * pr)])
    _add(f"QB{pr}", [("win", COL_Q + 1024 + 128 * pr), ("win", COL_BGP + 128 * pr)])
for hb in range(2):
    _add(f"AI{hb}", [("win", COL_AIN + 256 * hb), ("win", COL_AIN + 256 * hb + 128)])
    _add(f"AG{hb}", [("win", COL_AGATE + 256 * hb), ("win", COL_AGATE + 256 * hb + 128)])
for q in range(4):
    _add(f"GA{q}", [("win", COL_G + 256 * q), ("win", COL_G + 256 * q + 128)])
    _add(f"GB{q}", [("win", COL_G + 1024 + 256 * q), ("win", COL_G + 1024 + 256 * q + 128)])
for q in range(4):
    _add(f"PAB{q}", [("pab", 256 * q)])
    _add(f"WO{q}", [("wout", 256 * q)])
NBLK = len(_blocks)


class Sem:
    def __init__(self, h):
        self.h = h
        self.val = 0


class Eng:
    def __init__(self, name, h, sem, inorder=False):
        self.name, self.h, self.sem, self.inorder = name, h, sem, inorder
        self.waited = {}


class Buf:
    __slots__ = ("w", "r", "name")

    def __init__(self, name=""):
        self.w = None
        self.r = {}
        self.name = name


class Prog:
    def __init__(self, nc, es):
        self.nc, self.es = nc, es
        self.nsem = 0
        self.PE = Eng("pe", nc.tensor, self.sem("pe"), inorder=True)
        self.ACT = Eng("act", nc.scalar, self.sem("act"))
        self.DVE = Eng("dve", nc.vector, self.sem("dve"))
        self.POOL = Eng("pool", nc.gpsimd, self.sem("pool"))
        self.SP = Eng("sp", nc.sync, self.sem("sp"))
        self.engs = [self.PE, self.ACT, self.DVE, self.POOL, self.SP]
        self.all_sems = []

    def sem(self, name):
        self.nsem += 1
        s = Sem(self.es.enter_context(self.nc.semaphore(f"{name}_{self.nsem}")))
        if not hasattr(self, "_sems"):
            self._sems = []
        self._sems.append(s)
        return s

    def sb(self, name, shape, dt):
        return self.es.enter_context(self.nc.sbuf_tensor("s_" + name, list(shape), dt))

    def ps(self, name, shape, dt):
        return self.es.enter_context(self.nc.psum_tensor("p_" + name, list(shape), dt))

    def _waits(self, E, reads, writes):
        deps = {}

        def need(ev):
            if ev is None:
                return
            s, v = ev
            if deps.get(s, 0) < v:
                deps[s] = v

        for b in reads:
            need(b.w)
        for b in writes:
            need(b.w)
            for s, v in b.r.items():
                need((s, v))
        for s, v in deps.items():
            if s is E.sem and E.inorder:
                continue
            if E.waited.get(s, 0) < v:
                E.h.wait_ge(s.h, v)
                E.waited[s] = v

    def _record(self, ev, reads, writes):
        s, v = ev
        for b in reads:
            if b.r.get(s, 0) < v:
                b.r[s] = v
        for b in writes:
            b.w = ev
            b.r = {}

    def op(self, E, fn, reads=(), writes=(), signal=True):
        self._waits(E, reads, writes)
        ins = fn()
        if signal:
            E.sem.val += 1
            ins.then_inc(E.sem.h, 1)
            ev = (E.sem, E.sem.val)
        else:
            ev = (E.sem, E.sem.val + 1)
        self._record(ev, reads, writes)
        return ins

    def dma(self, E, out, in_, sem, reads=(), writes=(), **kw):
        self._waits(E, reads, writes)
        ins = E.h.dma_start(out=out, in_=in_, **kw)
        sem.val += 16
        ins.then_inc(sem.h, 16)
        self._record((sem, sem.val), reads, writes)
        return ins

    def barrier(self):
        for E in self.engs:
            for s in self._sems:
                if s.val > 0 and E.waited.get(s, 0) < s.val:
                    E.h.wait_ge(s.h, s.val)
                    E.waited[s] = s.val


def build_program(debug=False):
    nc = bass.Bass("TRN2", target_bir_lowering=False)
    dram = lambda n, s, dt, k="ExternalInput": nc.dram_tensor(n, list(s), dt, kind=k)
    x_d = dram("x", [EXT, D], F32)
    win_d = dram("w_in", [D, 8192], F32)
    wpa_d = dram("w_proj_a", [512, D], F32)
    wpb_d = dram("w_proj_b", [512, D], F32)
    wout_d = dram("w_out", [D, D], F32)
    wpool_d = dram("w_pool", [4, 128, 128], F32)
    ng_d = dram("norm_gain", [D], F32)
    fg_d = dram("final_gain", [D], F32)
    bg_d = dram("b_gate", [2, D], F32)
    psc_d = dram("pool_scale", [512], F32)
    rb_d = dram("rel_bias", [32, 24], F32)
    ident_d = dram("ident", [128, 128], BF16)
    onehot_d = dram("onehot", [33, 7 * 512], F32)
    band_d = dram("band", [128, 12 * 144], BF16)
    valid_d = dram("valid", [128, NCH], F32)
    sel2_d = dram("sel2", [128, 256], F32)
    out_d = dram("out", [8 * T, D], F32, "ExternalOutput")
    wb_d = dram("wb_scr", [NBLK, 128, 8 * 256], BF16, "Internal")
    pd_d = dram("pd_scr", [56, 128 * 512], BF16, "Internal")
    mk_d = dram("mk_scr", [8, 128, 256 + 384 + 640], BF16, "Internal")

    es = ExitStack()
    P = Prog(nc, es)
    PE, ACT, DVE, POOL, SP = P.PE, P.ACT, P.DVE, P.POOL, P.SP
    dbg_sem = P.sem("dbg")
    DBG_CHUNK = 2

    def dump(name, ap, reads, dt=None):
        if not debug:
            return
        shape = list(ap.shape)
        t = nc.dram_tensor("dbg_" + name, shape, dt or ap.dtype, kind="ExternalOutput")
        P.dma(SP, t.ap(), ap, dbg_sem, reads=reads)

    pj = [P.ps(f"pj{i}", [128, 512], F32) for i in range(2)]
    pjb = [Buf(f"pj{i}") for i in range(2)]
    st = [P.ps(f"st{i}", [128, 512], F32) for i in range(2)]
    stb = [Buf(f"st{i}") for i in range(2)]
    o0s = [P.ps(f"o0_{i}", [128, 512], F32) for i in range(2)]
    o0bs = [Buf() for _ in range(2)]
    o12s = [P.ps(f"o12_{i}", [128, 512], F32) for i in range(2)]
    o12bs = [Buf() for _ in range(2)]
    rot = {"pj": 0, "st": 0, "wide": True}
    wide_banks = [(pj[0], pjb[0]), (pj[1], pjb[1]), (st[0], stb[0]), (st[1], stb[1]),
                  (o0s[0], o0bs[0]), (o0s[1], o0bs[1]), (o12s[0], o12bs[0]), (o12s[1], o12bs[1])]

    def next_pj():
        lst = wide_banks if rot["wide"] else wide_banks[0:2]
        i = rot["pj"] % len(lst)
        rot["pj"] = i + 1
        return lst[i]

    st4 = [(st[0], stb[0]), (st[1], stb[1]), (pj[0], pjb[0]), (pj[1], pjb[1])]

    def next_st():
        i = rot["st"]
        rot["st"] = (i + 1) % 4
        return st4[i]

    with ExitStack() as es0:
        NST = 4
        stg = [es0.enter_context(nc.sbuf_tensor(f"stg{i}", [128, 8, 256], F32)) for i in range(NST)]
        stgb = [Buf() for _ in range(NST)]
        cvt = [es0.enter_context(nc.sbuf_tensor(f"cvt{i}", [128, 8 * 256], BF16)) for i in range(NST)]
        cvtb = [Buf() for _ in range(NST)]
        ld_sem = [P.sem("p0ld") for _ in range(NST)]
        st_sem = [P.sem("p0st") for _ in range(NST)]
        cv_engs = [DVE, ACT, POOL]
        for bi, segs in enumerate(_blocks):
            k = bi % NST
            if segs[0][0] == "win":
                for si, (_, c0) in enumerate(segs):
                    src = win_d.ap()[:, c0:c0 + 128].rearrange("(kc p) c -> p kc c", p=128)
                    P.dma(SP, stg[k][:, :, si * 128:(si + 1) * 128], src, ld_sem[k], writes=[stgb[k]] if si == 0 else [])
                stgb[k].w = (ld_sem[k], ld_sem[k].val)
            elif segs[0][0] == "pab":
                c0 = segs[0][1]
                P.dma(SP, stg[k][:, 0:4, :], wpa_d.ap()[:, c0:c0 + 256].rearrange("(kc p) c -> p kc c", p=128),
                      ld_sem[k], writes=[stgb[k]])
                P.dma(SP, stg[k][:, 4:8, :], wpb_d.ap()[:, c0:c0 + 256].rearrange("(kc p) c -> p kc c", p=128),
                      ld_sem[k], writes=[])
                stgb[k].w = (ld_sem[k], ld_sem[k].val)
            else:
                c0 = segs[0][1]
                P.dma(SP, stg[k][:, :, :], wout_d.ap()[:, c0:c0 + 256].rearrange("(kc p) c -> p kc c", p=128),
                      ld_sem[k], writes=[stgb[k]])
            E = cv_engs[bi % 3]
            src_ap = stg[k][:, :, :].rearrange("p a b -> p (a b)")
            if E is ACT:
                P.op(E, lambda s=src_ap, o=cvt[k]: nc.scalar.copy(out=o[:, :], in_=s), reads=[stgb[k]], writes=[cvtb[k]])
            else:
                P.op(E, lambda s=src_ap, o=cvt[k], e=E: e.h.tensor_copy(out=o[:, :], in_=s), reads=[stgb[k]], writes=[cvtb[k]])
            P.dma(POOL, wb_d.ap()[bi], cvt[k][:, :], st_sem[k], reads=[cvtb[k]])
        P.barrier()

    stat = P.sb("stat", [128, 16], F32)
    statb = Buf()
    ident = P.sb("ident", [128, 128], BF16)
    band = P.sb("band", [128, 12, 144], BF16)
    valid = P.sb("valid", [128, NCH], F32)
    gain = P.sb("gain", [128, 8], F32)
    hbias = P.sb("hbias", [128, 16], F32)
    psh = P.sb("psh", [128, 4], F32)
    fgb = P.sb("fgb", [128, D], F32)
    wpool = P.sb("wpool", [128, 4, 128], BF16)
    fstat = P.sb("fstat", [128, 8], F32)
    fstatb = Buf()
    sel2 = P.sb("sel2", [128, 256], F32)
    negh = P.sb("negh", [128, 4], F32)
    constb = Buf()
    MKW = 256 + 384 + 640

    su_sem = P.sem("setupA")
    su_semB = P.sem("setupB")
    su_semC = P.sem("setupC")
    su_semD = P.sem("setupD")
    with ExitStack() as es1:
        rb_aug = es1.enter_context(nc.sbuf_tensor("rb_aug", [64, 24], F32))
        onehot = es1.enter_context(nc.sbuf_tensor("s_onehot", [64, 7 * 512], F32))
        pv = es1.enter_context(nc.sbuf_tensor("pv", [32, 7, 512], BF16))
        mkall = es1.enter_context(nc.sbuf_tensor("mkall", [128, 8, MKW], BF16))
        wpool_f = es1.enter_context(nc.sbuf_tensor("wpool_f", [128, 4, 128], F32))
        bg_f = es1.enter_context(nc.sbuf_tensor("bg_f", [128, 16], F32))
        ps_f = es1.enter_context(nc.sbuf_tensor("ps_f", [128, 4], F32))
        sub = Buf()
        pvb = Buf()
        pdb = Buf()
        mkb = Buf()

        P.op(POOL, lambda: nc.gpsimd.memset(rb_aug[:, :], NEG), writes=[sub])
        P.op(POOL, lambda: nc.gpsimd.memset(stat[:, :], EPS), writes=[statb])
        P.op(POOL, lambda: nc.gpsimd.memset(negh[:, :], -0.5), writes=[constb])
        P.dma(SP, rb_aug[0:32, :], rb_d.ap(), su_sem, writes=[sub])
        P.dma(SP, onehot[0:33, :], onehot_d.ap(), su_sem, writes=[])
        P.dma(SP, ident[:, :], ident_d.ap(), su_sem, writes=[])
        P.dma(SP, band[:, :, :], band_d.ap().rearrange("p (a b) -> p a b", b=144), su_sem, writes=[])
        P.dma(SP, valid[:, :], valid_d.ap(), su_sem, writes=[])
        P.dma(SP, sel2[:, :], sel2_d.ap(), su_sem, writes=[])
        P.dma(SP, gain[:, :], ng_d.ap().rearrange("(dc p) -> p dc", p=128), su_sem, writes=[], allow_slow_non_contiguous=True)
        P.dma(SP, bg_f[:, :].rearrange("p (j oc) -> p j oc", j=2),
              bg_d.ap().rearrange("j (oc p) -> p j oc", p=128), su_sem, writes=[], allow_slow_non_contiguous=True)
        P.dma(SP, ps_f[:, :], psc_d.ap().rearrange("(g p) -> p g", p=128), su_sem, writes=[], allow_slow_non_contiguous=True)
        P.dma(SP, fgb[:, :], fg_d.ap().partition_broadcast(128), su_sem, writes=[])
        P.dma(SP, wpool_f[:, :, :], wpool_d.ap().rearrange("g c d -> c g d"), su_sem, writes=[])
        sub.w = (su_sem, su_sem.val)
        constb.w = (su_sem, su_sem.val)
        P.op(DVE, lambda: nc.vector.tensor_copy(out=wpool[:, :, :], in_=wpool_f[:, :, :]), reads=[constb], writes=[constb])
        P.op(DVE, lambda: nc.vector.tensor_scalar_mul(out=hbias[:, :], in0=bg_f[:, :], scalar1=0.5), reads=[constb], writes=[constb])
        P.op(DVE, lambda: nc.vector.tensor_scalar_mul(out=psh[:, :], in0=ps_f[:, :], scalar1=0.5), reads=[constb], writes=[constb])

        SETG = (0, 1, 2, 2, 2, 2, 2)
        for sidx in range(7):
            pbank, pbb = next_pj()
            P.op(PE, lambda sidx=sidx, pbank=pbank: nc.tensor.matmul(pbank[0:24, 0:512], lhsT=rb_aug[0:33, 0:24],
                                                                     rhs=onehot[0:33, sidx * 512:(sidx + 1) * 512],
                                                                     start=True, stop=True),
                 reads=[sub], writes=[pbb])
            P.op(ACT, lambda sidx=sidx, pbank=pbank: nc.scalar.activation(out=pv[0:24, sidx, :], in_=pbank[0:24, 0:512], func=AF.Exp),
                 reads=[pbb], writes=[pvb])
        for sidx in range(7):
            g = SETG[sidx]
            dst = pd_d.ap()[8 * sidx:8 * sidx + 8, :].rearrange("h (r i) -> h r i", i=512)
            src = pv[8 * g:8 * g + 8, sidx, :].unsqueeze(1).to_broadcast([8, 128, 512])
            P.dma(SP, dst, src, su_semB, reads=[pvb], writes=[])
        pdb.w = (su_semB, su_semB.val)

        def toeplitz_src(row, off, nrow, ncol):
            return bass.AP(pd_d, row * 128 * 512 + off, [[511, nrow], [1, ncol]])

        for h in range(8):
            P.dma(SP, mkall[:, h, 0:256], toeplitz_src(h, 64, 128, 256), su_semC, reads=[pdb], writes=[])
            P.dma(SP, mkall[:, h, 256:640], toeplitz_src(8 + h, 0, 128, 384), su_semC, reads=[pdb], writes=[])
            for di in range(5):
                P.dma(SP, mkall[:, h, 640 + 128 * di:640 + 128 * di + 128], toeplitz_src(8 * (2 + di) + h, 128, 128, 128),
                      su_semC, reads=[pdb], writes=[])
        mkb.w = (su_semC, su_semC.val)
        P.dma(SP, mk_d.ap().rearrange("h p c -> p h c"), mkall[:, :, :], su_semD, reads=[mkb])
        P.barrier()

    xb = [P.sb(f"xb{t}", [128, D], F32) for t in range(4)]
    xbb = [Buf() for _ in range(4)]
    xb_sem = [P.sem("xb") for _ in range(4)]
    xs = [P.sb(f"xs{t}", [128, D], BF16) for t in range(4)]
    xsb = [Buf() for _ in range(4)]
    hT = P.sb("hT", [128, 3, 8, T], BF16)
    hTb = [Buf() for _ in range(3)]
    kT = [P.sb("kT0", [128, 4, 3, T], BF16), P.sb("kT1", [128, 4, 3, T], BF16), P.sb("kT2", [128, 4, 5, T], BF16)]
    NSLOT = (3, 3, 5)
    kTb = [[Buf() for _ in range(NSLOT[g])] for g in range(3)]
    va = [P.sb(f"va{g}", [128, NSLOT[g], 4, 8, 65], BF16) for g in range(3)]
    vab = [[Buf() for _ in range(NSLOT[g])] for g in range(3)]
    qT = P.sb("qT", [128, 3, T], BF16)
    qTb = Buf()
    wbuf = [P.sb(f"wbuf{i}", [128, 8, 256], BF16) for i in range(4)]
    wbufb = [Buf() for _ in range(4)]
    wb_sem = [P.sem("wb") for _ in range(4)]
    mk = [P.sb(f"mk{i}", [128, MKW], BF16) for i in range(2)]
    mkbb = [Buf() for _ in range(2)]
    mk_sem = [P.sem("mk") for _ in range(2)]
    NU = 6
    uring = P.sb("uring", [128, NU, 512], BF16)
    uringb = [Buf() for _ in range(NU)]
    BT = P.sb("BT", [128, 4, T], BF16)
    BTb = [Buf() for _ in range(4)]
    AT = P.sb("AT", [128, 4, T], BF16)
    ATb = [Buf() for _ in range(4)]
    E0 = [P.sb(f"E0_{i}", [128, 512], BF16) for i in range(2)]
    E0hb = [[Buf(), Buf()] for _ in range(2)]
    Em = [P.sb(f"Em_{i}", [128, 512], BF16) for i in range(2)]
    Emhb = [[Buf(), Buf()] for _ in range(2)]
    Emb = [None, None]
    s12 = P.sb("s12", [128, T], F32)
    s12b = Buf()
    att = P.sb("att", [128, T], F32)
    attb = Buf()
    th = P.sb("th", [128, T], F32)
    thb = Buf()
    sgs = P.sb("sgs", [128, 4, T], BF16)
    sgsb = [Buf() for _ in range(4)]
    pooledT = Em[0]
    sgp = P.sb("sgp", [128, T], BF16)
    sgpb = Buf()
    den_sem = P.sem("den")
    merged = P.sb("merged", [128, 8, T], BF16)
    mergedb = [Buf() for _ in range(8)]
    gy, gyb = s12, s12b
    res = AT[:, :, :].rearrange("p a b -> p (a b)").bitcast(F32)
    xres0 = BT[:, :, :].rearrange("p a b -> p (a b)").bitcast(F32)
    xres1 = sgs[:, :, :].rearrange("p a b -> p (a b)").bitcast(F32)
    resb_l = ATb
    xres_bufs = [(xres0, BTb), (xres1, sgsb)]
    xres_sems = [P.sem("xres0"), P.sem("xres1")]

    def load_xres(c, ti):
        k = (ti + 1) % 2
        xr, xrb = xres_bufs[k]
        P.dma(POOL, xr[:, :], x_d.ap()[c * T + 128 * ti:c * T + 128 * ti + 128, :], xres_sems[k], writes=xrb)
    xres_sem = P.sem("xres")
    out_sems = [P.sem("out0"), P.sem("out1")]

    P.op(POOL, lambda: nc.gpsimd.memset(att[:, :], 1.0), writes=[attb])

    wstate = {"n": 0}

    def load_block(name):
        i = wstate["n"] % 4
        wstate["n"] += 1
        P.dma(SP, wbuf[i][:, :, :].rearrange("p a b -> p (a b)"), wb_d.ap()[BLK[name]], wb_sem[i], writes=[wbufb[i]])
        return wbuf[i], wbufb[i]

    def hslot(c):
        return c % 3

    def tok_ap(c, g, idx):
        s = hslot(c)
        if g == 0:
            return lambda dc: hT[:, s, dc, 128 * idx:128 * idx + 128]
        return lambda dc: hT[:, s, dc, :].rearrange("p (m r) -> p r m", r=4)[:, idx, :]

    def stageA1a_load(c):
        for t in range(4):
            P.dma(SP, xb[t][:, :], x_d.ap()[c * T + 128 * t:c * T + 128 * t + 128, :], xb_sem[t], writes=[xbb[t]])

    def stageA1a(c, load=True):
        if load:
            stageA1a_load(c)
        for t in range(4):
            P.op(ACT, lambda t=t: nc.scalar.activation(out=xs[t][:, :], in_=xb[t][:, :], func=AF.Square,
                                                       accum_out=stat[:, t:t + 1]),
                 reads=[xbb[t]], writes=[xsb[t], statb])
        P.op(DVE, lambda: nc.vector.tensor_scalar(out=stat[:, 4:8], in0=stat[:, 0:4], scalar1=1.0 / D, scalar2=EPS,
                                                  op0=ALU.mult, op1=ALU.add),
             reads=[statb], writes=[statb])
        P.op(POOL, lambda: nc.gpsimd.tensor_tensor(out=stat[:, 8:12], in0=stat[:, 4:8], in1=negh[:, 0:4], op=ALU.pow),
             reads=[statb, constb], writes=[statb])
        for t in range(4):
            P.op(POOL, lambda t=t: nc.gpsimd.tensor_tensor(out=xs[t][:, :], in0=xb[t][:, :],
                                                           in1=stat[:, 8 + t:9 + t].to_broadcast([128, D]), op=ALU.mult),
                 reads=[xbb[t], statb], writes=[xsb[t]])

    def evac_copy(i, out, in_, reads, writes):
        if i % 2 == 0:
            P.op(ACT, lambda: nc.scalar.copy(out=out, in_=in_), reads=reads, writes=writes)
        else:
            P.op(DVE, lambda: nc.vector.tensor_copy(out=out, in_=in_), reads=reads, writes=writes)

    def kv_stage(c, g):
        s = hslot(c)
        ks = c % NSLOT[g]
        for hb in range(2):
            wbk, wbb = load_block(f"K{g}{hb}")
            for seg in range(2):
                pair = 2 * hb + seg
                bank, bb = next_pj()
                for dc in range(8):
                    P.op(PE, lambda dc=dc, bank=bank, wbk=wbk, seg=seg: nc.tensor.matmul(
                        bank[:, 0:T], lhsT=wbk[:, dc, 128 * seg:128 * seg + 128], rhs=hT[:, s, dc, :],
                        start=(dc == 0), stop=(dc == 7)),
                         reads=[wbb, hTb[s]], writes=[bb], signal=(dc == 7))
                if g == 0:
                    src, dst = bank[:, 0:T], kT[0][:, pair, ks, :]
                else:
                    src = bank[:, 0:T].rearrange("p (m r) -> p r m", r=4)
                    dst = kT[g][:, pair, ks, :].rearrange("p (r m) -> p r m", r=4)
                evac_copy(pair, dst, src, [bb], [kTb[g][ks]])
        for hb in range(2):
            wbk, wbb = load_block(f"V{g}{hb}")
            for idx in range(4):
                bank, bb = next_pj()
                tf = tok_ap(c, g, idx)
                for dc in range(8):
                    P.op(PE, lambda dc=dc, bank=bank, wbk=wbk, tf=tf: nc.tensor.matmul(
                        bank[:, 0:256], lhsT=tf(dc), rhs=wbk[:, dc, :], start=(dc == 0), stop=(dc == 7)),
                         reads=[wbb, hTb[s]], writes=[bb], signal=(dc == 7))
                dst = va[g][:, ks, idx, 4 * hb:4 * hb + 4, 0:64]
                src = bank[:, 0:256].rearrange("p (h d) -> p h d", d=64)
                evac_copy(idx + 1, dst, src, [bb], [vab[g][ks]])
        P.op(POOL, lambda: nc.gpsimd.tensor_copy(out=va[g][:, ks, :, :, 64],
                                                 in_=valid[:, c:c + 1].unsqueeze(2).to_broadcast([128, 4, 8])),
             reads=[constb], writes=[vab[g][ks]])

    def stageA1b(c):
        s = hslot(c)
        for d2 in range(4):
            bank, bb = next_pj()
            bankh = bank[:, :].bitcast(BF16)
            for dd in range(2):
                dc = 2 * d2 + dd
                for t in range(4):
                    P.op(PE, lambda dc=dc, t=t, dd=dd, bankh=bankh: nc.tensor.transpose(
                        bankh[:, dd * 512 + 128 * t:dd * 512 + 128 * t + 128], xs[t][:, 128 * dc:128 * dc + 128], ident[:, :]),
                         reads=[xsb[t], constb], writes=[bb], signal=(dd == 1 and t == 3))
            for dd in range(2):
                dc = 2 * d2 + dd
                P.op(DVE, lambda dc=dc, dd=dd, bankh=bankh: nc.vector.tensor_scalar_mul(
                    out=hT[:, s, dc, :], in0=bankh[:, dd * 512:dd * 512 + 512], scalar1=gain[:, dc:dc + 1]),
                     reads=[bb, constb], writes=[hTb[s]])
        kv_stage(c, 2)

    def u_tile(tt):
        c, ti = tt // 4, tt % 4
        s = hslot(c)
        for hb in range(2):
            wbk, wbb = load_block(f"AI{hb}")
            bank, bb = next_pj()
            for dc in range(8):
                P.op(PE, lambda dc=dc, bank=bank, wbk=wbk: nc.tensor.matmul(
                    bank[:, 0:256], lhsT=hT[:, s, dc, 128 * ti:128 * ti + 128], rhs=wbk[:, dc, :],
                    start=(dc == 0), stop=(dc == 7)),
                     reads=[wbb, hTb[s]], writes=[bb], signal=(dc == 7))
            evac_copy(hb, uring[:, tt % NU, 256 * hb:256 * hb + 256], bank[:, 0:256], [bb], [uringb[tt % NU]])

    def u_tiles(tts):
        groups = {}
        for tt in tts:
            groups.setdefault(tt, None)
        for hb in range(2):
            wbk, wbb = load_block(f"AI{hb}")
            for tt in tts:
                c, ti = tt // 4, tt % 4
                s = hslot(c)
                bank, bb = next_pj()
                for dc in range(8):
                    P.op(PE, lambda dc=dc, bank=bank, wbk=wbk, s=s, ti=ti: nc.tensor.matmul(
                        bank[:, 0:256], lhsT=hT[:, s, dc, 128 * ti:128 * ti + 128], rhs=wbk[:, dc, :],
                        start=(dc == 0), stop=(dc == 7)),
                         reads=[wbb, hTb[s]], writes=[bb], signal=(dc == 7))
                evac_copy(tt + hb, uring[:, tt % NU, 256 * hb:256 * hb + 256], bank[:, 0:256], [bb], [uringb[tt % NU]])

    estate = {"n": 0}
    mstate = {"n": 0}

    def attention_pair(c, pr):
        hbanks = []
        for h in (2 * pr, 2 * pr + 1):
            banks = []
            pb = 64 * (h % 2)
            ob = h % 2
            o0, o0b, o12, o12b = o0s[ob], o0bs[ob], o12s[ob], o12bs[ob]
            mi = h % 2
            mkt, maskb = mk[mi], mkbb[mi]
            if mstate.get(mi) != h:
                P.dma(POOL, mkt[:, :], mk_d.ap()[h], mk_sem[mi], writes=[maskb])
                mstate[mi] = h
            tiles = [(c - 1, 3, 0, 64), (c, 0, 0, 192), (c, 1, 64, 320), (c, 2, 192, 448), (c, 3, 320, 512), (c + 1, 0, 448, 512)]
            first_o0 = True
            for half in range(2):
                rec = {"st": [], "mask": [], "pv": [], "maskb": maskb, "finish": None}
                col = 0
                for ti_, (cc, t_, q0, q1) in enumerate(tiles[3 * half:3 * half + 3]):
                    n = q1 - q0
                    ks = cc % 3
                    rec["st"].append((lambda stbank, col=col, n=n, ks=ks, t_=t_, q0=q0, q1=q1, pb=pb: nc.tensor.matmul(
                        stbank[:, col:col + n], lhsT=kT[0][pb:pb + 64, pr, ks, 128 * t_:128 * t_ + 128],
                        rhs=qT[pb:pb + 64, 0, q0:q1], start=True, stop=True), [kTb[0][ks], qTb]))
                    qq0 = (c * T + q0) - (cc * T + 128 * t_) + 128
                    rec["mask"].append((col, col + n, mkt[:, qq0 - 64:qq0 - 64 + n], None))
                    rec["pv"].append((lambda em, col=col, n=n, ks=ks, t_=t_, q0=q0, q1=q1, st_=first_o0, o0=o0, h=h: nc.tensor.matmul(
                        o0[0:65, q0:q1], lhsT=va[0][:, ks, t_, h, :], rhs=em[:, col:col + n],
                        start=st_, stop=False, skip_group_check=True), [vab[0][ks]], [o0b], col))
                    first_o0 = False
                    col += n
                rec["ncols"] = col
                banks.append(rec)
            first_o12 = True
            for g, dcs in ((1, (-1, 0, 1)), (2, (-2, -1, 0, 1, 2))):
                for di, dc_ in enumerate(dcs):
                    cc = c + dc_
                    ks = cc % NSLOT[g]
                    rec = {"st": [], "mask": [], "pv": [], "maskb": maskb, "finish": None, "ncols": 512}
                    for r4 in range(4):
                        rec["st"].append((lambda stbank, r4=r4, ks=ks, g=g, pb=pb: nc.tensor.matmul(
                            stbank[:, 128 * r4:128 * r4 + 128], lhsT=kT[g][pb:pb + 64, pr, ks, 128 * r4:128 * r4 + 128],
                            rhs=qT[pb:pb + 64, g, 128 * r4:128 * r4 + 128], start=True, stop=True), [kTb[g][ks], qTb]))
                        rec["pv"].append((lambda em, r4=r4, ks=ks, g=g, st_=first_o12, o12=o12, h=h: nc.tensor.matmul(
                            o12[0:65, 128 * r4:128 * r4 + 128], lhsT=va[g][:, ks, r4, h, :], rhs=em[:, 128 * r4:128 * r4 + 128],
                            start=st_, stop=False, skip_group_check=True), [vab[g][ks]], [o12b], 128 * r4))
                        first_o12 = False
                    moff = (256 + 128 - 128 * dc_) if g == 1 else (640 + 128 * di)
                    for hf_ in range(2):
                        rec["mask"].append((256 * hf_, 256 * hf_ + 256, mkt[:, moff:moff + 128].unsqueeze(1).to_broadcast([128, 2, 128]), 128))
                    banks.append(rec)

            def finish(h=h, pb=pb, o0=o0, o0b=o0b, o12=o12, o12b=o12b):
                P.op(ACT, lambda: nc.scalar.copy(out=s12[0:65, :].rearrange("p (m r) -> p r m", r=4),
                                                 in_=o12[0:65, :].rearrange("p (r m) -> p r m", r=4)),
                     reads=[o12b], writes=[s12b])
                if c == DBG_CHUNK:
                    dump(f"s12_h{h}", s12[0:65, :], [s12b])
                P.op(DVE, lambda: nc.vector.tensor_tensor(out=BT[pb:pb + 64, pr, :], in0=o0[0:64, :], in1=s12[0:64, :], op=ALU.add),
                     reads=[o0b, s12b], writes=[BTb[pr]])
                P.op(DVE, lambda: nc.vector.tensor_tensor(out=s12[64:65, :], in0=o0[64:65, :], in1=s12[64:65, :], op=ALU.add),
                     reads=[o0b, s12b], writes=[s12b])
                row = 32 * (h // 4) + (h % 4)
                P.dma(POOL, att[row:row + 1, :], s12[64:65, :], den_sem, reads=[s12b], writes=[attb])
                hn = (h + 2) % 8
                P.dma(POOL, mk[h % 2][:, :], mk_d.ap()[hn], mk_sem[h % 2], writes=[mkbb[h % 2]])
                mstate[h % 2] = hn
            banks[-1]["finish"] = finish
            hbanks.append(banks)

        def emit_st(stage):
            recs = [hbanks[0][stage], hbanks[1][stage]]
            for rec in recs:
                rec["stbank"], rec["stbb"] = next_st()
            nst = len(recs[0]["st"])
            for i in range(nst):
                for rec in recs:
                    fn, reads = rec["st"][i]
                    P.op(PE, lambda fn=fn, rec=rec: fn(rec["stbank"]), reads=reads, writes=[rec["stbb"]], signal=(i == nst - 1))

        def emit_rest(rec):
            k = estate["n"] % 2
            estate["n"] += 1
            stbank, stbb, ncols, maskb = rec["stbank"], rec["stbb"], rec["ncols"], rec["maskb"]
            for hf_ in range(2):
                lo, hi = 256 * hf_, min(256 * hf_ + 256, ncols)
                if hi <= lo:
                    continue
                P.op(ACT, lambda lo=lo, hi=hi: nc.scalar.activation(out=E0[k][:, lo:hi], in_=stbank[:, lo:hi], func=AF.Exp, scale=0.125),
                     reads=[stbb], writes=[E0hb[k][hf_]])
                for (c0, c1, m_ap, shape3) in rec["mask"]:
                    if not (lo <= c0 and c1 <= hi):
                        continue
                    if shape3 is None:
                        o_ap, i_ap = Em[k][:, c0:c1], E0[k][:, c0:c1]
                    else:
                        o_ap = Em[k][:, c0:c1].rearrange("p (a b) -> p a b", b=shape3)
                        i_ap = E0[k][:, c0:c1].rearrange("p (a b) -> p a b", b=shape3)
                    P.op(DVE, lambda o_ap=o_ap, i_ap=i_ap, m_ap=m_ap: nc.vector.tensor_tensor(out=o_ap, in0=i_ap, in1=m_ap, op=ALU.mult),
                         reads=[E0hb[k][hf_], maskb], writes=[Emhb[k][hf_]])
                pvs = [p for p in rec["pv"] if lo <= p[3] < hi]
                for i, (fn, reads, writes, _c0) in enumerate(pvs):
                    P.op(PE, lambda fn=fn: fn(Em[k]), reads=[Emhb[k][hf_]] + reads, writes=writes, signal=(i == len(pvs) - 1))
            if rec["finish"] is not None:
                rec["finish"]()

        nstage = len(hbanks[0])
        emit_st(0)
        for n in range(nstage):
            if n + 1 < nstage:
                emit_st(n + 1)
            emit_rest(hbanks[0][n])
            emit_rest(hbanks[1][n])

    def attention_recip(c):
        P.op(DVE, lambda: nc.vector.reciprocal(out=att[0:36, :], in_=att[0:36, :]), reads=[attb], writes=[attb])

    def attention_normalise(c):
        for pr in range(4):
            base = 32 * (pr // 2)
            a = pr % 2
            bcbank, bcb = next_pj()
            P.op(PE, lambda: nc.tensor.matmul(bcbank[:, 0:T], lhsT=sel2[base:base + 4, 128 * a:128 * a + 128],
                                              rhs=att[base:base + 4, :], start=True, stop=True),
                 reads=[attb, constb], writes=[bcb])
            P.op(DVE, lambda: nc.vector.scalar_tensor_tensor(out=th[:, :], in0=bcbank[:, 0:T], scalar=0.5,
                                                             in1=sgs[:, pr, :], op0=ALU.mult, op1=ALU.mult),
                 reads=[bcb, sgsb[pr]], writes=[thb])
            P.op(POOL, lambda: nc.gpsimd.tensor_tensor(out=BT[:, pr, :], in0=BT[:, pr, :], in1=th[:, :], op=ALU.mult),
                 reads=[thb, BTb[pr]], writes=[BTb[pr]])

    def silu_half(bank, bb, dst, dstb):
        P.op(ACT, lambda: nc.scalar.activation(out=th[:, :], in_=bank[:, 0:T], func=AF.Tanh, scale=0.5), reads=[bb], writes=[thb])
        P.op(DVE, lambda: nc.vector.scalar_tensor_tensor(out=dst, in0=th[:, :], scalar=1.0, in1=bank[:, 0:T],
                                                         op0=ALU.add, op1=ALU.mult),
             reads=[thb, bb], writes=[dstb])

    def proj_fm(wbk, wbb, seg, s):
        bank, bb = next_pj()
        for dc in range(8):
            P.op(PE, lambda dc=dc: nc.tensor.matmul(bank[:, 0:T], lhsT=wbk[:, dc, 128 * seg:128 * seg + 128],
                                                    rhs=hT[:, s, dc, :], start=(dc == 0), stop=(dc == 7)),
                 reads=[wbb, hTb[s]], writes=[bb], signal=(dc == 7))
        return bank, bb

    def stageB(c):
        s = hslot(c)
        rot["wide"] = False
        for pr in range(4):
            wa, wab = load_block(f"QA{pr}")
            wq, wqb = load_block(f"QB{pr}")
            for g in range(3):
                wbk, wbb, seg = (wa, wab, g) if g < 2 else (wq, wqb, 0)
                bank, bb = proj_fm(wbk, wbb, seg, s)
                if g == 0:
                    src, dst = bank[:, 0:T], qT[:, 0, :]
                else:
                    src = bank[:, 0:T].rearrange("p (m r) -> p r m", r=4)
                    dst = qT[:, g, :].rearrange("p (r m) -> p r m", r=4)
                evac_copy(g, dst, src, [bb], [qTb])
            bank, bb = proj_fm(wq, wqb, 1, s)
            silu_half(bank, bb, sgs[:, pr, :], sgsb[pr])
            if c == DBG_CHUNK:
                dump(f"qT_p{pr}", qT[:, :, :], [qTb])
                dump(f"sg_p{pr}", sgs[:, pr, :], [sgsb[pr]])
                if pr == 0:
                    dump("hT", hT[:, s, :, :], [hTb[s]])
                    for g in range(3):
                        dump(f"kT{g}", kT[g][:, :, :, :], kTb[g])
                        dump(f"va{g}", va[g][:, :, :, :, :].rearrange("p a b c d -> p (a b c d)"), vab[g])
            attention_pair(c, pr)
        rot["wide"] = True
        attention_recip(c)
        pbufs = [(Em[0], Emhb[0]), (Em[1], Emhb[1]), (E0[0], E0hb[0]), (E0[1], E0hb[1])]
        bands = []
        for g in range(4):
            bank, bb = next_pj()
            first = True
            for tp in range(-1, 5):
                tt = 4 * c + tp
                q0, q1 = max(0, 128 * tp - 8), min(T, 128 * tp + 136)
                qq0 = q0 - (128 * tp - 8)
                kind = 1
                if c == MAIN0 and tp == 0:
                    kind = 0
                if c == MAIN1 - 1 and tp == 3:
                    kind = 2
                P.op(PE, lambda tt=tt, q0=q0, q1=q1, qq0=qq0, kind=kind, g=g, first=first, bank=bank: nc.tensor.matmul(
                    bank[:, q0:q1], lhsT=uring[:, tt % NU, 128 * g:128 * g + 128],
                    rhs=band[:, 4 * kind + g, qq0:qq0 + (q1 - q0)], start=first, stop=False, skip_group_check=True),
                     reads=[uringb[tt % NU], constb], writes=[bb], signal=(tp == 4))
                first = False
            bands.append((bank, bb))
        for g in range(4):
            bank, bb = bands[g]
            evac_copy(g, pbufs[g][0][:, :], bank[:, 0:T], [bb], pbufs[g][1])
            if c == DBG_CHUNK:
                dump(f"pooledT_g{g}", pbufs[g][0][:, :], pbufs[g][1])
        mg = []
        for g in range(4):
            mbank, mbb = next_pj()
            P.op(PE, lambda g=g, mbank=mbank: nc.tensor.matmul(mbank[:, 0:T], lhsT=wpool[:, g, :], rhs=pbufs[g][0][:, :], start=True, stop=True),
                 reads=pbufs[g][1] + [constb], writes=[mbb])
            if g % 2 == 0:
                wg, wgb = load_block(f"AG{g // 2}")
            gbank, gbb = proj_fm(wg, wgb, g % 2, s)
            mg.append((mbank, mbb, gbank, gbb))
        for g in range(4):
            mbank, mbb, gbank, gbb = mg[g]
            silu_half(gbank, gbb, sgp[:, :], sgpb)
            P.op(DVE, lambda g=g, mbank=mbank: nc.vector.scalar_tensor_tensor(out=AT[:, g, :], in0=mbank[:, 0:T], scalar=psh[:, g:g + 1],
                                                                             in1=sgp[:, :], op0=ALU.mult, op1=ALU.mult),
                 reads=[mbb, sgpb, constb], writes=[ATb[g]])
        attention_normalise(c)
        if c == DBG_CHUNK:
            dump("BT", BT[:, :, :], BTb)
            dump("uring", uring[:, :, :], uringb)
        if c == DBG_CHUNK:
            dump("AT", AT[:, :, :], ATb)
        load_xres(c, 0)
        for q in range(4):
            wp, wpb_ = load_block(f"PAB{q}")
            wga, wgab = load_block(f"GA{q}")
            wgb_, wgbb = load_block(f"GB{q}")
            for seg in range(2):
                oc = 2 * q + seg
                for br in range(2):
                    ybank, ybb = next_pj()
                    src_t, src_b = (AT, ATb) if br == 0 else (BT, BTb)
                    for kc in range(4):
                        P.op(PE, lambda kc=kc, br=br, src_t=src_t: nc.tensor.matmul(
                            ybank[:, 0:T], lhsT=wp[:, 4 * br + kc, 128 * seg:128 * seg + 128], rhs=src_t[:, kc, :],
                            start=(kc == 0), stop=(kc == 3)),
                             reads=[wpb_, src_b[kc]], writes=[ybb], signal=(kc == 3))
                    gw, gwb = (wga, wgab) if br == 0 else (wgb_, wgbb)
                    gbank, gbb = proj_fm(gw, gwb, seg, s)
                    P.op(ACT, lambda br=br, oc=oc, gbank=gbank: nc.scalar.activation(
                        out=th[:, :], in_=gbank[:, 0:T], func=AF.Tanh, scale=0.5, bias=hbias[:, 8 * br + oc:8 * br + oc + 1]),
                         reads=[gbb, constb], writes=[thb])
                    if br == 0:
                        P.op(DVE, lambda ybank=ybank: nc.vector.scalar_tensor_tensor(
                            out=gy[:, :], in0=th[:, :], scalar=1.0, in1=ybank[:, 0:T], op0=ALU.add, op1=ALU.mult),
                             reads=[thb, ybb], writes=[gyb])
                    else:
                        P.op(DVE, lambda ybank=ybank: nc.vector.scalar_tensor_tensor(
                            out=att[:, :], in0=th[:, :], scalar=1.0, in1=ybank[:, 0:T], op0=ALU.add, op1=ALU.mult),
                             reads=[thb, ybb], writes=[attb])
                        P.op(POOL, lambda oc=oc: nc.gpsimd.tensor_tensor(out=merged[:, oc, :], in0=gy[:, :], in1=att[:, :], op=ALU.add),
                             reads=[gyb, attb], writes=[mergedb[oc]])
        if c == DBG_CHUNK:
            dump("merged", merged[:, :, :], mergedb)
        wo = [load_block(f"WO{q}") for q in range(4)]
        load_xres(c, 1)
        for ti in range(4):
            row0 = (c - MAIN0) * T + 128 * ti
            xr, xrb = xres_bufs[(ti + 1) % 2]
            for half in range(2):
                obank, obb = next_pj()
                first = True
                for qq in range(2):
                    wbk, wbb = wo[2 * half + qq]
                    for oc in range(8):
                        last = (qq == 1 and oc == 7)
                        P.op(PE, lambda oc=oc, qq=qq, wbk=wbk, first=first, obank=obank: nc.tensor.matmul(
                            obank[:, 256 * qq:256 * qq + 256], lhsT=merged[:, oc, 128 * ti:128 * ti + 128], rhs=wbk[:, oc, :],
                            start=first, stop=False, skip_group_check=True),
                             reads=[wbb, mergedb[oc]], writes=[obb], signal=last)
                        first = False
                P.op(DVE, lambda half=half, obank=obank, xr=xr: nc.vector.scalar_tensor_tensor(
                    out=res[:, 512 * half:512 * half + 512], in0=obank[:, 0:512], scalar=0.5,
                    in1=xr[:, 512 * half:512 * half + 512], op0=ALU.mult, op1=ALU.add),
                     reads=[obb] + xrb, writes=resb_l)
            P.op(ACT, lambda: nc.scalar.activation(out=th[:, :].bitcast(BF16), in_=res[:, :], func=AF.Square, accum_out=fstat[:, 0:1]),
                 reads=resb_l, writes=[thb, fstatb])
            P.op(DVE, lambda: nc.vector.tensor_scalar(out=fstat[:, 1:2], in0=fstat[:, 0:1], scalar1=1.0 / D, scalar2=EPS,
                                                      op0=ALU.mult, op1=ALU.add),
                 reads=[fstatb], writes=[fstatb])
            P.op(POOL, lambda: nc.gpsimd.tensor_tensor(out=fstat[:, 2:3], in0=fstat[:, 1:2], in1=negh[:, 0:1], op=ALU.pow),
                 reads=[fstatb, constb], writes=[fstatb])
            P.op(DVE, lambda xr=xr: nc.vector.scalar_tensor_tensor(out=xr[:, :], in0=res[:, :], scalar=fstat[:, 2:3], in1=fgb[:, :],
                                                                   op0=ALU.mult, op1=ALU.mult),
                 reads=resb_l + [fstatb, constb], writes=xrb)
            P.dma(POOL, out_d.ap()[row0:row0 + 128, :], xr[:, :], out_sems[(ti + 1) % 2], reads=xrb)
            if ti + 2 < 4:
                load_xres(c, ti + 2)

    P.op(POOL, lambda: nc.gpsimd.memset(kT[2][:, :, 0:2, :], 0.0), writes=[kTb[2][0], kTb[2][1]])
    P.op(POOL, lambda: nc.gpsimd.memset(va[2][:, 0:2, :, :, :], 0.0), writes=[vab[2][0], vab[2][1]])
    for g in (0, 1):
        P.op(POOL, lambda g=g: nc.gpsimd.memset(kT[g][:, :, 1, :], 0.0), writes=[kTb[g][1]])
        P.op(POOL, lambda g=g: nc.gpsimd.memset(va[g][:, 1, :, :, :], 0.0), writes=[vab[g][1]])
    P.op(POOL, lambda: nc.gpsimd.memset(uring[:, 7 % NU, :], 0.0), writes=[uringb[7 % NU]])
    stageA1a(2); stageA1b(2)
    stageA1a(3); stageA1b(3)
    for g in (0, 1):
        kv_stage(2, g)
    u_tiles([8])
    for i in range(MAIN0, MAIN1):
        kv_stage(i + 1, 0)
        if i + 2 < NCH:
            stageA1a(i + 2, load=True)
        kv_stage(i + 1, 1)
        u_tiles([4 * i + 1, 4 * i + 2, 4 * i + 3, 4 * i + 4])
        if i + 2 < NCH:
            stageA1b(i + 2)
        stageB(i)
    for osem in out_sems:
        POOL.h.wait_ge(osem.h, osem.val)
        SP.h.wait_ge(osem.h, osem.val)
    if debug and dbg_sem.val:
        SP.h.wait_ge(dbg_sem.h, dbg_sem.val)
    es.close()
    return nc


def _t5_bucket_np(rel):
    nb = 16
    ret = (rel > 0).astype(np.int32) * nb
    n = np.abs(rel)
    max_exact = nb // 2
    nf = np.maximum(n, 1).astype(np.float32)
    large = max_exact + (np.log(nf / np.float32(max_exact)) / np.float32(math.log(1024 / max_exact))
                         * np.float32(nb - max_exact)).astype(np.int32)
    large = np.minimum(large, nb - 1)
    return ret + np.where(n < max_exact, n, large)


def _host_consts(hf):
    sign = 1 if hf == 0 else -1
    ident = np.eye(128, dtype=np.float32).astype(ml_dtypes.bfloat16)
    onehot = np.zeros((33, 7, 512), np.float32)
    idx = np.arange(512)
    for g in range(2):
        j = 128 - idx
        inwin = (idx >= 64) & (idx <= 192)
        b = _t5_bucket_np((sign * j * DIL[g]).astype(np.int32))
        b = np.where(inwin, b, 32)
        onehot[b, g, idx] = 1.0
    for di, dc_ in enumerate((-2, -1, 0, 1, 2)):
        delta = 128 - idx
        j = 32 * dc_ + delta // 4
        ok = (idx >= 1) & (idx <= 255) & (delta % 4 == 0) & (np.abs(j) <= 64)
        b = _t5_bucket_np((sign * j * DIL[2]).astype(np.int32))
        b = np.where(ok, b, 32)
        onehot[b, 2 + di, idx] = 1.0
    band = np.zeros((128, 3, 4, 144), np.float32)
    k = np.arange(128)[:, None]
    qp = (np.arange(144) - 8)[None, :]
    for g, w in enumerate((2, 4, 8, 16)):
        hw = w // 2
        if hf == 0:
            inw = ((k >= qp - hw) & (k <= qp + hw - 1)).astype(np.float32)
            cnt_first = (np.minimum(np.maximum(qp, 0), hw) + hw).astype(np.float32)
        else:
            inw = ((k >= qp - hw + 1) & (k <= qp + hw)).astype(np.float32)
            cnt_first = (hw + np.minimum(hw, np.maximum(qp, 0) + 1)).astype(np.float32)
        cnt_first = np.minimum(cnt_first, w)
        eye = (k == qp).astype(np.float32)
        cnt_mid = np.full((1, 144), float(w), np.float32)
        cnts = [cnt_first, cnt_mid, cnt_mid]
        for kind in range(3):
            band[:, kind, g, :] = inw / cnts[kind] - eye
    band = band.reshape(128, 12 * 144).astype(ml_dtypes.bfloat16)
    valid = np.ones((128, NCH), np.float32)
    valid[:, 0:2] = 0.0
    sel2 = np.zeros((128, 256), np.float32)
    for base in (0, 32):
        for r in range(4):
            for a in range(2):
                for m in range(128):
                    if r == 2 * a + m // 64:
                        sel2[base + r, 128 * a + m] = 1.0
    return ident, onehot.reshape(33, 7 * 512), band, valid, sel2


_CACHE = {}


def kernel(x, norm_gain, w_in, b_gate, rel_bias, w_pool, pool_scale, w_proj_a, w_proj_b, w_out, final_gain):
    x = np.asarray(x, np.float32)
    if "nc" not in _CACHE:
        _CACHE["nc"] = build_program()
    nc = _CACHE["nc"]
    common = {
        "w_in": np.ascontiguousarray(np.asarray(w_in, np.float32)[0]),
        "w_proj_a": np.ascontiguousarray(np.asarray(w_proj_a, np.float32)[0]),
        "w_proj_b": np.ascontiguousarray(np.asarray(w_proj_b, np.float32)[0]),
        "w_out": np.ascontiguousarray(np.asarray(w_out, np.float32)[0]),
        "w_pool": np.ascontiguousarray(np.asarray(w_pool, np.float32)[0]),
        "norm_gain": np.ascontiguousarray(np.asarray(norm_gain, np.float32)[0]),
        "final_gain": np.ascontiguousarray(np.asarray(final_gain, np.float32)),
        "b_gate": np.ascontiguousarray(np.asarray(b_gate, np.float32)[0]),
        "pool_scale": np.ascontiguousarray(np.asarray(pool_scale, np.float32)[0]),
        "rel_bias": np.ascontiguousarray(np.asarray(rel_bias, np.float32)),
    }
    in_maps = []
    for core in range(NCORES):
        b, hf = core // 2, core % 2
        xe = np.zeros((EXT, D), np.float32)
        if hf == 0:
            xe[1024:EXT] = x[b, 0:EXT - 1024]
        else:
            xe[1024:EXT] = x[b, SEQ - (EXT - 1024):SEQ][::-1]
        ident, onehot, band, valid, sel2 = _host_consts(hf)
        m = dict(common)
        m.update({"x": xe, "ident": ident, "onehot": onehot, "band": band, "valid": valid, "sel2": sel2})
        in_maps.append(m)
    res = run_bass_kernel_spmd(nc, in_maps, core_ids=list(range(NCORES)))
    out = np.empty((4, SEQ, D), np.float32)
    for core in range(NCORES):
        b, hf = core // 2, core % 2
        o = np.asarray(res.results[core]["out"], np.float32)
        out[b, hf * 4096:(hf + 1) * 4096] = o if hf == 0 else o[::-1]
    return out
```

```python
import math
from contextlib import ExitStack

import numpy as np
import ml_dtypes

import concourse.bass as bass
import concourse.mybir as mybir
from concourse.bass_utils import run_bass_kernel_spmd

F32 = mybir.dt.float32
BF16 = mybir.dt.bfloat16
ALU = mybir.AluOpType
AF = mybir.ActivationFunctionType

D = 1024
SEQ = 8192
NCORES = 8
T = 512
NCH = 12
EXT = NCH * T
MAIN0, MAIN1 = 2, 10
COL_AIN, COL_AGATE, COL_Q, COL_K, COL_V, COL_BGP, COL_G = 0, 512, 1024, 2560, 4096, 5632, 6144
DIL = (1, 4, 16)
NEG = -30000.0
EPS = 1e-6

BLK = {}
_blocks = []


def _add(name, segs):
    BLK[name] = len(_blocks)
    _blocks.append(segs)


for g in range(3):
    for hb in range(2):
        _add(f"K{g}{hb}", [("win", COL_K + 512 * g + 256 * hb), ("win", COL_K + 512 * g + 256 * hb + 128)])
        _add(f"V{g}{hb}", [("win", COL_V + 512 * g + 256 * hb), ("win", COL_V + 512 * g + 256 * hb + 128)])
for pr in range(4):
    _add(f"QA{pr}", [("win", COL_Q + 128 * pr), ("win", COL_Q + 512 + 128 * pr)])
    _add(f"QB{pr}", [("win", COL_Q + 1024 + 128 * pr), ("win", COL_BGP + 128 * pr)])
for hb in range(2):
    _add(f"AI{hb}", [("win", COL_AIN + 256 * hb), ("win", COL_AIN + 256 * hb + 128)])
    _add(f"AG{hb}", [("win", COL_AGATE + 256 * hb), ("win", COL_AGATE + 256 * hb + 128)])
for q in range(4):
    _add(f"GA{q}", [("win", COL_G + 256 * q), ("win", COL_G + 256 * q + 128)])
    _add(f"GB{q}", [("win", COL_G + 1024 + 256 * q), ("win", COL_G + 1024 + 256 * q + 128)])
for q in range(4):
    _add(f"PAB{q}", [("pab", 256 * q)])
    _add(f"WO{q}", [("wout", 256 * q)])
NBLK = len(_blocks)


class Sem:
    def __init__(self, h):
        self.h = h
        self.val = 0


class Eng:
    def __init__(self, name, h, sem, inorder=False):
        self.name, self.h, self.sem, self.inorder = name, h, sem, inorder
        self.waited = {}


class Buf:
    __slots__ = ("w", "r", "name")

    def __init__(self, name=""):
        self.w = None
        self.r = {}
        self.name = name


class Prog:
    def __init__(self, nc, es):
        self.nc, self.es = nc, es
        self.nsem = 0
        self.PE = Eng("pe", nc.tensor, self.sem("pe"), inorder=True)
        self.ACT = Eng("act", nc.scalar, self.sem("act"))
        self.DVE = Eng("dve", nc.vector, self.sem("dve"))
        self.POOL = Eng("pool", nc.gpsimd, self.sem("pool"))
        self.SP = Eng("sp", nc.sync, self.sem("sp"))
        self.engs = [self.PE, self.ACT, self.DVE, self.POOL, self.SP]
        self.all_sems = []

    def sem(self, name):
        self.nsem += 1
        s = Sem(self.es.enter_context(self.nc.semaphore(f"{name}_{self.nsem}")))
        if not hasattr(self, "_sems"):
            self._sems = []
        self._sems.append(s)
        return s

    def sb(self, name, shape, dt):
        return self.es.enter_context(self.nc.sbuf_tensor("s_" + name, list(shape), dt))

    def ps(self, name, shape, dt):
        return self.es.enter_context(self.nc.psum_tensor("p_" + name, list(shape), dt))

    def _waits(self, E, reads, writes):
        deps = {}

        def need(ev):
            if ev is None:
                return
            s, v = ev
            if deps.get(s, 0) < v:
                deps[s] = v

        for b in reads:
            need(b.w)
        for b in writes:
            need(b.w)
            for s, v in b.r.items():
                need((s, v))
        for s, v in deps.items():
            if s is E.sem and E.inorder:
                continue
            if E.waited.get(s, 0) < v:
                E.h.wait_ge(s.h, v)
                E.waited[s] = v

    def _record(self, ev, reads, writes):
        s, v = ev
        for b in reads:
            if b.r.get(s, 0) < v:
                b.r[s] = v
        for b in writes:
            b.w = ev
            b.r = {}

    def op(self, E, fn, reads=(), writes=(), signal=True):
        self._waits(E, reads, writes)
        ins = fn()
        if signal:
            E.sem.val += 1
            ins.then_inc(E.sem.h, 1)
            ev = (E.sem, E.sem.val)
        else:
            ev = (E.sem, E.sem.val + 1)
        self._record(ev, reads, writes)
        return ins

    def dma(self, E, out, in_, sem, reads=(), writes=(), **kw):
        self._waits(E, reads, writes)
        ins = E.h.dma_start(out=out, in_=in_, **kw)
        sem.val += 16
        ins.then_inc(sem.h, 16)
        self._record((sem, sem.val), reads, writes)
        return ins

    def barrier(self):
        for E in self.engs:
            for s in self._sems:
                if s.val > 0 and E.waited.get(s, 0) < s.val:
                    E.h.wait_ge(s.h, s.val)
                    E.waited[s] = s.val


def build_program(debug=False):
    nc = bass.Bass("TRN2", target_bir_lowering=False)
    dram = lambda n, s, dt, k="ExternalInput": nc.dram_tensor(n, list(s), dt, kind=k)
    x_d = dram("x", [EXT, D], F32)
    win_d = dram("w_in", [D, 8192], F32)
    wpa_d = dram("w_proj_a", [512, D], F32)
    wpb_d = dram("w_proj_b", [512, D], F32)
    wout_d = dram("w_out", [D, D], F32)
    wpool_d = dram("w_pool", [4, 128, 128], F32)
    ng_d = dram("norm_gain", [D], F32)
    fg_d = dram("final_gain", [D], F32)
    bg_d = dram("b_gate", [2, D], F32)
    psc_d = dram("pool_scale", [512], F32)
    rb_d = dram("rel_bias", [32, 24], F32)
    ident_d = dram("ident", [128, 128], BF16)
    onehot_d = dram("onehot", [33, 7 * 512], F32)
    band_d = dram("band", [128, 12 * 144], BF16)
    valid_d = dram("valid", [128, NCH], F32)
    sel2_d = dram("sel2", [128, 256], F32)
    out_d = dram("out", [8 * T, D], F32, "ExternalOutput")
    wb_d = dram("wb_scr", [NBLK, 128, 8 * 256], BF16, "Internal")
    pd_d = dram("pd_scr", [56, 128 * 512], BF16, "Internal")
    mk_d = dram("mk_scr", [8, 128, 256 + 384 + 640], BF16, "Internal")

    es = ExitStack()
    P = Prog(nc, es)
    PE, ACT, DVE, POOL, SP = P.PE, P.ACT, P.DVE, P.POOL, P.SP
    dbg_sem = P.sem("dbg")
    DBG_CHUNK = 2

    def dump(name, ap, reads, dt=None):
        if not debug:
            return
        shape = list(ap.shape)
        t = nc.dram_tensor("dbg_" + name, shape, dt or ap.dtype, kind="ExternalOutput")
        P.dma(SP, t.ap(), ap, dbg_sem, reads=reads)

    pj = [P.ps(f"pj{i}", [128, 512], F32) for i in range(2)]
    pjb = [Buf(f"pj{i}") for i in range(2)]
    st = [P.ps(f"st{i}", [128, 512], F32) for i in range(2)]
    stb = [Buf(f"st{i}") for i in range(2)]
    o0s = [P.ps(f"o0_{i}", [128, 512], F32) for i in range(2)]
    o0bs = [Buf() for _ in range(2)]
    o12s = [P.ps(f"o12_{i}", [128, 512], F32) for i in range(2)]
    o12bs = [Buf() for _ in range(2)]
    rot = {"pj": 0, "st": 0, "wide": True}
    wide_banks = [(pj[0], pjb[0]), (pj[1], pjb[1]), (st[0], stb[0]), (st[1], stb[1]),
                  (o0s[0], o0bs[0]), (o0s[1], o0bs[1]), (o12s[0], o12bs[0]), (o12s[1], o12bs[1])]

    def next_pj():
        lst = wide_banks if rot["wide"] else wide_banks[0:2]
        i = rot["pj"] % len(lst)
        rot["pj"] = i + 1
        return lst[i]

    st4 = [(st[0], stb[0]), (st[1], stb[1]), (pj[0], pjb[0]), (pj[1], pjb[1])]

    def next_st():
        i = rot["st"]
        rot["st"] = (i + 1) % 4
        return st4[i]

    with ExitStack() as es0:
        NST = 4
        stg = [es0.enter_context(nc.sbuf_tensor(f"stg{i}", [128, 8, 256], F32)) for i in range(NST)]
        stgb = [Buf() for _ in range(NST)]
        cvt = [es0.enter_context(nc.sbuf_tensor(f"cvt{i}", [128, 8 * 256], BF16)) for i in range(NST)]
        cvtb = [Buf() for _ in range(NST)]
        ld_sem = [P.sem("p0ld") for _ in range(NST)]
        st_sem = [P.sem("p0st") for _ in range(NST)]
        cv_engs = [DVE, ACT, POOL]
        for bi, segs in enumerate(_blocks):
            k = bi % NST
            if segs[0][0] == "win":
                for si, (_, c0) in enumerate(segs):
                    src = win_d.ap()[:, c0:c0 + 128].rearrange("(kc p) c -> p kc c", p=128)
                    P.dma(SP, stg[k][:, :, si * 128:(si + 1) * 128], src, ld_sem[k], writes=[stgb[k]] if si == 0 else [])
                stgb[k].w = (ld_sem[k], ld_sem[k].val)
            elif segs[0][0] == "pab":
                c0 = segs[0][1]
                P.dma(SP, stg[k][:, 0:4, :], wpa_d.ap()[:, c0:c0 + 256].rearrange("(kc p) c -> p kc c", p=128),
                      ld_sem[k], writes=[stgb[k]])
                P.dma(SP, stg[k][:, 4:8, :], wpb_d.ap()[:, c0:c0 + 256].rearrange("(kc p) c -> p kc c", p=128),
                      ld_sem[k], writes=[])
                stgb[k].w = (ld_sem[k], ld_sem[k].val)
            else:
                c0 = segs[0][1]
                P.dma(SP, stg[k][:, :, :], wout_d.ap()[:, c0:c0 + 256].rearrange("(kc p) c -> p kc c", p=128),
                      ld_sem[k], writes=[stgb[k]])
            E = cv_engs[bi % 3]
            src_ap = stg[k][:, :, :].rearrange("p a b -> p (a b)")
            if E is ACT:
                P.op(E, lambda s=src_ap, o=cvt[k]: nc.scalar.copy(out=o[:, :], in_=s), reads=[stgb[k]], writes=[cvtb[k]])
            else:
                P.op(E, lambda s=src_ap, o=cvt[k], e=E: e.h.tensor_copy(out=o[:, :], in_=s), reads=[stgb[k]], writes=[cvtb[k]])
            P.dma(POOL, wb_d.ap()[bi], cvt[k][:, :], st_sem[k], reads=[cvtb[k]])
        P.barrier()

    stat = P.sb("stat", [128, 16], F32)
    statb = Buf()
    ident = P.sb("ident", [128, 128], BF16)
    band = P.sb("band", [128, 12, 144], BF16)
    valid = P.sb("valid", [128, NCH], F32)
    gain = P.sb("gain", [128, 8], F32)
    hbias = P.sb("hbias", [128, 16], F32)
    psh = P.sb("psh", [128, 4], F32)
    fgb = P.sb("fgb", [128, D], F32)
    wpool = P.sb("wpool", [128, 4, 128], BF16)
    fstat = P.sb("fstat", [128, 8], F32)
    fstatb = Buf()
    sel2 = P.sb("sel2", [128, 256], F32)
    negh = P.sb("negh", [128, 4], F32)
    constb = Buf()
    MKW = 256 + 384 + 640

    su_sem = P.sem("setupA")
    su_semB = P.sem("setupB")
    su_semC = P.sem("setupC")
    su_semD = P.sem("setupD")
    with ExitStack() as es1:
        rb_aug = es1.enter_context(nc.sbuf_tensor("rb_aug", [64, 24], F32))
        onehot = es1.enter_context(nc.sbuf_tensor("s_onehot", [64, 7 * 512], F32))
        pv = es1.enter_context(nc.sbuf_tensor("pv", [32, 7, 512], BF16))
        mkall = es1.enter_context(nc.sbuf_tensor("mkall", [128, 8, MKW], BF16))
        wpool_f = es1.enter_context(nc.sbuf_tensor("wpool_f", [128, 4, 128], F32))
        bg_f = es1.enter_context(nc.sbuf_tensor("bg_f", [128, 16], F32))
        ps_f = es1.enter_context(nc.sbuf_tensor("ps_f", [128, 4], F32))
        sub = Buf()
        pvb = Buf()
        pdb = Buf()
        mkb = Buf()

        P.op(POOL, lambda: nc.gpsimd.memset(rb_aug[:, :], NEG), writes=[sub])
        P.op(POOL, lambda: nc.gpsimd.memset(stat[:, :], EPS), writes=[statb])
        P.op(POOL, lambda: nc.gpsimd.memset(negh[:, :], -0.5), writes=[constb])
        P.dma(SP, rb_aug[0:32, :], rb_d.ap(), su_sem, writes=[sub])
        P.dma(SP, onehot[0:33, :], onehot_d.ap(), su_sem, writes=[])
        P.dma(SP, ident[:, :], ident_d.ap(), su_sem, writes=[])
        P.dma(SP, band[:, :, :], band_d.ap().rearrange("p (a b) -> p a b", b=144), su_sem, writes=[])
        P.dma(SP, valid[:, :], valid_d.ap(), su_sem, writes=[])
        P.dma(SP, sel2[:, :], sel2_d.ap(), su_sem, writes=[])
        P.dma(SP, gain[:, :], ng_d.ap().rearrange("(dc p) -> p dc", p=128), su_sem, writes=[], allow_slow_non_contiguous=True)
        P.dma(SP, bg_f[:, :].rearrange("p (j oc) -> p j oc", j=2),
              bg_d.ap().rearrange("j (oc p) -> p j oc", p=128), su_sem, writes=[], allow_slow_non_contiguous=True)
        P.dma(SP, ps_f[:, :], psc_d.ap().rearrange("(g p) -> p g", p=128), su_sem, writes=[], allow_slow_non_contiguous=True)
        P.dma(SP, fgb[:, :], fg_d.ap().partition_broadcast(128), su_sem, writes=[])
        P.dma(SP, wpool_f[:, :, :], wpool_d.ap().rearrange("g c d -> c g d"), su_sem, writes=[])
        sub.w = (su_sem, su_sem.val)
        constb.w = (su_sem, su_sem.val)
        P.op(DVE, lambda: nc.vector.tensor_copy(out=wpool[:, :, :], in_=wpool_f[:, :, :]), reads=[constb], writes=[constb])
        P.op(DVE, lambda: nc.vector.tensor_scalar_mul(out=hbias[:, :], in0=bg_f[:, :], scalar1=0.5), reads=[constb], writes=[constb])
        P.op(DVE, lambda: nc.vector.tensor_scalar_mul(out=psh[:, :], in0=ps_f[:, :], scalar1=0.5), reads=[constb], writes=[constb])

        SETG = (0, 1, 2, 2, 2, 2, 2)
        for sidx in range(7):
            pbank, pbb = next_pj()
            P.op(PE, lambda sidx=sidx, pbank=pbank: nc.tensor.matmul(pbank[0:24, 0:512], lhsT=rb_aug[0:33, 0:24],
                                                                     rhs=onehot[0:33, sidx * 512:(sidx + 1) * 512],
                                                                     start=True, stop=True),
                 reads=[sub], writes=[pbb])
            P.op(ACT, lambda sidx=sidx, pbank=pbank: nc.scalar.activation(out=pv[0:24, sidx, :], in_=pbank[0:24, 0:512], func=AF.Exp),
                 reads=[pbb], writes=[pvb])
        for sidx in range(7):
            g = SETG[sidx]
            dst = pd_d.ap()[8 * sidx:8 * sidx + 8, :].rearrange("h (r i) -> h r i", i=512)
            src = pv[8 * g:8 * g + 8, sidx, :].unsqueeze(1).to_broadcast([8, 128, 512])
            P.dma(SP, dst, src, su_semB, reads=[pvb], writes=[])
        pdb.w = (su_semB, su_semB.val)

        def toeplitz_src(row, off, nrow, ncol):
            return bass.AP(pd_d, row * 128 * 512 + off, [[511, nrow], [1, ncol]])

        for h in range(8):
            P.dma(SP, mkall[:, h, 0:256], toeplitz_src(h, 64, 128, 256), su_semC, reads=[pdb], writes=[])
            P.dma(SP, mkall[:, h, 256:640], toeplitz_src(8 + h, 0, 128, 384), su_semC, reads=[pdb], writes=[])
            for di in range(5):
                P.dma(SP, mkall[:, h, 640 + 128 * di:640 + 128 * di + 128], toeplitz_src(8 * (2 + di) + h, 128, 128, 128),
                      su_semC, reads=[pdb], writes=[])
        mkb.w = (su_semC, su_semC.val)
        P.dma(SP, mk_d.ap().rearrange("h p c -> p h c"), mkall[:, :, :], su_semD, reads=[mkb])
        P.barrier()

    xb = [P.sb(f"xb{t}", [128, D], F32) for t in range(4)]
    xbb = [Buf() for _ in range(4)]
    xb_sem = [P.sem("xb") for _ in range(4)]
    xs = [P.sb(f"xs{t}", [128, D], BF16) for t in range(4)]
    xsb = [Buf() for _ in range(4)]
    hT = P.sb("hT", [128, 3, 8, T], BF16)
    hTb = [Buf() for _ in range(3)]
    kT = [P.sb("kT0", [128, 4, 3, T], BF16), P.sb("kT1", [128, 4, 3, T], BF16), P.sb("kT2", [128, 4, 5, T], BF16)]
    NSLOT = (3, 3, 5)
    kTb = [[Buf() for _ in range(NSLOT[g])] for g in range(3)]
    va = [P.sb(f"va{g}", [128, NSLOT[g], 4, 8, 65], BF16) for g in range(3)]
    vab = [[Buf() for _ in range(NSLOT[g])] for g in range(3)]
    qT = P.sb("qT", [128, 3, T], BF16)
    qTb = Buf()
    wbuf = [P.sb(f"wbuf{i}", [128, 8, 256], BF16) for i in range(4)]
    wbufb = [Buf() for _ in range(4)]
    wb_sem = [P.sem("wb") for _ in range(4)]
    mk = [P.sb(f"mk{i}", [128, MKW], BF16) for i in range(2)]
    mkbb = [Buf() for _ in range(2)]
    mk_sem = [P.sem("mk") for _ in range(2)]
    NU = 6
    uring = P.sb("uring", [128, NU, 512], BF16)
    uringb = [Buf() for _ in range(NU)]
    BT = P.sb("BT", [128, 4, T], BF16)
    BTb = [Buf() for _ in range(4)]
    AT = P.sb("AT", [128, 4, T], BF16)
    ATb = [Buf() for _ in range(4)]
    E0 = [P.sb(f"E0_{i}", [128, 512], BF16) for i in range(2)]
    E0hb = [[Buf(), Buf()] for _ in range(2)]
    Em = [P.sb(f"Em_{i}", [128, 512], BF16) for i in range(2)]
    Emhb = [[Buf(), Buf()] for _ in range(2)]
    Emb = [None, None]
    s12 = P.sb("s12", [128, T], F32)
    s12b = Buf()
    att = P.sb("att", [128, T], F32)
    attb = Buf()
    th = P.sb("th", [128, T], F32)
    thb = Buf()
    sgs = P.sb("sgs", [128, 4, T], BF16)
    sgsb = [Buf() for _ in range(4)]
    pooledT = Em[0]
    sgp = P.sb("sgp", [128, T], BF16)
    sgpb = Buf()
    den_sem = P.sem("den")
    merged = P.sb("merged", [128, 8, T], BF16)
    mergedb = [Buf() for _ in range(8)]
    gy, gyb = s12, s12b
    res = AT[:, :, :].rearrange("p a b -> p (a b)").bitcast(F32)
    xres0 = BT[:, :, :].rearrange("p a b -> p (a b)").bitcast(F32)
    xres1 = sgs[:, :, :].rearrange("p a b -> p (a b)").bitcast(F32)
    resb_l = ATb
    xres_bufs = [(xres0, BTb), (xres1, sgsb)]
    xres_sems = [P.sem("xres0"), P.sem("xres1")]

    def load_xres(c, ti):
        k = (ti + 1) % 2
        xr, xrb = xres_bufs[k]
        P.dma(POOL, xr[:, :], x_d.ap()[c * T + 128 * ti:c * T + 128 * ti + 128, :], xres_sems[k], writes=xrb)
    xres_sem = P.sem("xres")
    out_sems = [P.sem("out0"), P.sem("out1")]

    P.op(POOL, lambda: nc.gpsimd.memset(att[:, :], 1.0), writes=[attb])

    wstate = {"n": 0}

    def load_block(name):
        i = wstate["n"] % 4
        wstate["n"] += 1
        P.dma(SP, wbuf[i][:, :, :].rearrange("p a b -> p (a b)"), wb_d.ap()[BLK[name]], wb_sem[i], writes=[wbufb[i]])
        return wbuf[i], wbufb[i]

    def hslot(c):
        return c % 3

    def tok_ap(c, g, idx):
        s = hslot(c)
        if g == 0:
            return lambda dc: hT[:, s, dc, 128 * idx:128 * idx + 128]
        return lambda dc: hT[:, s, dc, :].rearrange("p (m r) -> p r m", r=4)[:, idx, :]

    def stageA1a_load(c):
        for t in range(4):
            P.dma(SP, xb[t][:, :], x_d.ap()[c * T + 128 * t:c * T + 128 * t + 128, :], xb_sem[t], writes=[xbb[t]])

    def stageA1a(c, load=True):
        if load:
            stageA1a_load(c)
        for t in range(4):
            P.op(ACT, lambda t=t: nc.scalar.activation(out=xs[t][:, :], in_=xb[t][:, :], func=AF.Square,
                                                       accum_out=stat[:, t:t + 1]),
                 reads=[xbb[t]], writes=[xsb[t], statb])
        P.op(DVE, lambda: nc.vector.tensor_scalar(out=stat[:, 4:8], in0=stat[:, 0:4], scalar1=1.0 / D, scalar2=EPS,
                                                  op0=ALU.mult, op1=ALU.add),
             reads=[statb], writes=[statb])
        P.op(POOL, lambda: nc.gpsimd.tensor_tensor(out=stat[:, 8:12], in0=stat[:, 4:8], in1=negh[:, 0:4], op=ALU.pow),
             reads=[statb, constb], writes=[statb])
        for t in range(4):
            P.op(POOL, lambda t=t: nc.gpsimd.tensor_tensor(out=xs[t][:, :], in0=xb[t][:, :],
                                                           in1=stat[:, 8 + t:9 + t].to_broadcast([128, D]), op=ALU.mult),
                 reads=[xbb[t], statb], writes=[xsb[t]])

    def evac_copy(i, out, in_, reads, writes):
        if i % 2 == 0:
            P.op(ACT, lambda: nc.scalar.copy(out=out, in_=in_), reads=reads, writes=writes)
        else:
            P.op(DVE, lambda: nc.vector.tensor_copy(out=out, in_=in_), reads=reads, writes=writes)

    def kv_stage(c, g):
        s = hslot(c)
        ks = c % NSLOT[g]
        for hb in range(2):
            wbk, wbb = load_block(f"K{g}{hb}")
            for seg in range(2):
                pair = 2 * hb + seg
                bank, bb = next_pj()
                for dc in range(8):
                    P.op(PE, lambda dc=dc, bank=bank, wbk=wbk, seg=seg: nc.tensor.matmul(
                        bank[:, 0:T], lhsT=wbk[:, dc, 128 * seg:128 * seg + 128], rhs=hT[:, s, dc, :],
                        start=(dc == 0), stop=(dc == 7)),
                         reads=[wbb, hTb[s]], writes=[bb], signal=(dc == 7))
                if g == 0:
                    src, dst = bank[:, 0:T], kT[0][:, pair, ks, :]
                else:
                    src = bank[:, 0:T].rearrange("p (m r) -> p r m", r=4)
                    dst = kT[g][:, pair, ks, :].rearrange("p (r m) -> p r m", r=4)
                evac_copy(pair, dst, src, [bb], [kTb[g][ks]])
        for hb in range(2):
            wbk, wbb = load_block(f"V{g}{hb}")
            for idx in range(4):
                bank, bb = next_pj()
                tf = tok_ap(c, g, idx)
                for dc in range(8):
                    P.op(PE, lambda dc=dc, bank=bank, wbk=wbk, tf=tf: nc.tensor.matmul(
                        bank[:, 0:256], lhsT=tf(dc), rhs=wbk[:, dc, :], start=(dc == 0), stop=(dc == 7)),
                         reads=[wbb, hTb[s]], writes=[bb], signal=(dc == 7))
                dst = va[g][:, ks, idx, 4 * hb:4 * hb + 4, 0:64]
                src = bank[:, 0:256].rearrange("p (h d) -> p h d", d=64)
                evac_copy(idx + 1, dst, src, [bb], [vab[g][ks]])
        P.op(POOL, lambda: nc.gpsimd.tensor_copy(out=va[g][:, ks, :, :, 64],
                                                 in_=valid[:, c:c + 1].unsqueeze(2).to_broadcast([128, 4, 8])),
             reads=[constb], writes=[vab[g][ks]])

    def stageA1b(c):
        s = hslot(c)
        for d2 in range(4):
            bank, bb = next_pj()
            bankh = bank[:, :].bitcast(BF16)
            for dd in range(2):
                dc = 2 * d2 + dd
                for t in range(4):
                    P.op(PE, lambda dc=dc, t=t, dd=dd, bankh=bankh: nc.tensor.transpose(
                        bankh[:, dd * 512 + 128 * t:dd * 512 + 128 * t + 128], xs[t][:, 128 * dc:128 * dc + 128], ident[:, :]),
                         reads=[xsb[t], constb], writes=[bb], signal=(dd == 1 and t == 3))
            for dd in range(2):
                dc = 2 * d2 + dd
                P.op(DVE, lambda dc=dc, dd=dd, bankh=bankh: nc.vector.tensor_scalar_mul(
                    out=hT[:, s, dc, :], in0=bankh[:, dd * 512:dd * 512 + 512], scalar1=gain[:, dc:dc + 1]),
                     reads=[bb, constb], writes=[hTb[s]])
        kv_stage(c, 2)

    def u_tile(tt):
        c, ti = tt // 4, tt % 4
        s = hslot(c)
        for hb in range(2):
            wbk, wbb = load_block(f"AI{hb}")
            bank, bb = next_pj()
            for dc in range(8):
                P.op(PE, lambda dc=dc, bank=bank, wbk=wbk: nc.tensor.matmul(
                    bank[:, 0:256], lhsT=hT[:, s, dc, 128 * ti:128 * ti + 128], rhs=wbk[:, dc, :],
                    start=(dc == 0), stop=(dc == 7)),
                     reads=[wbb, hTb[s]], writes=[bb], signal=(dc == 7))
            evac_copy(hb, uring[:, tt % NU, 256 * hb:256 * hb + 256], bank[:, 0:256], [bb], [uringb[tt % NU]])

    def u_tiles(tts):
        groups = {}
        for tt in tts:
            groups.setdefault(tt, None)
        for hb in range(2):
            wbk, wbb = load_block(f"AI{hb}")
            for tt in tts:
                c, ti = tt // 4, tt % 4
                s = hslot(c)
                bank, bb = next_pj()
                for dc in range(8):
                    P.op(PE, lambda dc=dc, bank=bank, wbk=wbk, s=s, ti=ti: nc.tensor.matmul(
                        bank[:, 0:256], lhsT=hT[:, s, dc, 128 * ti:128 * ti + 128], rhs=wbk[:, dc, :],
                        start=(dc == 0), stop=(dc == 7)),
                         reads=[wbb, hTb[s]], writes=[bb], signal=(dc == 7))
                evac_copy(tt + hb, uring[:, tt % NU, 256 * hb:256 * hb + 256], bank[:, 0:256], [bb], [uringb[tt % NU]])

    estate = {"n": 0}
    mstate = {"n": 0}

    def attention_pair(c, pr):
        hbanks = []
        for h in (2 * pr, 2 * pr + 1):
            banks = []
            pb = 64 * (h % 2)
            ob = h % 2
            o0, o0b, o12, o12b = o0s[ob], o0bs[ob], o12s[ob], o12bs[ob]
            mi = h % 2
            mkt, maskb = mk[mi], mkbb[mi]
            if mstate.get(mi) != h:
                P.dma(POOL, mkt[:, :], mk_d.ap()[h], mk_sem[mi], writes=[maskb])
                mstate[mi] = h
            tiles = [(c - 1, 3, 0, 64), (c, 0, 0, 192), (c, 1, 64, 320), (c, 2, 192, 448), (c, 3, 320, 512), (c + 1, 0, 448, 512)]
            first_o0 = True
            for half in range(2):
                rec = {"st": [], "mask": [], "pv": [], "maskb": maskb, "finish": None}
                col = 0
                for ti_, (cc, t_, q0, q1) in enumerate(tiles[3 * half:3 * half + 3]):
                    n = q1 - q0
                    ks = cc % 3
                    rec["st"].append((lambda stbank, col=col, n=n, ks=ks, t_=t_, q0=q0, q1=q1, pb=pb: nc.tensor.matmul(
                        stbank[:, col:col + n], lhsT=kT[0][pb:pb + 64, pr, ks, 128 * t_:128 * t_ + 128],
                        rhs=qT[pb:pb + 64, 0, q0:q1], start=True, stop=True), [kTb[0][ks], qTb]))
                    qq0 = (c * T + q0) - (cc * T + 128 * t_) + 128
                    rec["mask"].append((col, col + n, mkt[:, qq0 - 64:qq0 - 64 + n], None))
                    rec["pv"].append((lambda em, col=col, n=n, ks=ks, t_=t_, q0=q0, q1=q1, st_=first_o0, o0=o0, h=h: nc.tensor.matmul(
                        o0[0:65, q0:q1], lhsT=va[0][:, ks, t_, h, :], rhs=em[:, col:col + n],
                        start=st_, stop=False, skip_group_check=True), [vab[0][ks]], [o0b], col))
                    first_o0 = False
                    col += n
                rec["ncols"] = col
                banks.append(rec)
            first_o12 = True
            for g, dcs in ((1, (-1, 0, 1)), (2, (-2, -1, 0, 1, 2))):
                for di, dc_ in enumerate(dcs):
                    cc = c + dc_
                    ks = cc % NSLOT[g]
                    rec = {"st": [], "mask": [], "pv": [], "maskb": maskb, "finish": None, "ncols": 512}
                    for r4 in range(4):
                        rec["st"].append((lambda stbank, r4=r4, ks=ks, g=g, pb=pb: nc.tensor.matmul(
                            stbank[:, 128 * r4:128 * r4 + 128], lhsT=kT[g][pb:pb + 64, pr, ks, 128 * r4:128 * r4 + 128],
                            rhs=qT[pb:pb + 64, g, 128 * r4:128 * r4 + 128], start=True, stop=True), [kTb[g][ks], qTb]))
                        rec["pv"].append((lambda em, r4=r4, ks=ks, g=g, st_=first_o12, o12=o12, h=h: nc.tensor.matmul(
                            o12[0:65, 128 * r4:128 * r4 + 128], lhsT=va[g][:, ks, r4, h, :], rhs=em[:, 128 * r4:128 * r4 + 128],
                            start=st_, stop=False, skip_group_check=True), [vab[g][ks]], [o12b], 128 * r4))
                        first_o12 = False
                    moff = (256 + 128 - 128 * dc_) if g == 1 else (640 + 128 * di)
                    for hf_ in range(2):
                        rec["mask"].append((256 * hf_, 256 * hf_ + 256, mkt[:, moff:moff + 128].unsqueeze(1).to_broadcast([128, 2, 128]), 128))
                    banks.append(rec)

            def finish(h=h, pb=pb, o0=o0, o0b=o0b, o12=o12, o12b=o12b):
                P.op(ACT, lambda: nc.scalar.copy(out=s12[0:65, :].rearrange("p (m r) -> p r m", r=4),
                                                 in_=o12[0:65, :].rearrange("p (r m) -> p r m", r=4)),
                     reads=[o12b], writes=[s12b])
                if c == DBG_CHUNK:
                    dump(f"s12_h{h}", s12[0:65, :], [s12b])
                P.op(DVE, lambda: nc.vector.tensor_tensor(out=BT[pb:pb + 64, pr, :], in0=o0[0:64, :], in1=s12[0:64, :], op=ALU.add),
                     reads=[o0b, s12b], writes=[BTb[pr]])
                P.op(DVE, lambda: nc.vector.tensor_tensor(out=s12[64:65, :], in0=o0[64:65, :], in1=s12[64:65, :], op=ALU.add),
                     reads=[o0b, s12b], writes=[s12b])
                row = 32 * (h // 4) + (h % 4)
                P.dma(POOL, att[row:row + 1, :], s12[64:65, :], den_sem, reads=[s12b], writes=[attb])
                hn = (h + 2) % 8
                P.dma(POOL, mk[h % 2][:, :], mk_d.ap()[hn], mk_sem[h % 2], writes=[mkbb[h % 2]])
                mstate[h % 2] = hn
            banks[-1]["finish"] = finish
            hbanks.append(banks)

        def emit_st(stage):
            recs = [hbanks[0][stage], hbanks[1][stage]]
            for rec in recs:
                rec["stbank"], rec["stbb"] = next_st()
            nst = len(recs[0]["st"])
            for i in range(nst):
                for rec in recs:
                    fn, reads = rec["st"][i]
                    P.op(PE, lambda fn=fn, rec=rec: fn(rec["stbank"]), reads=reads, writes=[rec["stbb"]], signal=(i == nst - 1))

        def emit_rest(rec):
            k = estate["n"] % 2
            estate["n"] += 1
            stbank, stbb, ncols, maskb = rec["stbank"], rec["stbb"], rec["ncols"], rec["maskb"]
            for hf_ in range(2):
                lo, hi = 256 * hf_, min(256 * hf_ + 256, ncols)
                if hi <= lo:
                    continue
                P.op(ACT, lambda lo=lo, hi=hi: nc.scalar.activation(out=E0[k][:, lo:hi], in_=stbank[:, lo:hi], func=AF.Exp, scale=0.125),
                     reads=[stbb], writes=[E0hb[k][hf_]])
                for (c0, c1, m_ap, shape3) in rec["mask"]:
                    if not (lo <= c0 and c1 <= hi):
                        continue
                    if shape3 is None:
                        o_ap, i_ap = Em[k][:, c0:c1], E0[k][:, c0:c1]
                    else:
                        o_ap = Em[k][:, c0:c1].rearrange("p (a b) -> p a b", b=shape3)
                        i_ap = E0[k][:, c0:c1].rearrange("p (a b) -> p a b", b=shape3)
                    P.op(DVE, lambda o_ap=o_ap, i_ap=i_ap, m_ap=m_ap: nc.vector.tensor_tensor(out=o_ap, in0=i_ap, in1=m_ap, op=ALU.mult),
                         reads=[E0hb[k][hf_], maskb], writes=[Emhb[k][hf_]])
                pvs = [p for p in rec["pv"] if lo <= p[3] < hi]
                for i, (fn, reads, writes, _c0) in enumerate(pvs):
                    P.op(PE, lambda fn=fn: fn(Em[k]), reads=[Emhb[k][hf_]] + reads, writes=writes, signal=(i == len(pvs) - 1))
            if rec["finish"] is not None:
                rec["finish"]()

        nstage = len(hbanks[0])
        emit_st(0)
        for n in range(nstage):
            if n + 1 < nstage:
                emit_st(n + 1)
            emit_rest(hbanks[0][n])
            emit_rest(hbanks[1][n])

    def attention_recip(c):
        P.op(DVE, lambda: nc.vector.reciprocal(out=att[0:36, :], in_=att[0:36, :]), reads=[attb], writes=[attb])

    def attention_normalise(c):
        for pr in range(4):
            base = 32 * (pr // 2)
            a = pr % 2
            bcbank, bcb = next_pj()
            P.op(PE, lambda: nc.tensor.matmul(bcbank[:, 0:T], lhsT=sel2[base:base + 4, 128 * a:128 * a + 128],
                                              rhs=att[base:base + 4, :], start=True, stop=True),
                 reads=[attb, constb], writes=[bcb])
            P.op(DVE, lambda: nc.vector.scalar_tensor_tensor(out=th[:, :], in0=bcbank[:, 0:T], scalar=0.5,
                                                             in1=sgs[:, pr, :], op0=ALU.mult, op1=ALU.mult),
                 reads=[bcb, sgsb[pr]], writes=[thb])
            P.op(POOL, lambda: nc.gpsimd.tensor_tensor(out=BT[:, pr, :], in0=BT[:, pr, :], in1=th[:, :], op=ALU.mult),
                 reads=[thb, BTb[pr]], writes=[BTb[pr]])

    def silu_half(bank, bb, dst, dstb):
        P.op(ACT, lambda: nc.scalar.activation(out=th[:, :], in_=bank[:, 0:T], func=AF.Tanh, scale=0.5), reads=[bb], writes=[thb])
        P.op(DVE, lambda: nc.vector.scalar_tensor_tensor(out=dst, in0=th[:, :], scalar=1.0, in1=bank[:, 0:T],
                                                         op0=ALU.add, op1=ALU.mult),
             reads=[thb, bb], writes=[dstb])

    def proj_fm(wbk, wbb, seg, s):
        bank, bb = next_pj()
        for dc in range(8):
            P.op(PE, lambda dc=dc: nc.tensor.matmul(bank[:, 0:T], lhsT=wbk[:, dc, 128 * seg:128 * seg + 128],
                                                    rhs=hT[:, s, dc, :], start=(dc == 0), stop=(dc == 7)),
                 reads=[wbb, hTb[s]], writes=[bb], signal=(dc == 7))
        return bank, bb

    def stageB(c):
        s = hslot(c)
        rot["wide"] = False
        for pr in range(4):
            wa, wab = load_block(f"QA{pr}")
            wq, wqb = load_block(f"QB{pr}")
            for g in range(3):
                wbk, wbb, seg = (wa, wab, g) if g < 2 else (wq, wqb, 0)
                bank, bb = proj_fm(wbk, wbb, seg, s)
                if g == 0:
                    src, dst = bank[:, 0:T], qT[:, 0, :]
                else:
                    src = bank[:, 0:T].rearrange("p (m r) -> p r m", r=4)
                    dst = qT[:, g, :].rearrange("p (r m) -> p r m", r=4)
                evac_copy(1 if g < 2 else 0, dst, src, [bb], [qTb])
            bank, bb = proj_fm(wq, wqb, 1, s)
            silu_half(bank, bb, sgs[:, pr, :], sgsb[pr])
            if c == DBG_CHUNK:
                dump(f"qT_p{pr}", qT[:, :, :], [qTb])
                dump(f"sg_p{pr}", sgs[:, pr, :], [sgsb[pr]])
                if pr == 0:
                    dump("hT", hT[:, s, :, :], [hTb[s]])
                    for g in range(3):
                        dump(f"kT{g}", kT[g][:, :, :, :], kTb[g])
                        dump(f"va{g}", va[g][:, :, :, :, :].rearrange("p a b c d -> p (a b c d)"), vab[g])
            attention_pair(c, pr)
        rot["wide"] = True
        attention_recip(c)
        pbufs = [(Em[0], Emhb[0]), (Em[1], Emhb[1]), (E0[0], E0hb[0]), (E0[1], E0hb[1])]
        bands = []
        for g in range(4):
            bank, bb = next_pj()
            first = True
            for tp in range(-1, 5):
                tt = 4 * c + tp
                q0, q1 = max(0, 128 * tp - 8), min(T, 128 * tp + 136)
                qq0 = q0 - (128 * tp - 8)
                kind = 1
                if c == MAIN0 and tp == 0:
                    kind = 0
                if c == MAIN1 - 1 and tp == 3:
                    kind = 2
                P.op(PE, lambda tt=tt, q0=q0, q1=q1, qq0=qq0, kind=kind, g=g, first=first, bank=bank: nc.tensor.matmul(
                    bank[:, q0:q1], lhsT=uring[:, tt % NU, 128 * g:128 * g + 128],
                    rhs=band[:, 4 * kind + g, qq0:qq0 + (q1 - q0)], start=first, stop=False, skip_group_check=True),
                     reads=[uringb[tt % NU], constb], writes=[bb], signal=(tp == 4))
                first = False
            bands.append((bank, bb))
        for g in range(4):
            bank, bb = bands[g]
            evac_copy(g, pbufs[g][0][:, :], bank[:, 0:T], [bb], pbufs[g][1])
            if c == DBG_CHUNK:
                dump(f"pooledT_g{g}", pbufs[g][0][:, :], pbufs[g][1])
        mg = []
        for g in range(4):
            mbank, mbb = next_pj()
            P.op(PE, lambda g=g, mbank=mbank: nc.tensor.matmul(mbank[:, 0:T], lhsT=wpool[:, g, :], rhs=pbufs[g][0][:, :], start=True, stop=True),
                 reads=pbufs[g][1] + [constb], writes=[mbb])
            if g % 2 == 0:
                wg, wgb = load_block(f"AG{g // 2}")
            gbank, gbb = proj_fm(wg, wgb, g % 2, s)
            mg.append((mbank, mbb, gbank, gbb))
        for g in range(4):
            mbank, mbb, gbank, gbb = mg[g]
            silu_half(gbank, gbb, sgp[:, :], sgpb)
            P.op(DVE, lambda g=g, mbank=mbank: nc.vector.scalar_tensor_tensor(out=AT[:, g, :], in0=mbank[:, 0:T], scalar=psh[:, g:g + 1],
                                                                             in1=sgp[:, :], op0=ALU.mult, op1=ALU.mult),
                 reads=[mbb, sgpb, constb], writes=[ATb[g]])
        attention_normalise(c)
        if c == DBG_CHUNK:
            dump("BT", BT[:, :, :], BTb)
            dump("uring", uring[:, :, :], uringb)
        if c == DBG_CHUNK:
            dump("AT", AT[:, :, :], ATb)
        load_xres(c, 0)
        for q in range(4):
            wp, wpb_ = load_block(f"PAB{q}")
            wga, wgab = load_block(f"GA{q}")
            wgb_, wgbb = load_block(f"GB{q}")
            for seg in range(2):
                oc = 2 * q + seg
                for br in range(2):
                    ybank, ybb = next_pj()
                    src_t, src_b = (AT, ATb) if br == 0 else (BT, BTb)
                    for kc in range(4):
                        P.op(PE, lambda kc=kc, br=br, src_t=src_t: nc.tensor.matmul(
                            ybank[:, 0:T], lhsT=wp[:, 4 * br + kc, 128 * seg:128 * seg + 128], rhs=src_t[:, kc, :],
                            start=(kc == 0), stop=(kc == 3)),
                             reads=[wpb_, src_b[kc]], writes=[ybb], signal=(kc == 3))
                    gw, gwb = (wga, wgab) if br == 0 else (wgb_, wgbb)
                    gbank, gbb = proj_fm(gw, gwb, seg, s)
                    P.op(ACT, lambda br=br, oc=oc, gbank=gbank: nc.scalar.activation(
                        out=th[:, :], in_=gbank[:, 0:T], func=AF.Tanh, scale=0.5, bias=hbias[:, 8 * br + oc:8 * br + oc + 1]),
                         reads=[gbb, constb], writes=[thb])
                    if br == 0:
                        P.op(DVE, lambda ybank=ybank: nc.vector.scalar_tensor_tensor(
                            out=gy[:, :], in0=th[:, :], scalar=1.0, in1=ybank[:, 0:T], op0=ALU.add, op1=ALU.mult),
                             reads=[thb, ybb], writes=[gyb])
                    else:
                        P.op(DVE, lambda ybank=ybank: nc.vector.scalar_tensor_tensor(
                            out=att[:, :], in0=th[:, :], scalar=1.0, in1=ybank[:, 0:T], op0=ALU.add, op1=ALU.mult),
                             reads=[thb, ybb], writes=[attb])
                        P.op(POOL, lambda oc=oc: nc.gpsimd.tensor_tensor(out=merged[:, oc, :], in0=gy[:, :], in1=att[:, :], op=ALU.add),
                             reads=[gyb, attb], writes=[mergedb[oc]])
        if c == DBG_CHUNK:
            dump("merged", merged[:, :, :], mergedb)
        wo = [load_block(f"WO{q}") for q in range(4)]
        load_xres(c, 1)
        for ti in range(4):
            row0 = (c - MAIN0) * T + 128 * ti
            xr, xrb = xres_bufs[(ti + 1) % 2]
            for half in range(2):
                obank, obb = next_pj()
                first = True
                for qq in range(2):
                    wbk, wbb = wo[2 * half + qq]
                    for oc in range(8):
                        last = (qq == 1 and oc == 7)
                        P.op(PE, lambda oc=oc, qq=qq, wbk=wbk, first=first, obank=obank: nc.tensor.matmul(
                            obank[:, 256 * qq:256 * qq + 256], lhsT=merged[:, oc, 128 * ti:128 * ti + 128], rhs=wbk[:, oc, :],
                            start=first, stop=False, skip_group_check=True),
                             reads=[wbb, mergedb[oc]], writes=[obb], signal=last)
                        first = False
                P.op(DVE, lambda half=half, obank=obank, xr=xr: nc.vector.scalar_tensor_tensor(
                    out=res[:, 512 * half:512 * half + 512], in0=obank[:, 0:512], scalar=0.5,
                    in1=xr[:, 512 * half:512 * half + 512], op0=ALU.mult, op1=ALU.add),
                     reads=[obb] + xrb, writes=resb_l)
            P.op(ACT, lambda: nc.scalar.activation(out=th[:, :].bitcast(BF16), in_=res[:, :], func=AF.Square, accum_out=fstat[:, 0:1]),
                 reads=resb_l, writes=[thb, fstatb])
            P.op(DVE, lambda: nc.vector.tensor_scalar(out=fstat[:, 1:2], in0=fstat[:, 0:1], scalar1=1.0 / D, scalar2=EPS,
                                                      op0=ALU.mult, op1=ALU.add),
                 reads=[fstatb], writes=[fstatb])
            P.op(POOL, lambda: nc.gpsimd.tensor_tensor(out=fstat[:, 2:3], in0=fstat[:, 1:2], in1=negh[:, 0:1], op=ALU.pow),
                 reads=[fstatb, constb], writes=[fstatb])
            P.op(DVE, lambda xr=xr: nc.vector.scalar_tensor_tensor(out=xr[:, :], in0=res[:, :], scalar=fstat[:, 2:3], in1=fgb[:, :],
                                                                   op0=ALU.mult, op1=ALU.mult),
                 reads=resb_l + [fstatb, constb], writes=xrb)
            P.dma(POOL, out_d.ap()[row0:row0 + 128, :], xr[:, :], out_sems[(ti + 1) % 2], reads=xrb)
            if ti + 2 < 4:
                load_xres(c, ti + 2)

    P.op(POOL, lambda: nc.gpsimd.memset(kT[2][:, :, 0:2, :], 0.0), writes=[kTb[2][0], kTb[2][1]])
    P.op(POOL, lambda: nc.gpsimd.memset(va[2][:, 0:2, :, :, :], 0.0), writes=[vab[2][0], vab[2][1]])
    for g in (0, 1):
        P.op(POOL, lambda g=g: nc.gpsimd.memset(kT[g][:, :, 1, :], 0.0), writes=[kTb[g][1]])
        P.op(POOL, lambda g=g: nc.gpsimd.memset(va[g][:, 1, :, :, :], 0.0), writes=[vab[g][1]])
    P.op(POOL, lambda: nc.gpsimd.memset(uring[:, 7 % NU, :], 0.0), writes=[uringb[7 % NU]])
    stageA1a(2); stageA1b(2)
    stageA1a(3); stageA1b(3)
    for g in (0, 1):
        kv_stage(2, g)
    u_tiles([8])
    for i in range(MAIN0, MAIN1):
        if i + 2 < NCH:
            stageA1a_load(i + 2)
        kv_stage(i + 1, 0)
        if i + 2 < NCH:
            stageA1a(i + 2, load=False)
        kv_stage(i + 1, 1)
        u_tiles([4 * i + 1, 4 * i + 2, 4 * i + 3, 4 * i + 4])
        if i + 2 < NCH:
            stageA1b(i + 2)
        stageB(i)
    for osem in out_sems:
        POOL.h.wait_ge(osem.h, osem.val)
        SP.h.wait_ge(osem.h, osem.val)
    if debug and dbg_sem.val:
        SP.h.wait_ge(dbg_sem.h, dbg_sem.val)
    es.close()
    return nc


def _t5_bucket_np(rel):
    nb = 16
    ret = (rel > 0).astype(np.int32) * nb
    n = np.abs(rel)
    max_exact = nb // 2
    nf = np.maximum(n, 1).astype(np.float32)
    large = max_exact + (np.log(nf / np.float32(max_exact)) / np.float32(math.log(1024 / max_exact))
                         * np.float32(nb - max_exact)).astype(np.int32)
    large = np.minimum(large, nb - 1)
    return ret + np.where(n < max_exact, n, large)


def _host_consts(hf):
    sign = 1 if hf == 0 else -1
    ident = np.eye(128, dtype=np.float32).astype(ml_dtypes.bfloat16)
    onehot = np.zeros((33, 7, 512), np.float32)
    idx = np.arange(512)
    for g in range(2):
        j = 128 - idx
        inwin = (idx >= 64) & (idx <= 192)
        b = _t5_bucket_np((sign * j * DIL[g]).astype(np.int32))
        b = np.where(inwin, b, 32)
        onehot[b, g, idx] = 1.0
    for di, dc_ in enumerate((-2, -1, 0, 1, 2)):
        delta = 128 - idx
        j = 32 * dc_ + delta // 4
        ok = (idx >= 1) & (idx <= 255) & (delta % 4 == 0) & (np.abs(j) <= 64)
        b = _t5_bucket_np((sign * j * DIL[2]).astype(np.int32))
        b = np.where(ok, b, 32)
        onehot[b, 2 + di, idx] = 1.0
    band = np.zeros((128, 3, 4, 144), np.float32)
    k = np.arange(128)[:, None]
    qp = (np.arange(144) - 8)[None, :]
    for g, w in enumerate((2, 4, 8, 16)):
        hw = w // 2
        if hf == 0:
            inw = ((k >= qp - hw) & (k <= qp + hw - 1)).astype(np.float32)
            cnt_first = (np.minimum(np.maximum(qp, 0), hw) + hw).astype(np.float32)
        else:
            inw = ((k >= qp - hw + 1) & (k <= qp + hw)).astype(np.float32)
            cnt_first = (hw + np.minimum(hw, np.maximum(qp, 0) + 1)).astype(np.float32)
        cnt_first = np.minimum(cnt_first, w)
        eye = (k == qp).astype(np.float32)
        cnt_mid = np.full((1, 144), float(w), np.float32)
        cnts = [cnt_first, cnt_mid, cnt_mid]
        for kind in range(3):
            band[:, kind, g, :] = inw / cnts[kind] - eye
    band = band.reshape(128, 12 * 144).astype(ml_dtypes.bfloat16)
    valid = np.ones((128, NCH), np.float32)
    valid[:, 0:2] = 0.0
    sel2 = np.zeros((128, 256), np.float32)
    for base in (0, 32):
        for r in range(4):
            for a in range(2):
                for m in range(128):
                    if r == 2 * a + m // 64:
                        sel2[base + r, 128 * a + m] = 1.0
    return ident, onehot.reshape(33, 7 * 512), band, valid, sel2


_CACHE = {}


def kernel(x, norm_gain, w_in, b_gate, rel_bias, w_pool, pool_scale, w_proj_a, w_proj_b, w_out, final_gain):
    x = np.asarray(x, np.float32)
    if "nc" not in _CACHE:
        _CACHE["nc"] = build_program()
    nc = _CACHE["nc"]
    common = {
        "w_in": np.ascontiguousarray(np.asarray(w_in, np.float32)[0]),
        "w_proj_a": np.ascontiguousarray(np.asarray(w_proj_a, np.float32)[0]),
        "w_proj_b": np.ascontiguousarray(np.asarray(w_proj_b, np.float32)[0]),
        "w_out": np.ascontiguousarray(np.asarray(w_out, np.float32)[0]),
        "w_pool": np.ascontiguousarray(np.asarray(w_pool, np.float32)[0]),
        "norm_gain": np.ascontiguousarray(np.asarray(norm_gain, np.float32)[0]),
        "final_gain": np.ascontiguousarray(np.asarray(final_gain, np.float32)),
        "b_gate": np.ascontiguousarray(np.asarray(b_gate, np.float32)[0]),
        "pool_scale": np.ascontiguousarray(np.asarray(pool_scale, np.float32)[0]),
        "rel_bias": np.ascontiguousarray(np.asarray(rel_bias, np.float32)),
    }
    in_maps = []
    for core in range(NCORES):
        b, hf = core // 2, core % 2
        xe = np.zeros((EXT, D), np.float32)
        if hf == 0:
            xe[1024:EXT] = x[b, 0:EXT - 1024]
        else:
            xe[1024:EXT] = x[b, SEQ - (EXT - 1024):SEQ][::-1]
        ident, onehot, band, valid, sel2 = _host_consts(hf)
        m = dict(common)
        m.update({"x": xe, "ident": ident, "onehot": onehot, "band": band, "valid": valid, "sel2": sel2})
        in_maps.append(m)
    res = run_bass_kernel_spmd(nc, in_maps, core_ids=list(range(NCORES)))
    out = np.empty((4, SEQ, D), np.float32)
    for core in range(NCORES):
        b, hf = core // 2, core % 2
        o = np.asarray(res.results[core]["out"], np.float32)
        out[b, hf * 4096:(hf + 1) * 4096] = o if hf == 0 else o[::-1]
    return out
```

```python
import math
from contextlib import ExitStack

import numpy as np
import ml_dtypes

import concourse.bass as bass
import concourse.mybir as mybir
from concourse.bass_utils import run_bass_kernel_spmd

F32 = mybir.dt.float32
BF16 = mybir.dt.bfloat16
ALU = mybir.AluOpType
AF = mybir.ActivationFunctionType

D = 1024
SEQ = 8192
NCORES = 8
T = 512
NCH = 12
EXT = NCH * T
MAIN0, MAIN1 = 2, 10
COL_AIN, COL_AGATE, COL_Q, COL_K, COL_V, COL_BGP, COL_G = 0, 512, 1024, 2560, 4096, 5632, 6144
DIL = (1, 4, 16)
NEG = -30000.0
EPS = 1e-6

BLK = {}
_blocks = []


def _add(name, segs):
    BLK[name] = len(_blocks)
    _blocks.append(segs)


for g in range(3):
    for hb in range(2):
        _add(f"K{g}{hb}", [("win", COL_K + 512 * g + 256 * hb), ("win", COL_K + 512 * g + 256 * hb + 128)])
        _add(f"V{g}{hb}", [("win", COL_V + 512 * g + 256 * hb), ("win", COL_V + 512 * g + 256 * hb + 128)])
for pr in range(4):
    _add(f"QA{pr}", [("win", COL_Q + 128 * pr), ("win", COL_Q + 512 + 128 * pr)])
    _add(f"QB{pr}", [("win", COL_Q + 1024 + 128 * pr), ("win", COL_BGP + 128 * pr)])
for hb in range(2):
    _add(f"AI{hb}", [("win", COL_AIN + 256 * hb), ("win", COL_AIN + 256 * hb + 128)])
    _add(f"AG{hb}", [("win", COL_AGATE + 256 * hb), ("win", COL_AGATE + 256 * hb + 128)])
for q in range(4):
    _add(f"GA{q}", [("win", COL_G + 256 * q), ("win", COL_G + 256 * q + 128)])
    _add(f"GB{q}", [("win", COL_G + 1024 + 256 * q), ("win", COL_G + 1024 + 256 * q + 128)])
for q in range(4):
    _add(f"PAB{q}", [("pab", 256 * q)])
    _add(f"WO{q}", [("wout", 256 * q)])
NBLK = len(_blocks)


class Sem:
    def __init__(self, h):
        self.h = h
        self.val = 0


class Eng:
    def __init__(self, name, h, sem, inorder=False):
        self.name, self.h, self.sem, self.inorder = name, h, sem, inorder
        self.waited = {}


class Buf:
    __slots__ = ("w", "r", "name")

    def __init__(self, name=""):
        self.w = None
        self.r = {}
        self.name = name


class Prog:
    def __init__(self, nc, es):
        self.nc, self.es = nc, es
        self.nsem = 0
        self.PE = Eng("pe", nc.tensor, self.sem("pe"), inorder=True)
        self.ACT = Eng("act", nc.scalar, self.sem("act"))
        self.DVE = Eng("dve", nc.vector, self.sem("dve"))
        self.POOL = Eng("pool", nc.gpsimd, self.sem("pool"))
        self.SP = Eng("sp", nc.sync, self.sem("sp"))
        self.engs = [self.PE, self.ACT, self.DVE, self.POOL, self.SP]
        self.all_sems = []

    def sem(self, name):
        self.nsem += 1
        s = Sem(self.es.enter_context(self.nc.semaphore(f"{name}_{self.nsem}")))
        if not hasattr(self, "_sems"):
            self._sems = []
        self._sems.append(s)
        return s

    def sb(self, name, shape, dt):
        return self.es.enter_context(self.nc.sbuf_tensor("s_" + name, list(shape), dt))

    def ps(self, name, shape, dt):
        return self.es.enter_context(self.nc.psum_tensor("p_" + name, list(shape), dt))

    def _waits(self, E, reads, writes):
        deps = {}

        def need(ev):
            if ev is None:
                return
            s, v = ev
            if deps.get(s, 0) < v:
                deps[s] = v

        for b in reads:
            need(b.w)
        for b in writes:
            need(b.w)
            for s, v in b.r.items():
                need((s, v))
        for s, v in deps.items():
            if s is E.sem and E.inorder:
                continue
            if E.waited.get(s, 0) < v:
                E.h.wait_ge(s.h, v)
                E.waited[s] = v

    def _record(self, ev, reads, writes):
        s, v = ev
        for b in reads:
            if b.r.get(s, 0) < v:
                b.r[s] = v
        for b in writes:
            b.w = ev
            b.r = {}

    def op(self, E, fn, reads=(), writes=(), signal=True):
        self._waits(E, reads, writes)
        ins = fn()
        if signal:
            E.sem.val += 1
            ins.then_inc(E.sem.h, 1)
            ev = (E.sem, E.sem.val)
        else:
            ev = (E.sem, E.sem.val + 1)
        self._record(ev, reads, writes)
        return ins

    def dma(self, E, out, in_, sem, reads=(), writes=(), **kw):
        self._waits(E, reads, writes)
        ins = E.h.dma_start(out=out, in_=in_, **kw)
        sem.val += 16
        ins.then_inc(sem.h, 16)
        self._record((sem, sem.val), reads, writes)
        return ins

    def barrier(self):
        for E in self.engs:
            for s in self._sems:
                if s.val > 0 and E.waited.get(s, 0) < s.val:
                    E.h.wait_ge(s.h, s.val)
                    E.waited[s] = s.val


def build_program(debug=False):
    nc = bass.Bass("TRN2", target_bir_lowering=False)
    dram = lambda n, s, dt, k="ExternalInput": nc.dram_tensor(n, list(s), dt, kind=k)
    x_d = dram("x", [EXT, D], F32)
    win_d = dram("w_in", [D, 8192], F32)
    wpa_d = dram("w_proj_a", [512, D], F32)
    wpb_d = dram("w_proj_b", [512, D], F32)
    wout_d = dram("w_out", [D, D], F32)
    wpool_d = dram("w_pool", [4, 128, 128], F32)
    ng_d = dram("norm_gain", [D], F32)
    fg_d = dram("final_gain", [D], F32)
    bg_d = dram("b_gate", [2, D], F32)
    psc_d = dram("pool_scale", [512], F32)
    rb_d = dram("rel_bias", [32, 24], F32)
    ident_d = dram("ident", [128, 128], BF16)
    onehot_d = dram("onehot", [33, 7 * 512], F32)
    band_d = dram("band", [128, 12 * 144], BF16)
    valid_d = dram("valid", [128, NCH], F32)
    sel2_d = dram("sel2", [128, 256], F32)
    out_d = dram("out", [8 * T, D], F32, "ExternalOutput")
    wb_d = dram("wb_scr", [NBLK, 128, 8 * 256], BF16, "Internal")
    pd_d = dram("pd_scr", [56, 128 * 512], BF16, "Internal")
    mk_d = dram("mk_scr", [8, 128, 256 + 384 + 640], BF16, "Internal")

    es = ExitStack()
    P = Prog(nc, es)
    PE, ACT, DVE, POOL, SP = P.PE, P.ACT, P.DVE, P.POOL, P.SP
    dbg_sem = P.sem("dbg")
    DBG_CHUNK = 2

    def dump(name, ap, reads, dt=None):
        if not debug:
            return
        shape = list(ap.shape)
        t = nc.dram_tensor("dbg_" + name, shape, dt or ap.dtype, kind="ExternalOutput")
        P.dma(SP, t.ap(), ap, dbg_sem, reads=reads)

    pj = [P.ps(f"pj{i}", [128, 512], F32) for i in range(2)]
    pjb = [Buf(f"pj{i}") for i in range(2)]
    st = [P.ps(f"st{i}", [128, 512], F32) for i in range(2)]
    stb = [Buf(f"st{i}") for i in range(2)]
    o0s = [P.ps(f"o0_{i}", [128, 512], F32) for i in range(2)]
    o0bs = [Buf() for _ in range(2)]
    o12s = [P.ps(f"o12_{i}", [128, 512], F32) for i in range(2)]
    o12bs = [Buf() for _ in range(2)]
    rot = {"pj": 0, "st": 0, "wide": True}
    wide_banks = [(pj[0], pjb[0]), (pj[1], pjb[1]), (st[0], stb[0]), (st[1], stb[1]),
                  (o0s[0], o0bs[0]), (o0s[1], o0bs[1]), (o12s[0], o12bs[0]), (o12s[1], o12bs[1])]

    def next_pj():
        lst = wide_banks if rot["wide"] else wide_banks[0:2]
        i = rot["pj"] % len(lst)
        rot["pj"] = i + 1
        return lst[i]

    st4 = [(st[0], stb[0]), (st[1], stb[1]), (pj[0], pjb[0]), (pj[1], pjb[1])]

    def next_st():
        i = rot["st"]
        rot["st"] = (i + 1) % 4
        return st4[i]

    with ExitStack() as es0:
        NST = 4
        stg = [es0.enter_context(nc.sbuf_tensor(f"stg{i}", [128, 8, 256], F32)) for i in range(NST)]
        stgb = [Buf() for _ in range(NST)]
        cvt = [es0.enter_context(nc.sbuf_tensor(f"cvt{i}", [128, 8 * 256], BF16)) for i in range(NST)]
        cvtb = [Buf() for _ in range(NST)]
        ld_sem = [P.sem("p0ld") for _ in range(NST)]
        st_sem = [P.sem("p0st") for _ in range(NST)]
        cv_engs = [DVE, ACT, POOL]
        for bi, segs in enumerate(_blocks):
            k = bi % NST
            if segs[0][0] == "win":
                for si, (_, c0) in enumerate(segs):
                    src = win_d.ap()[:, c0:c0 + 128].rearrange("(kc p) c -> p kc c", p=128)
                    P.dma(SP, stg[k][:, :, si * 128:(si + 1) * 128], src, ld_sem[k], writes=[stgb[k]] if si == 0 else [])
                stgb[k].w = (ld_sem[k], ld_sem[k].val)
            elif segs[0][0] == "pab":
                c0 = segs[0][1]
                P.dma(SP, stg[k][:, 0:4, :], wpa_d.ap()[:, c0:c0 + 256].rearrange("(kc p) c -> p kc c", p=128),
                      ld_sem[k], writes=[stgb[k]])
                P.dma(SP, stg[k][:, 4:8, :], wpb_d.ap()[:, c0:c0 + 256].rearrange("(kc p) c -> p kc c", p=128),
                      ld_sem[k], writes=[])
                stgb[k].w = (ld_sem[k], ld_sem[k].val)
            else:
                c0 = segs[0][1]
                P.dma(SP, stg[k][:, :, :], wout_d.ap()[:, c0:c0 + 256].rearrange("(kc p) c -> p kc c", p=128),
                      ld_sem[k], writes=[stgb[k]])
            E = cv_engs[bi % 3]
            src_ap = stg[k][:, :, :].rearrange("p a b -> p (a b)")
            if E is ACT:
                P.op(E, lambda s=src_ap, o=cvt[k]: nc.scalar.copy(out=o[:, :], in_=s), reads=[stgb[k]], writes=[cvtb[k]])
            else:
                P.op(E, lambda s=src_ap, o=cvt[k], e=E: e.h.tensor_copy(out=o[:, :], in_=s), reads=[stgb[k]], writes=[cvtb[k]])
            P.dma(POOL, wb_d.ap()[bi], cvt[k][:, :], st_sem[k], reads=[cvtb[k]])
        P.barrier()

    stat = P.sb("stat", [128, 16], F32)
    statb = Buf()
    ident = P.sb("ident", [128, 128], BF16)
    band = P.sb("band", [128, 12, 144], BF16)
    valid = P.sb("valid", [128, NCH], F32)
    gain = P.sb("gain", [128, 8], F32)
    hbias = P.sb("hbias", [128, 16], F32)
    psh = P.sb("psh", [128, 4], F32)
    fgb = P.sb("fgb", [128, D], F32)
    wpool = P.sb("wpool", [128, 4, 128], BF16)
    fstat = P.sb("fstat", [128, 8], F32)
    fstatb = Buf()
    sel2 = P.sb("sel2", [128, 256], F32)
    negh = P.sb("negh", [128, 4], F32)
    constb = Buf()
    MKW = 256 + 384 + 640

    su_sem = P.sem("setupA")
    su_semB = P.sem("setupB")
    su_semC = P.sem("setupC")
    su_semD = P.sem("setupD")
    with ExitStack() as es1:
        rb_aug = es1.enter_context(nc.sbuf_tensor("rb_aug", [64, 24], F32))
        onehot = es1.enter_context(nc.sbuf_tensor("s_onehot", [64, 7 * 512], F32))
        pv = es1.enter_context(nc.sbuf_tensor("pv", [32, 7, 512], BF16))
        mkall = es1.enter_context(nc.sbuf_tensor("mkall", [128, 8, MKW], BF16))
        wpool_f = es1.enter_context(nc.sbuf_tensor("wpool_f", [128, 4, 128], F32))
        bg_f = es1.enter_context(nc.sbuf_tensor("bg_f", [128, 16], F32))
        ps_f = es1.enter_context(nc.sbuf_tensor("ps_f", [128, 4], F32))
        sub = Buf()
        pvb = Buf()
        pdb = Buf()
        mkb = Buf()

        P.op(POOL, lambda: nc.gpsimd.memset(rb_aug[:, :], NEG), writes=[sub])
        P.op(POOL, lambda: nc.gpsimd.memset(stat[:, :], EPS), writes=[statb])
        P.op(POOL, lambda: nc.gpsimd.memset(negh[:, :], -0.5), writes=[constb])
        P.dma(SP, rb_aug[0:32, :], rb_d.ap(), su_sem, writes=[sub])
        P.dma(SP, onehot[0:33, :], onehot_d.ap(), su_sem, writes=[])
        P.dma(SP, ident[:, :], ident_d.ap(), su_sem, writes=[])
        P.dma(SP, band[:, :, :], band_d.ap().rearrange("p (a b) -> p a b", b=144), su_sem, writes=[])
        P.dma(SP, valid[:, :], valid_d.ap(), su_sem, writes=[])
        P.dma(SP, sel2[:, :], sel2_d.ap(), su_sem, writes=[])
        P.dma(SP, gain[:, :], ng_d.ap().rearrange("(dc p) -> p dc", p=128), su_sem, writes=[], allow_slow_non_contiguous=True)
        P.dma(SP, bg_f[:, :].rearrange("p (j oc) -> p j oc", j=2),
              bg_d.ap().rearrange("j (oc p) -> p j oc", p=128), su_sem, writes=[], allow_slow_non_contiguous=True)
        P.dma(SP, ps_f[:, :], psc_d.ap().rearrange("(g p) -> p g", p=128), su_sem, writes=[], allow_slow_non_contiguous=True)
        P.dma(SP, fgb[:, :], fg_d.ap().partition_broadcast(128), su_sem, writes=[])
        P.dma(SP, wpool_f[:, :, :], wpool_d.ap().rearrange("g c d -> c g d"), su_sem, writes=[])
        sub.w = (su_sem, su_sem.val)
        constb.w = (su_sem, su_sem.val)
        P.op(DVE, lambda: nc.vector.tensor_copy(out=wpool[:, :, :], in_=wpool_f[:, :, :]), reads=[constb], writes=[constb])
        P.op(DVE, lambda: nc.vector.tensor_scalar_mul(out=hbias[:, :], in0=bg_f[:, :], scalar1=0.5), reads=[constb], writes=[constb])
        P.op(DVE, lambda: nc.vector.tensor_scalar_mul(out=psh[:, :], in0=ps_f[:, :], scalar1=0.5), reads=[constb], writes=[constb])

        SETG = (0, 1, 2, 2, 2, 2, 2)
        for sidx in range(7):
            pbank, pbb = next_pj()
            P.op(PE, lambda sidx=sidx, pbank=pbank: nc.tensor.matmul(pbank[0:24, 0:512], lhsT=rb_aug[0:33, 0:24],
                                                                     rhs=onehot[0:33, sidx * 512:(sidx + 1) * 512],
                                                                     start=True, stop=True),
                 reads=[sub], writes=[pbb])
            P.op(ACT, lambda sidx=sidx, pbank=pbank: nc.scalar.activation(out=pv[0:24, sidx, :], in_=pbank[0:24, 0:512], func=AF.Exp),
                 reads=[pbb], writes=[pvb])
        for sidx in range(7):
            g = SETG[sidx]
            dst = pd_d.ap()[8 * sidx:8 * sidx + 8, :].rearrange("h (r i) -> h r i", i=512)
            src = pv[8 * g:8 * g + 8, sidx, :].unsqueeze(1).to_broadcast([8, 128, 512])
            P.dma(SP, dst, src, su_semB, reads=[pvb], writes=[])
        pdb.w = (su_semB, su_semB.val)

        def toeplitz_src(row, off, nrow, ncol):
            return bass.AP(pd_d, row * 128 * 512 + off, [[511, nrow], [1, ncol]])

        for h in range(8):
            P.dma(SP, mkall[:, h, 0:256], toeplitz_src(h, 64, 128, 256), su_semC, reads=[pdb], writes=[])
            P.dma(SP, mkall[:, h, 256:640], toeplitz_src(8 + h, 0, 128, 384), su_semC, reads=[pdb], writes=[])
            for di in range(5):
                P.dma(SP, mkall[:, h, 640 + 128 * di:640 + 128 * di + 128], toeplitz_src(8 * (2 + di) + h, 128, 128, 128),
                      su_semC, reads=[pdb], writes=[])
        mkb.w = (su_semC, su_semC.val)
        P.dma(SP, mk_d.ap().rearrange("h p c -> p h c"), mkall[:, :, :], su_semD, reads=[mkb])
        P.barrier()

    xb = [P.sb(f"xb{t}", [128, D], F32) for t in range(4)]
    xbb = [Buf() for _ in range(4)]
    xb_sem = [P.sem("xb") for _ in range(4)]
    xs = [P.sb(f"xs{t}", [128, D], BF16) for t in range(4)]
    xsb = [Buf() for _ in range(4)]
    hT = P.sb("hT", [128, 3, 8, T], BF16)
    hTb = [Buf() for _ in range(3)]
    kT = [P.sb("kT0", [128, 4, 3, T], BF16), P.sb("kT1", [128, 4, 3, T], BF16), P.sb("kT2", [128, 4, 5, T], BF16)]
    NSLOT = (3, 3, 5)
    kTb = [[Buf() for _ in range(NSLOT[g])] for g in range(3)]
    va = [P.sb(f"va{g}", [128, NSLOT[g], 4, 8, 65], BF16) for g in range(3)]
    vab = [[Buf() for _ in range(NSLOT[g])] for g in range(3)]
    qT = P.sb("qT", [128, 3, T], BF16)
    qTb = Buf()
    wbuf = [P.sb(f"wbuf{i}", [128, 8, 256], BF16) for i in range(4)]
    wbufb = [Buf() for _ in range(4)]
    wb_sem = [P.sem("wb") for _ in range(4)]
    mk = [P.sb(f"mk{i}", [128, MKW], BF16) for i in range(2)]
    mkbb = [Buf() for _ in range(2)]
    mk_sem = [P.sem("mk") for _ in range(2)]
    NU = 6
    uring = P.sb("uring", [128, NU, 512], BF16)
    uringb = [Buf() for _ in range(NU)]
    BT = P.sb("BT", [128, 4, T], BF16)
    BTb = [Buf() for _ in range(4)]
    AT = P.sb("AT", [128, 4, T], BF16)
    ATb = [Buf() for _ in range(4)]
    E0 = [P.sb(f"E0_{i}", [128, 512], BF16) for i in range(2)]
    E0hb = [[Buf(), Buf()] for _ in range(2)]
    Em = [P.sb(f"Em_{i}", [128, 512], BF16) for i in range(2)]
    Emhb = [[Buf(), Buf()] for _ in range(2)]
    Emb = [None, None]
    s12 = P.sb("s12", [128, T], F32)
    s12b = Buf()
    att = P.sb("att", [128, T], F32)
    attb = Buf()
    th = P.sb("th", [128, T], F32)
    thb = Buf()
    sgs = P.sb("sgs", [128, 4, T], BF16)
    sgsb = [Buf() for _ in range(4)]
    pooledT = Em[0]
    sgp = P.sb("sgp", [128, T], BF16)
    sgpb = Buf()
    den_sem = P.sem("den")
    merged = P.sb("merged", [128, 8, T], BF16)
    mergedb = [Buf() for _ in range(8)]
    gy, gyb = s12, s12b
    res = AT[:, :, :].rearrange("p a b -> p (a b)").bitcast(F32)
    xres0 = BT[:, :, :].rearrange("p a b -> p (a b)").bitcast(F32)
    xres1 = sgs[:, :, :].rearrange("p a b -> p (a b)").bitcast(F32)
    resb_l = ATb
    xres_bufs = [(xres0, BTb), (xres1, sgsb)]
    xres_sems = [P.sem("xres0"), P.sem("xres1")]

    def load_xres(c, ti):
        k = (ti + 1) % 2
        xr, xrb = xres_bufs[k]
        P.dma(POOL, xr[:, :], x_d.ap()[c * T + 128 * ti:c * T + 128 * ti + 128, :], xres_sems[k], writes=xrb)
    xres_sem = P.sem("xres")
    out_sems = [P.sem("out0"), P.sem("out1")]

    P.op(POOL, lambda: nc.gpsimd.memset(att[:, :], 1.0), writes=[attb])

    wstate = {"n": 0}

    def load_block(name):
        i = wstate["n"] % 4
        wstate["n"] += 1
        P.dma(SP, wbuf[i][:, :, :].rearrange("p a b -> p (a b)"), wb_d.ap()[BLK[name]], wb_sem[i], writes=[wbufb[i]])
        return wbuf[i], wbufb[i]

    def hslot(c):
        return c % 3

    def tok_ap(c, g, idx):
        s = hslot(c)
        if g == 0:
            return lambda dc: hT[:, s, dc, 128 * idx:128 * idx + 128]
        return lambda dc: hT[:, s, dc, :].rearrange("p (m r) -> p r m", r=4)[:, idx, :]

    def stageA1a_load(c):
        for t in range(4):
            P.dma(SP, xb[t][:, :], x_d.ap()[c * T + 128 * t:c * T + 128 * t + 128, :], xb_sem[t], writes=[xbb[t]])

    def stageA1a(c, load=True):
        if load:
            stageA1a_load(c)
        for t in range(4):
            P.op(ACT, lambda t=t: nc.scalar.activation(out=xs[t][:, :], in_=xb[t][:, :], func=AF.Square,
                                                       accum_out=stat[:, t:t + 1]),
                 reads=[xbb[t]], writes=[xsb[t], statb])
        P.op(DVE, lambda: nc.vector.tensor_scalar(out=stat[:, 4:8], in0=stat[:, 0:4], scalar1=1.0 / D, scalar2=EPS,
                                                  op0=ALU.mult, op1=ALU.add),
             reads=[statb], writes=[statb])
        P.op(POOL, lambda: nc.gpsimd.tensor_tensor(out=stat[:, 8:12], in0=stat[:, 4:8], in1=negh[:, 0:4], op=ALU.pow),
             reads=[statb, constb], writes=[statb])
        for t in range(4):
            P.op(POOL, lambda t=t: nc.gpsimd.tensor_tensor(out=xs[t][:, :], in0=xb[t][:, :],
                                                           in1=stat[:, 8 + t:9 + t].to_broadcast([128, D]), op=ALU.mult),
                 reads=[xbb[t], statb], writes=[xsb[t]])

    def evac_copy(i, out, in_, reads, writes):
        if i % 2 == 0:
            P.op(ACT, lambda: nc.scalar.copy(out=out, in_=in_), reads=reads, writes=writes)
        else:
            P.op(DVE, lambda: nc.vector.tensor_copy(out=out, in_=in_), reads=reads, writes=writes)

    def kv_stage(c, g):
        s = hslot(c)
        ks = c % NSLOT[g]
        for hb in range(2):
            wbk, wbb = load_block(f"K{g}{hb}")
            for seg in range(2):
                pair = 2 * hb + seg
                bank, bb = next_pj()
                for dc in range(8):
                    P.op(PE, lambda dc=dc, bank=bank, wbk=wbk, seg=seg: nc.tensor.matmul(
                        bank[:, 0:T], lhsT=wbk[:, dc, 128 * seg:128 * seg + 128], rhs=hT[:, s, dc, :],
                        start=(dc == 0), stop=(dc == 7)),
                         reads=[wbb, hTb[s]], writes=[bb], signal=(dc == 7))
                if g == 0:
                    src, dst = bank[:, 0:T], kT[0][:, pair, ks, :]
                else:
                    src = bank[:, 0:T].rearrange("p (m r) -> p r m", r=4)
                    dst = kT[g][:, pair, ks, :].rearrange("p (r m) -> p r m", r=4)
                evac_copy(pair, dst, src, [bb], [kTb[g][ks]])
        for hb in range(2):
            wbk, wbb = load_block(f"V{g}{hb}")
            for idx in range(4):
                bank, bb = next_pj()
                tf = tok_ap(c, g, idx)
                for dc in range(8):
                    P.op(PE, lambda dc=dc, bank=bank, wbk=wbk, tf=tf: nc.tensor.matmul(
                        bank[:, 0:256], lhsT=tf(dc), rhs=wbk[:, dc, :], start=(dc == 0), stop=(dc == 7)),
                         reads=[wbb, hTb[s]], writes=[bb], signal=(dc == 7))
                dst = va[g][:, ks, idx, 4 * hb:4 * hb + 4, 0:64]
                src = bank[:, 0:256].rearrange("p (h d) -> p h d", d=64)
                evac_copy(idx + 1, dst, src, [bb], [vab[g][ks]])
        P.op(POOL, lambda: nc.gpsimd.tensor_copy(out=va[g][:, ks, :, :, 64],
                                                 in_=valid[:, c:c + 1].unsqueeze(2).to_broadcast([128, 4, 8])),
             reads=[constb], writes=[vab[g][ks]])

    def stageA1b(c):
        s = hslot(c)
        for d2 in range(4):
            bank, bb = next_pj()
            bankh = bank[:, :].bitcast(BF16)
            for dd in range(2):
                dc = 2 * d2 + dd
                for t in range(4):
                    P.op(PE, lambda dc=dc, t=t, dd=dd, bankh=bankh: nc.tensor.transpose(
                        bankh[:, dd * 512 + 128 * t:dd * 512 + 128 * t + 128], xs[t][:, 128 * dc:128 * dc + 128], ident[:, :]),
                         reads=[xsb[t], constb], writes=[bb], signal=(dd == 1 and t == 3))
            for dd in range(2):
                dc = 2 * d2 + dd
                P.op(DVE, lambda dc=dc, dd=dd, bankh=bankh: nc.vector.tensor_scalar_mul(
                    out=hT[:, s, dc, :], in0=bankh[:, dd * 512:dd * 512 + 512], scalar1=gain[:, dc:dc + 1]),
                     reads=[bb, constb], writes=[hTb[s]])
        kv_stage(c, 2)

    def u_tile(tt):
        c, ti = tt // 4, tt % 4
        s = hslot(c)
        for hb in range(2):
            wbk, wbb = load_block(f"AI{hb}")
            bank, bb = next_pj()
            for dc in range(8):
                P.op(PE, lambda dc=dc, bank=bank, wbk=wbk: nc.tensor.matmul(
                    bank[:, 0:256], lhsT=hT[:, s, dc, 128 * ti:128 * ti + 128], rhs=wbk[:, dc, :],
                    start=(dc == 0), stop=(dc == 7)),
                     reads=[wbb, hTb[s]], writes=[bb], signal=(dc == 7))
            evac_copy(hb, uring[:, tt % NU, 256 * hb:256 * hb + 256], bank[:, 0:256], [bb], [uringb[tt % NU]])

    def u_tiles(tts):
        groups = {}
        for tt in tts:
            groups.setdefault(tt, None)
        for hb in range(2):
            wbk, wbb = load_block(f"AI{hb}")
            for tt in tts:
                c, ti = tt // 4, tt % 4
                s = hslot(c)
                bank, bb = next_pj()
                for dc in range(8):
                    P.op(PE, lambda dc=dc, bank=bank, wbk=wbk, s=s, ti=ti: nc.tensor.matmul(
                        bank[:, 0:256], lhsT=hT[:, s, dc, 128 * ti:128 * ti + 128], rhs=wbk[:, dc, :],
                        start=(dc == 0), stop=(dc == 7)),
                         reads=[wbb, hTb[s]], writes=[bb], signal=(dc == 7))
                evac_copy(tt + hb, uring[:, tt % NU, 256 * hb:256 * hb + 256], bank[:, 0:256], [bb], [uringb[tt % NU]])

    estate = {"n": 0}
    mstate = {"n": 0}

    def attention_pair(c, pr):
        hbanks = []
        for h in (2 * pr, 2 * pr + 1):
            banks = []
            pb = 64 * (h % 2)
            ob = h % 2
            o0, o0b, o12, o12b = o0s[ob], o0bs[ob], o12s[ob], o12bs[ob]
            mi = h % 2
            mkt, maskb = mk[mi], mkbb[mi]
            if mstate.get(mi) != h:
                P.dma(POOL, mkt[:, :], mk_d.ap()[h], mk_sem[mi], writes=[maskb])
                mstate[mi] = h
            tiles = [(c - 1, 3, 0, 64), (c, 0, 0, 192), (c, 1, 64, 320), (c, 2, 192, 448), (c, 3, 320, 512), (c + 1, 0, 448, 512)]
            first_o0 = True
            for half in range(2):
                rec = {"st": [], "mask": [], "pv": [], "maskb": maskb, "finish": None}
                col = 0
                for ti_, (cc, t_, q0, q1) in enumerate(tiles[3 * half:3 * half + 3]):
                    n = q1 - q0
                    ks = cc % 3
                    rec["st"].append((lambda stbank, col=col, n=n, ks=ks, t_=t_, q0=q0, q1=q1, pb=pb: nc.tensor.matmul(
                        stbank[:, col:col + n], lhsT=kT[0][pb:pb + 64, pr, ks, 128 * t_:128 * t_ + 128],
                        rhs=qT[pb:pb + 64, 0, q0:q1], start=True, stop=True), [kTb[0][ks], qTb]))
                    qq0 = (c * T + q0) - (cc * T + 128 * t_) + 128
                    rec["mask"].append((col, col + n, mkt[:, qq0 - 64:qq0 - 64 + n], None))
                    rec["pv"].append((lambda em, col=col, n=n, ks=ks, t_=t_, q0=q0, q1=q1, st_=first_o0, o0=o0, h=h: nc.tensor.matmul(
                        o0[0:65, q0:q1], lhsT=va[0][:, ks, t_, h, :], rhs=em[:, col:col + n],
                        start=st_, stop=False, skip_group_check=True), [vab[0][ks]], [o0b], col))
                    first_o0 = False
                    col += n
                rec["ncols"] = col
                banks.append(rec)
            first_o12 = True
            for g, dcs in ((1, (-1, 0, 1)), (2, (-2, -1, 0, 1, 2))):
                for di, dc_ in enumerate(dcs):
                    cc = c + dc_
                    ks = cc % NSLOT[g]
                    rec = {"st": [], "mask": [], "pv": [], "maskb": maskb, "finish": None, "ncols": 512}
                    for r4 in range(4):
                        rec["st"].append((lambda stbank, r4=r4, ks=ks, g=g, pb=pb: nc.tensor.matmul(
                            stbank[:, 128 * r4:128 * r4 + 128], lhsT=kT[g][pb:pb + 64, pr, ks, 128 * r4:128 * r4 + 128],
                            rhs=qT[pb:pb + 64, g, 128 * r4:128 * r4 + 128], start=True, stop=True), [kTb[g][ks], qTb]))
                        rec["pv"].append((lambda em, r4=r4, ks=ks, g=g, st_=first_o12, o12=o12, h=h: nc.tensor.matmul(
                            o12[0:65, 128 * r4:128 * r4 + 128], lhsT=va[g][:, ks, r4, h, :], rhs=em[:, 128 * r4:128 * r4 + 128],
                            start=st_, stop=False, skip_group_check=True), [vab[g][ks]], [o12b], 128 * r4))
                        first_o12 = False
                    moff = (256 + 128 - 128 * dc_) if g == 1 else (640 + 128 * di)
                    for hf_ in range(2):
                        rec["mask"].append((256 * hf_, 256 * hf_ + 256, mkt[:, moff:moff + 128].unsqueeze(1).to_broadcast([128, 2, 128]), 128))
                    banks.append(rec)

            def finish(h=h, pb=pb, o0=o0, o0b=o0b, o12=o12, o12b=o12b):
                P.op(ACT, lambda: nc.scalar.copy(out=s12[0:65, :].rearrange("p (m r) -> p r m", r=4),
                                                 in_=o12[0:65, :].rearrange("p (r m) -> p r m", r=4)),
                     reads=[o12b], writes=[s12b])
                if c == DBG_CHUNK:
                    dump(f"s12_h{h}", s12[0:65, :], [s12b])
                P.op(DVE, lambda: nc.vector.tensor_tensor(out=BT[pb:pb + 64, pr, :], in0=o0[0:64, :], in1=s12[0:64, :], op=ALU.add),
                     reads=[o0b, s12b], writes=[BTb[pr]])
                P.op(DVE, lambda: nc.vector.tensor_tensor(out=s12[64:65, :], in0=o0[64:65, :], in1=s12[64:65, :], op=ALU.add),
                     reads=[o0b, s12b], writes=[s12b])
                row = 32 * (h // 4) + (h % 4)
                P.dma(POOL, att[row:row + 1, :], s12[64:65, :], den_sem, reads=[s12b], writes=[attb])
                hn = (h + 2) % 8
                P.dma(POOL, mk[h % 2][:, :], mk_d.ap()[hn], mk_sem[h % 2], writes=[mkbb[h % 2]])
                mstate[h % 2] = hn
            banks[-1]["finish"] = finish
            hbanks.append(banks)

        def emit_st(stage):
            recs = [hbanks[0][stage], hbanks[1][stage]]
            for rec in recs:
                rec["stbank"], rec["stbb"] = next_st()
            nst = len(recs[0]["st"])
            for i in range(nst):
                for rec in recs:
                    fn, reads = rec["st"][i]
                    P.op(PE, lambda fn=fn, rec=rec: fn(rec["stbank"]), reads=reads, writes=[rec["stbb"]], signal=(i == nst - 1))

        def emit_rest(rec):
            k = estate["n"] % 2
            estate["n"] += 1
            stbank, stbb, ncols, maskb = rec["stbank"], rec["stbb"], rec["ncols"], rec["maskb"]
            for hf_ in range(2):
                lo, hi = 256 * hf_, min(256 * hf_ + 256, ncols)
                if hi <= lo:
                    continue
                P.op(ACT, lambda lo=lo, hi=hi: nc.scalar.activation(out=E0[k][:, lo:hi], in_=stbank[:, lo:hi], func=AF.Exp, scale=0.125),
                     reads=[stbb], writes=[E0hb[k][hf_]])
                for (c0, c1, m_ap, shape3) in rec["mask"]:
                    if not (lo <= c0 and c1 <= hi):
                        continue
                    if shape3 is None:
                        o_ap, i_ap = Em[k][:, c0:c1], E0[k][:, c0:c1]
                    else:
                        o_ap = Em[k][:, c0:c1].rearrange("p (a b) -> p a b", b=shape3)
                        i_ap = E0[k][:, c0:c1].rearrange("p (a b) -> p a b", b=shape3)
                    P.op(DVE, lambda o_ap=o_ap, i_ap=i_ap, m_ap=m_ap: nc.vector.tensor_tensor(out=o_ap, in0=i_ap, in1=m_ap, op=ALU.mult),
                         reads=[E0hb[k][hf_], maskb], writes=[Emhb[k][hf_]])
                pvs = [p for p in rec["pv"] if lo <= p[3] < hi]
                for i, (fn, reads, writes, _c0) in enumerate(pvs):
                    P.op(PE, lambda fn=fn: fn(Em[k]), reads=[Emhb[k][hf_]] + reads, writes=writes, signal=(i == len(pvs) - 1))
            if rec["finish"] is not None:
                rec["finish"]()

        nstage = len(hbanks[0])
        emit_st(0)
        for n in range(nstage):
            if n + 1 < nstage:
                emit_st(n + 1)
            emit_rest(hbanks[0][n])
            emit_rest(hbanks[1][n])

    def attention_recip(c):
        P.op(DVE, lambda: nc.vector.reciprocal(out=att[0:36, :], in_=att[0:36, :]), reads=[attb], writes=[attb])

    def attention_normalise(c):
        for pr in range(4):
            base = 32 * (pr // 2)
            a = pr % 2
            bcbank, bcb = next_pj()
            P.op(PE, lambda: nc.tensor.matmul(bcbank[:, 0:T], lhsT=sel2[base:base + 4, 128 * a:128 * a + 128],
                                              rhs=att[base:base + 4, :], start=True, stop=True),
                 reads=[attb, constb], writes=[bcb])
            P.op(DVE, lambda: nc.vector.scalar_tensor_tensor(out=th[:, :], in0=bcbank[:, 0:T], scalar=0.5,
                                                             in1=sgs[:, pr, :], op0=ALU.mult, op1=ALU.mult),
                 reads=[bcb, sgsb[pr]], writes=[thb])
            P.op(POOL, lambda: nc.gpsimd.tensor_tensor(out=BT[:, pr, :], in0=BT[:, pr, :], in1=th[:, :], op=ALU.mult),
                 reads=[thb, BTb[pr]], writes=[BTb[pr]])

    def silu_half(bank, bb, dst, dstb):
        P.op(ACT, lambda: nc.scalar.activation(out=th[:, :], in_=bank[:, 0:T], func=AF.Tanh, scale=0.5), reads=[bb], writes=[thb])
        P.op(DVE, lambda: nc.vector.scalar_tensor_tensor(out=dst, in0=th[:, :], scalar=1.0, in1=bank[:, 0:T],
                                                         op0=ALU.add, op1=ALU.mult),
             reads=[thb, bb], writes=[dstb])

    def proj_fm(wbk, wbb, seg, s):
        bank, bb = next_pj()
        for dc in range(8):
            P.op(PE, lambda dc=dc: nc.tensor.matmul(bank[:, 0:T], lhsT=wbk[:, dc, 128 * seg:128 * seg + 128],
                                                    rhs=hT[:, s, dc, :], start=(dc == 0), stop=(dc == 7)),
                 reads=[wbb, hTb[s]], writes=[bb], signal=(dc == 7))
        return bank, bb

    def stageB(c):
        s = hslot(c)
        rot["wide"] = False
        for pr in range(4):
            wa, wab = load_block(f"QA{pr}")
            wq, wqb = load_block(f"QB{pr}")
            for g in range(3):
                wbk, wbb, seg = (wa, wab, g) if g < 2 else (wq, wqb, 0)
                bank, bb = proj_fm(wbk, wbb, seg, s)
                if g == 0:
                    src, dst = bank[:, 0:T], qT[:, 0, :]
                else:
                    src = bank[:, 0:T].rearrange("p (m r) -> p r m", r=4)
                    dst = qT[:, g, :].rearrange("p (r m) -> p r m", r=4)
                evac_copy(1 if g < 2 else 0, dst, src, [bb], [qTb])
            bank, bb = proj_fm(wq, wqb, 1, s)
            silu_half(bank, bb, sgs[:, pr, :], sgsb[pr])
            if c == DBG_CHUNK:
                dump(f"qT_p{pr}", qT[:, :, :], [qTb])
                dump(f"sg_p{pr}", sgs[:, pr, :], [sgsb[pr]])
                if pr == 0:
                    dump("hT", hT[:, s, :, :], [hTb[s]])
                    for g in range(3):
                        dump(f"kT{g}", kT[g][:, :, :, :], kTb[g])
                        dump(f"va{g}", va[g][:, :, :, :, :].rearrange("p a b c d -> p (a b c d)"), vab[g])
            attention_pair(c, pr)
        rot["wide"] = True
        attention_recip(c)
        pbufs = [(Em[0], Emhb[0]), (Em[1], Emhb[1]), (E0[0], E0hb[0]), (E0[1], E0hb[1])]
        bands = []
        for g in range(4):
            bank, bb = next_pj()
            first = True
            for tp in range(-1, 5):
                tt = 4 * c + tp
                q0, q1 = max(0, 128 * tp - 8), min(T, 128 * tp + 136)
                qq0 = q0 - (128 * tp - 8)
                kind = 1
                if c == MAIN0 and tp == 0:
                    kind = 0
                if c == MAIN1 - 1 and tp == 3:
                    kind = 2
                P.op(PE, lambda tt=tt, q0=q0, q1=q1, qq0=qq0, kind=kind, g=g, first=first, bank=bank: nc.tensor.matmul(
                    bank[:, q0:q1], lhsT=uring[:, tt % NU, 128 * g:128 * g + 128],
                    rhs=band[:, 4 * kind + g, qq0:qq0 + (q1 - q0)], start=first, stop=False, skip_group_check=True),
                     reads=[uringb[tt % NU], constb], writes=[bb], signal=(tp == 4))
                first = False
            bands.append((bank, bb))
        for g in range(4):
            bank, bb = bands[g]
            evac_copy(g, pbufs[g][0][:, :], bank[:, 0:T], [bb], pbufs[g][1])
            if c == DBG_CHUNK:
                dump(f"pooledT_g{g}", pbufs[g][0][:, :], pbufs[g][1])
        mg = []
        for g in range(4):
            mbank, mbb = next_pj()
            P.op(PE, lambda g=g, mbank=mbank: nc.tensor.matmul(mbank[:, 0:T], lhsT=wpool[:, g, :], rhs=pbufs[g][0][:, :], start=True, stop=True),
                 reads=pbufs[g][1] + [constb], writes=[mbb])
            if g % 2 == 0:
                wg, wgb = load_block(f"AG{g // 2}")
            gbank, gbb = proj_fm(wg, wgb, g % 2, s)
            mg.append((mbank, mbb, gbank, gbb))
        for g in range(4):
            mbank, mbb, gbank, gbb = mg[g]
            silu_half(gbank, gbb, sgp[:, :], sgpb)
            P.op(DVE, lambda g=g, mbank=mbank: nc.vector.scalar_tensor_tensor(out=AT[:, g, :], in0=mbank[:, 0:T], scalar=psh[:, g:g + 1],
                                                                             in1=sgp[:, :], op0=ALU.mult, op1=ALU.mult),
                 reads=[mbb, sgpb, constb], writes=[ATb[g]])
        attention_normalise(c)
        if c == DBG_CHUNK:
            dump("BT", BT[:, :, :], BTb)
            dump("uring", uring[:, :, :], uringb)
        if c == DBG_CHUNK:
            dump("AT", AT[:, :, :], ATb)
        load_xres(c, 0)
        for q in range(4):
            wp, wpb_ = load_block(f"PAB{q}")
            wga, wgab = load_block(f"GA{q}")
            wgb_, wgbb = load_block(f"GB{q}")
            for seg in range(2):
                oc = 2 * q + seg
                for br in range(2):
                    ybank, ybb = next_pj()
                    src_t, src_b = (AT, ATb) if br == 0 else (BT, BTb)
                    for kc in range(4):
                        P.op(PE, lambda kc=kc, br=br, src_t=src_t: nc.tensor.matmul(
                            ybank[:, 0:T], lhsT=wp[:, 4 * br + kc, 128 * seg:128 * seg + 128], rhs=src_t[:, kc, :],
                            start=(kc == 0), stop=(kc == 3)),
                             reads=[wpb_, src_b[kc]], writes=[ybb], signal=(kc == 3))
                    gw, gwb = (wga, wgab) if br == 0 else (wgb_, wgbb)
                    gbank, gbb = proj_fm(gw, gwb, seg, s)
                    P.op(ACT, lambda br=br, oc=oc, gbank=gbank: nc.scalar.activation(
                        out=th[:, :], in_=gbank[:, 0:T], func=AF.Tanh, scale=0.5, bias=hbias[:, 8 * br + oc:8 * br + oc + 1]),
                         reads=[gbb, constb], writes=[thb])
                    if br == 0:
                        P.op(DVE, lambda ybank=ybank: nc.vector.scalar_tensor_tensor(
                            out=gy[:, :], in0=th[:, :], scalar=1.0, in1=ybank[:, 0:T], op0=ALU.add, op1=ALU.mult),
                             reads=[thb, ybb], writes=[gyb])
                    else:
                        P.op(DVE, lambda ybank=ybank: nc.vector.scalar_tensor_tensor(
                            out=att[:, :], in0=th[:, :], scalar=1.0, in1=ybank[:, 0:T], op0=ALU.add, op1=ALU.mult),
                             reads=[thb, ybb], writes=[attb])
                        P.op(POOL, lambda oc=oc: nc.gpsimd.tensor_tensor(out=merged[:, oc, :], in0=gy[:, :], in1=att[:, :], op=ALU.add),
                             reads=[gyb, attb], writes=[mergedb[oc]])
        if c == DBG_CHUNK:
            dump("merged", merged[:, :, :], mergedb)
        wo = [load_block(f"WO{q}") for q in range(4)]
        load_xres(c, 1)
        for ti in range(4):
            row0 = (c - MAIN0) * T + 128 * ti
            xr, xrb = xres_bufs[(ti + 1) % 2]
            for half in range(2):
                obank, obb = next_pj()
                first = True
                for qq in range(2):
                    wbk, wbb = wo[2 * half + qq]
                    for oc in range(8):
                        last = (qq == 1 and oc == 7)
                        P.op(PE, lambda oc=oc, qq=qq, wbk=wbk, first=first, obank=obank: nc.tensor.matmul(
                            obank[:, 256 * qq:256 * qq + 256], lhsT=merged[:, oc, 128 * ti:128 * ti + 128], rhs=wbk[:, oc, :],
                            start=first, stop=False, skip_group_check=True),
                             reads=[wbb, mergedb[oc]], writes=[obb], signal=last)
                        first = False
                P.op(DVE, lambda half=half, obank=obank, xr=xr: nc.vector.scalar_tensor_tensor(
                    out=res[:, 512 * half:512 * half + 512], in0=obank[:, 0:512], scalar=0.5,
                    in1=xr[:, 512 * half:512 * half + 512], op0=ALU.mult, op1=ALU.add),
                     reads=[obb] + xrb, writes=resb_l)
            P.op(ACT, lambda: nc.scalar.activation(out=th[:, :].bitcast(BF16), in_=res[:, :], func=AF.Square, accum_out=fstat[:, 0:1]),
                 reads=resb_l, writes=[thb, fstatb])
            P.op(DVE, lambda: nc.vector.tensor_scalar(out=fstat[:, 1:2], in0=fstat[:, 0:1], scalar1=1.0 / D, scalar2=EPS,
                                                      op0=ALU.mult, op1=ALU.add),
                 reads=[fstatb], writes=[fstatb])
            P.op(POOL, lambda: nc.gpsimd.tensor_tensor(out=fstat[:, 2:3], in0=fstat[:, 1:2], in1=negh[:, 0:1], op=ALU.pow),
                 reads=[fstatb, constb], writes=[fstatb])
            P.op(DVE, lambda xr=xr: nc.vector.scalar_tensor_tensor(out=xr[:, :], in0=res[:, :], scalar=fstat[:, 2:3], in1=fgb[:, :],
                                                                   op0=ALU.mult, op1=ALU.mult),
                 reads=resb_l + [fstatb, constb], writes=xrb)
            P.dma(POOL, out_d.ap()[row0:row0 + 128, :], xr[:, :], out_sems[(ti + 1) % 2], reads=xrb)
            if ti + 2 < 4:
                load_xres(c, ti + 2)

    P.op(POOL, lambda: nc.gpsimd.memset(kT[2][:, :, 0:2, :], 0.0), writes=[kTb[2][0], kTb[2][1]])
    P.op(POOL, lambda: nc.gpsimd.memset(va[2][:, 0:2, :, :, :], 0.0), writes=[vab[2][0], vab[2][1]])
    for g in (0, 1):
        P.op(POOL, lambda g=g: nc.gpsimd.memset(kT[g][:, :, 1, :], 0.0), writes=[kTb[g][1]])
        P.op(POOL, lambda g=g: nc.gpsimd.memset(va[g][:, 1, :, :, :], 0.0), writes=[vab[g][1]])
    P.op(POOL, lambda: nc.gpsimd.memset(uring[:, 7 % NU, :], 0.0), writes=[uringb[7 % NU]])
    stageA1a(2); stageA1b(2)
    stageA1a(3)
    for g in (0, 1):
        kv_stage(2, g)
    u_tiles([8])
    stageA1b(3)
    for i in range(MAIN0, MAIN1):
        if i + 2 < NCH:
            stageA1a_load(i + 2)
        kv_stage(i + 1, 0)
        if i + 2 < NCH:
            stageA1a(i + 2, load=False)
        kv_stage(i + 1, 1)
        u_tiles([4 * i + 1, 4 * i + 2, 4 * i + 3, 4 * i + 4])
        if i + 2 < NCH:
            stageA1b(i + 2)
        stageB(i)
    for osem in out_sems:
        POOL.h.wait_ge(osem.h, osem.val)
        SP.h.wait_ge(osem.h, osem.val)
    if debug and dbg_sem.val:
        SP.h.wait_ge(dbg_sem.h, dbg_sem.val)
    es.close()
    return nc


def _t5_bucket_np(rel):
    nb = 16
    ret = (rel > 0).astype(np.int32) * nb
    n = np.abs(rel)
    max_exact = nb // 2
    nf = np.maximum(n, 1).astype(np.float32)
    large = max_exact + (np.log(nf / np.float32(max_exact)) / np.float32(math.log(1024 / max_exact))
                         * np.float32(nb - max_exact)).astype(np.int32)
    large = np.minimum(large, nb - 1)
    return ret + np.where(n < max_exact, n, large)


def _host_consts(hf):
    sign = 1 if hf == 0 else -1
    ident = np.eye(128, dtype=np.float32).astype(ml_dtypes.bfloat16)
    onehot = np.zeros((33, 7, 512), np.float32)
    idx = np.arange(512)
    for g in range(2):
        j = 128 - idx
        inwin = (idx >= 64) & (idx <= 192)
        b = _t5_bucket_np((sign * j * DIL[g]).astype(np.int32))
        b = np.where(inwin, b, 32)
        onehot[b, g, idx] = 1.0
    for di, dc_ in enumerate((-2, -1, 0, 1, 2)):
        delta = 128 - idx
        j = 32 * dc_ + delta // 4
        ok = (idx >= 1) & (idx <= 255) & (delta % 4 == 0) & (np.abs(j) <= 64)
        b = _t5_bucket_np((sign * j * DIL[2]).astype(np.int32))
        b = np.where(ok, b, 32)
        onehot[b, 2 + di, idx] = 1.0
    band = np.zeros((128, 3, 4, 144), np.float32)
    k = np.arange(128)[:, None]
    qp = (np.arange(144) - 8)[None, :]
    for g, w in enumerate((2, 4, 8, 16)):
        hw = w // 2
        if hf == 0:
            inw = ((k >= qp - hw) & (k <= qp + hw - 1)).astype(np.float32)
            cnt_first = (np.minimum(np.maximum(qp, 0), hw) + hw).astype(np.float32)
        else:
            inw = ((k >= qp - hw + 1) & (k <= qp + hw)).astype(np.float32)
            cnt_first = (hw + np.minimum(hw, np.maximum(qp, 0) + 1)).astype(np.float32)
        cnt_first = np.minimum(cnt_first, w)
        eye = (k == qp).astype(np.float32)
        cnt_mid = np.full((1, 144), float(w), np.float32)
        cnts = [cnt_first, cnt_mid, cnt_mid]
        for kind in range(3):
            band[:, kind, g, :] = inw / cnts[kind] - eye
    band = band.reshape(128, 12 * 144).astype(ml_dtypes.bfloat16)
    valid = np.ones((128, NCH), np.float32)
    valid[:, 0:2] = 0.0
    sel2 = np.zeros((128, 256), np.float32)
    for base in (0, 32):
        for r in range(4):
            for a in range(2):
                for m in range(128):
                    if r == 2 * a + m // 64:
                        sel2[base + r, 128 * a + m] = 1.0
    return ident, onehot.reshape(33, 7 * 512), band, valid, sel2


_CACHE = {}


def kernel(x, norm_gain, w_in, b_gate, rel_bias, w_pool, pool_scale, w_proj_a, w_proj_b, w_out, final_gain):
    x = np.asarray(x, np.float32)
    if "nc" not in _CACHE:
        _CACHE["nc"] = build_program()
    nc = _CACHE["nc"]
    common = {
        "w_in": np.ascontiguousarray(np.asarray(w_in, np.float32)[0]),
        "w_proj_a": np.ascontiguousarray(np.asarray(w_proj_a, np.float32)[0]),
        "w_proj_b": np.ascontiguousarray(np.asarray(w_proj_b, np.float32)[0]),
        "w_out": np.ascontiguousarray(np.asarray(w_out, np.float32)[0]),
        "w_pool": np.ascontiguousarray(np.asarray(w_pool, np.float32)[0]),
        "norm_gain": np.ascontiguousarray(np.asarray(norm_gain, np.float32)[0]),
        "final_gain": np.ascontiguousarray(np.asarray(final_gain, np.float32)),
        "b_gate": np.ascontiguousarray(np.asarray(b_gate, np.float32)[0]),
        "pool_scale": np.ascontiguousarray(np.asarray(pool_scale, np.float32)[0]),
        "rel_bias": np.ascontiguousarray(np.asarray(rel_bias, np.float32)),
    }
    in_maps = []
    for core in range(NCORES):
        b, hf = core // 2, core % 2
        xe = np.zeros((EXT, D), np.float32)
        if hf == 0:
            xe[1024:EXT] = x[b, 0:EXT - 1024]
        else:
            xe[1024:EXT] = x[b, SEQ - (EXT - 1024):SEQ][::-1]
        ident, onehot, band, valid, sel2 = _host_consts(hf)
        m = dict(common)
        m.update({"x": xe, "ident": ident, "onehot": onehot, "band": band, "valid": valid, "sel2": sel2})
        in_maps.append(m)
    res = run_bass_kernel_spmd(nc, in_maps, core_ids=list(range(NCORES)))
    out = np.empty((4, SEQ, D), np.float32)
    for core in range(NCORES):
        b, hf = core // 2, core % 2
        o = np.asarray(res.results[core]["out"], np.float32)
        out[b, hf * 4096:(hf + 1) * 4096] = o if hf == 0 else o[::-1]
    return out
```

```python
import math
from contextlib import ExitStack

import numpy as np
import ml_dtypes

import concourse.bass as bass
import concourse.mybir as mybir
from concourse.bass_utils import run_bass_kernel_spmd

F32 = mybir.dt.float32
BF16 = mybir.dt.bfloat16
ALU = mybir.AluOpType
AF = mybir.ActivationFunctionType

D = 1024
SEQ = 8192
NCORES = 8
T = 512
NCH = 12
EXT = NCH * T
MAIN0, MAIN1 = 2, 10
COL_AIN, COL_AGATE, COL_Q, COL_K, COL_V, COL_BGP, COL_G = 0, 512, 1024, 2560, 4096, 5632, 6144
DIL = (1, 4, 16)
NEG = -30000.0
EPS = 1e-6

BLK = {}
_blocks = []


def _add(name, segs):
    BLK[name] = len(_blocks)
    _blocks.append(segs)


for g in range(3):
    for hb in range(2):
        _add(f"K{g}{hb}", [("win", COL_K + 512 * g + 256 * hb), ("win", COL_K + 512 * g + 256 * hb + 128)])
        _add(f"V{g}{hb}", [("win", COL_V + 512 * g + 256 * hb), ("win", COL_V + 512 * g + 256 * hb + 128)])
for pr in range(4):
    _add(f"QA{pr}", [("win", COL_Q + 128 * pr), ("win", COL_Q + 512 + 128 * pr)])
    _add(f"QB{pr}", [("win", COL_Q + 1024 + 128 * pr), ("win", COL_BGP + 128 * pr)])
for hb in range(2):
    _add(f"AI{hb}", [("win", COL_AIN + 256 * hb), ("win", COL_AIN + 256 * hb + 128)])
    _add(f"AG{hb}", [("win", COL_AGATE + 256 * hb), ("win", COL_AGATE + 256 * hb + 128)])
for q in range(4):
    _add(f"GA{q}", [("win", COL_G + 256 * q), ("win", COL_G + 256 * q + 128)])
    _add(f"GB{q}", [("win", COL_G + 1024 + 256 * q), ("win", COL_G + 1024 + 256 * q + 128)])
for q in range(4):
    _add(f"PAB{q}", [("pab", 256 * q)])
    _add(f"WO{q}", [("wout", 256 * q)])
NBLK = len(_blocks)


class Sem:
    def __init__(self, h):
        self.h = h
        self.val = 0


class Eng:
    def __init__(self, name, h, sem, inorder=False):
        self.name, self.h, self.sem, self.inorder = name, h, sem, inorder
        self.waited = {}


class Buf:
    __slots__ = ("w", "r", "name")

    def __init__(self, name=""):
        self.w = None
        self.r = {}
        self.name = name


class Prog:
    def __init__(self, nc, es):
        self.nc, self.es = nc, es
        self.nsem = 0
        self.PE = Eng("pe", nc.tensor, self.sem("pe"), inorder=True)
        self.ACT = Eng("act", nc.scalar, self.sem("act"))
        self.DVE = Eng("dve", nc.vector, self.sem("dve"))
        self.POOL = Eng("pool", nc.gpsimd, self.sem("pool"))
        self.SP = Eng("sp", nc.sync, self.sem("sp"))
        self.engs = [self.PE, self.ACT, self.DVE, self.POOL, self.SP]
        self.all_sems = []

    def sem(self, name):
        self.nsem += 1
        s = Sem(self.es.enter_context(self.nc.semaphore(f"{name}_{self.nsem}")))
        if not hasattr(self, "_sems"):
            self._sems = []
        self._sems.append(s)
        return s

    def sb(self, name, shape, dt):
        return self.es.enter_context(self.nc.sbuf_tensor("s_" + name, list(shape), dt))

    def ps(self, name, shape, dt):
        return self.es.enter_context(self.nc.psum_tensor("p_" + name, list(shape), dt))

    def _waits(self, E, reads, writes):
        deps = {}

        def need(ev):
            if ev is None:
                return
            s, v = ev
            if deps.get(s, 0) < v:
                deps[s] = v

        for b in reads:
            need(b.w)
        for b in writes:
            need(b.w)
            for s, v in b.r.items():
                need((s, v))
        for s, v in deps.items():
            if s is E.sem and E.inorder:
                continue
            if E.waited.get(s, 0) < v:
                E.h.wait_ge(s.h, v)
                E.waited[s] = v

    def _record(self, ev, reads, writes):
        s, v = ev
        for b in reads:
            if b.r.get(s, 0) < v:
                b.r[s] = v
        for b in writes:
            b.w = ev
            b.r = {}

    def op(self, E, fn, reads=(), writes=(), signal=True):
        self._waits(E, reads, writes)
        ins = fn()
        if signal:
            E.sem.val += 1
            ins.then_inc(E.sem.h, 1)
            ev = (E.sem, E.sem.val)
        else:
            ev = (E.sem, E.sem.val + 1)
        self._record(ev, reads, writes)
        return ins

    def dma(self, E, out, in_, sem, reads=(), writes=(), **kw):
        self._waits(E, reads, writes)
        ins = E.h.dma_start(out=out, in_=in_, **kw)
        sem.val += 16
        ins.then_inc(sem.h, 16)
        self._record((sem, sem.val), reads, writes)
        return ins

    def barrier(self):
        for E in self.engs:
            for s in self._sems:
                if s.val > 0 and E.waited.get(s, 0) < s.val:
                    E.h.wait_ge(s.h, s.val)
                    E.waited[s] = s.val


def build_program(debug=False):
    nc = bass.Bass("TRN2", target_bir_lowering=False)
    dram = lambda n, s, dt, k="ExternalInput": nc.dram_tensor(n, list(s), dt, kind=k)
    x_d = dram("x", [EXT, D], F32)
    win_d = dram("w_in", [D, 8192], F32)
    wpa_d = dram("w_proj_a", [512, D], F32)
    wpb_d = dram("w_proj_b", [512, D], F32)
    wout_d = dram("w_out", [D, D], F32)
    wpool_d = dram("w_pool", [4, 128, 128], F32)
    ng_d = dram("norm_gain", [D], F32)
    fg_d = dram("final_gain", [D], F32)
    bg_d = dram("b_gate", [2, D], F32)
    psc_d = dram("pool_scale", [512], F32)
    rb_d = dram("rel_bias", [32, 24], F32)
    ident_d = dram("ident", [128, 128], BF16)
    onehot_d = dram("onehot", [33, 7 * 512], F32)
    band_d = dram("band", [128, 12 * 144], BF16)
    valid_d = dram("valid", [128, NCH], F32)
    sel2_d = dram("sel2", [128, 256], F32)
    out_d = dram("out", [8 * T, D], F32, "ExternalOutput")
    wb_d = dram("wb_scr", [NBLK, 128, 8 * 256], BF16, "Internal")
    pd_d = dram("pd_scr", [56, 128 * 512], BF16, "Internal")
    mk_d = dram("mk_scr", [8, 128, 256 + 384 + 640], BF16, "Internal")

    es = ExitStack()
    P = Prog(nc, es)
    PE, ACT, DVE, POOL, SP = P.PE, P.ACT, P.DVE, P.POOL, P.SP
    dbg_sem = P.sem("dbg")
    DBG_CHUNK = 2

    def dump(name, ap, reads, dt=None):
        if not debug:
            return
        shape = list(ap.shape)
        t = nc.dram_tensor("dbg_" + name, shape, dt or ap.dtype, kind="ExternalOutput")
        P.dma(SP, t.ap(), ap, dbg_sem, reads=reads)

    pj = [P.ps(f"pj{i}", [128, 512], F32) for i in range(2)]
    pjb = [Buf(f"pj{i}") for i in range(2)]
    st = [P.ps(f"st{i}", [128, 512], F32) for i in range(2)]
    stb = [Buf(f"st{i}") for i in range(2)]
    o0s = [P.ps(f"o0_{i}", [128, 512], F32) for i in range(2)]
    o0bs = [Buf() for _ in range(2)]
    o12s = [P.ps(f"o12_{i}", [128, 512], F32) for i in range(2)]
    o12bs = [Buf() for _ in range(2)]
    rot = {"pj": 0, "st": 0, "wide": True}
    wide_banks = [(pj[0], pjb[0]), (pj[1], pjb[1]), (st[0], stb[0]), (st[1], stb[1]),
                  (o0s[0], o0bs[0]), (o0s[1], o0bs[1]), (o12s[0], o12bs[0]), (o12s[1], o12bs[1])]

    def next_pj():
        lst = wide_banks if rot["wide"] else wide_banks[0:2]
        i = rot["pj"] % len(lst)
        rot["pj"] = i + 1
        return lst[i]

    st4 = [(st[0], stb[0]), (st[1], stb[1]), (pj[0], pjb[0]), (pj[1], pjb[1])]

    def next_st():
        i = rot["st"]
        rot["st"] = (i + 1) % 4
        return st4[i]

    stat = P.sb("stat", [128, 16], F32)
    statb = Buf()
    ident = P.sb("ident", [128, 128], BF16)
    band = P.sb("band", [128, 12, 144], BF16)
    valid = P.sb("valid", [128, NCH], F32)
    gain = P.sb("gain", [128, 8], F32)
    hbias = P.sb("hbias", [128, 16], F32)
    psh = P.sb("psh", [128, 4], F32)
    fgb = P.sb("fgb", [128, D], F32)
    wpool = P.sb("wpool", [128, 4, 128], BF16)
    fstat = P.sb("fstat", [128, 8], F32)
    fstatb = Buf()
    sel2 = P.sb("sel2", [128, 256], F32)
    negh = P.sb("negh", [128, 4], F32)
    constb = Buf()
    MKW = 256 + 384 + 640

    su_sem = P.sem("setupA")
    su_semB = P.sem("setupB")
    su_semC = P.sem("setupC")
    su_semD = P.sem("setupD")
    with ExitStack() as es1:
        rb_aug = es1.enter_context(nc.sbuf_tensor("rb_aug", [64, 24], F32))
        onehot = es1.enter_context(nc.sbuf_tensor("s_onehot", [64, 7 * 512], F32))
        pv = es1.enter_context(nc.sbuf_tensor("pv", [32, 7, 512], BF16))
        mkall = es1.enter_context(nc.sbuf_tensor("mkall", [128, 8, MKW], BF16))
        wpool_f = es1.enter_context(nc.sbuf_tensor("wpool_f", [128, 4, 128], F32))
        bg_f = es1.enter_context(nc.sbuf_tensor("bg_f", [128, 16], F32))
        ps_f = es1.enter_context(nc.sbuf_tensor("ps_f", [128, 4], F32))
        sub = Buf()
        pvb = Buf()
        pdb = Buf()
        mkb = Buf()

        P.op(POOL, lambda: nc.gpsimd.memset(rb_aug[:, :], NEG), writes=[sub])
        P.op(POOL, lambda: nc.gpsimd.memset(stat[:, :], EPS), writes=[statb])
        P.op(POOL, lambda: nc.gpsimd.memset(negh[:, :], -0.5), writes=[constb])
        P.dma(ACT, rb_aug[0:32, :], rb_d.ap(), su_sem, writes=[sub])
        P.dma(ACT, onehot[0:33, :], onehot_d.ap(), su_sem, writes=[])
        P.dma(ACT, ident[:, :], ident_d.ap(), su_sem, writes=[])
        P.dma(ACT, band[:, :, :], band_d.ap().rearrange("p (a b) -> p a b", b=144), su_sem, writes=[])
        P.dma(ACT, valid[:, :], valid_d.ap(), su_sem, writes=[])
        P.dma(ACT, sel2[:, :], sel2_d.ap(), su_sem, writes=[])
        P.dma(ACT, gain[:, :], ng_d.ap().rearrange("(dc p) -> p dc", p=128), su_sem, writes=[], allow_slow_non_contiguous=True)
        P.dma(ACT, bg_f[:, :].rearrange("p (j oc) -> p j oc", j=2),
              bg_d.ap().rearrange("j (oc p) -> p j oc", p=128), su_sem, writes=[], allow_slow_non_contiguous=True)
        P.dma(ACT, ps_f[:, :], psc_d.ap().rearrange("(g p) -> p g", p=128), su_sem, writes=[], allow_slow_non_contiguous=True)
        P.dma(ACT, fgb[:, :], fg_d.ap().partition_broadcast(128), su_sem, writes=[])
        P.dma(ACT, wpool_f[:, :, :], wpool_d.ap().rearrange("g c d -> c g d"), su_sem, writes=[])
        sub.w = (su_sem, su_sem.val)
        constb.w = (su_sem, su_sem.val)
        P.op(DVE, lambda: nc.vector.tensor_copy(out=wpool[:, :, :], in_=wpool_f[:, :, :]), reads=[constb], writes=[constb])
        P.op(DVE, lambda: nc.vector.tensor_scalar_mul(out=hbias[:, :], in0=bg_f[:, :], scalar1=0.5), reads=[constb], writes=[constb])
        P.op(DVE, lambda: nc.vector.tensor_scalar_mul(out=psh[:, :], in0=ps_f[:, :], scalar1=0.5), reads=[constb], writes=[constb])

        SETG = (0, 1, 2, 2, 2, 2, 2)
        for sidx in range(7):
            pbank, pbb = next_pj()
            P.op(PE, lambda sidx=sidx, pbank=pbank: nc.tensor.matmul(pbank[0:24, 0:512], lhsT=rb_aug[0:33, 0:24],
                                                                     rhs=onehot[0:33, sidx * 512:(sidx + 1) * 512],
                                                                     start=True, stop=True),
                 reads=[sub], writes=[pbb])
            P.op(ACT, lambda sidx=sidx, pbank=pbank: nc.scalar.activation(out=pv[0:24, sidx, :], in_=pbank[0:24, 0:512], func=AF.Exp),
                 reads=[pbb], writes=[pvb])
        for sidx in range(7):
            g = SETG[sidx]
            dst = pd_d.ap()[8 * sidx:8 * sidx + 8, :].rearrange("h (r i) -> h r i", i=512)
            src = pv[8 * g:8 * g + 8, sidx, :].unsqueeze(1).to_broadcast([8, 128, 512])
            P.dma(ACT, dst, src, su_semB, reads=[pvb], writes=[])
        pdb.w = (su_semB, su_semB.val)

        def toeplitz_src(row, off, nrow, ncol):
            return bass.AP(pd_d, row * 128 * 512 + off, [[511, nrow], [1, ncol]])

        for h in range(8):
            P.dma(ACT, mkall[:, h, 0:256], toeplitz_src(h, 64, 128, 256), su_semC, reads=[pdb], writes=[])
            P.dma(ACT, mkall[:, h, 256:640], toeplitz_src(8 + h, 0, 128, 384), su_semC, reads=[pdb], writes=[])
            for di in range(5):
                P.dma(ACT, mkall[:, h, 640 + 128 * di:640 + 128 * di + 128], toeplitz_src(8 * (2 + di) + h, 128, 128, 128),
                      su_semC, reads=[pdb], writes=[])
        mkb.w = (su_semC, su_semC.val)
        P.dma(ACT, mk_d.ap().rearrange("h p c -> p h c"), mkall[:, :, :], su_semD, reads=[mkb])

        with ExitStack() as es0:
            NST = 4
            stg = [es0.enter_context(nc.sbuf_tensor(f"stg{i}", [128, 8, 256], F32)) for i in range(NST)]
            stgb = [Buf() for _ in range(NST)]
            cvt = [es0.enter_context(nc.sbuf_tensor(f"cvt{i}", [128, 8 * 256], BF16)) for i in range(NST)]
            cvtb = [Buf() for _ in range(NST)]
            ld_sem = [P.sem("p0ld") for _ in range(NST)]
            st_sem = [P.sem("p0st") for _ in range(NST)]
            cv_engs = [DVE, POOL]
            for bi, segs in enumerate(_blocks):
                k = bi % NST
                if segs[0][0] == "win":
                    for si, (_, c0) in enumerate(segs):
                        src = win_d.ap()[:, c0:c0 + 128].rearrange("(kc p) c -> p kc c", p=128)
                        P.dma(SP, stg[k][:, :, si * 128:(si + 1) * 128], src, ld_sem[k], writes=[stgb[k]] if si == 0 else [])
                    stgb[k].w = (ld_sem[k], ld_sem[k].val)
                elif segs[0][0] == "pab":
                    c0 = segs[0][1]
                    P.dma(SP, stg[k][:, 0:4, :], wpa_d.ap()[:, c0:c0 + 256].rearrange("(kc p) c -> p kc c", p=128),
                          ld_sem[k], writes=[stgb[k]])
                    P.dma(SP, stg[k][:, 4:8, :], wpb_d.ap()[:, c0:c0 + 256].rearrange("(kc p) c -> p kc c", p=128),
                          ld_sem[k], writes=[])
                    stgb[k].w = (ld_sem[k], ld_sem[k].val)
                else:
                    c0 = segs[0][1]
                    P.dma(SP, stg[k][:, :, :], wout_d.ap()[:, c0:c0 + 256].rearrange("(kc p) c -> p kc c", p=128),
                          ld_sem[k], writes=[stgb[k]])
                E = cv_engs[bi % 2]
                src_ap = stg[k][:, :, :].rearrange("p a b -> p (a b)")
                if E is ACT:
                    P.op(E, lambda s=src_ap, o=cvt[k]: nc.scalar.copy(out=o[:, :], in_=s), reads=[stgb[k]], writes=[cvtb[k]])
                else:
                    P.op(E, lambda s=src_ap, o=cvt[k], e=E: e.h.tensor_copy(out=o[:, :], in_=s), reads=[stgb[k]], writes=[cvtb[k]])
                P.dma(POOL, wb_d.ap()[bi], cvt[k][:, :], st_sem[k], reads=[cvtb[k]])
        P.barrier()

    xb = [P.sb(f"xb{t}", [128, D], F32) for t in range(4)]
    xbb = [Buf() for _ in range(4)]
    xb_sem = [P.sem("xb") for _ in range(4)]
    xs = [P.sb(f"xs{t}", [128, D], BF16) for t in range(4)]
    xsb = [Buf() for _ in range(4)]
    hT = P.sb("hT", [128, 3, 8, T], BF16)
    hTb = [Buf() for _ in range(3)]
    kT = [P.sb("kT0", [128, 4, 3, T], BF16), P.sb("kT1", [128, 4, 3, T], BF16), P.sb("kT2", [128, 4, 5, T], BF16)]
    NSLOT = (3, 3, 5)
    kTb = [[Buf() for _ in range(NSLOT[g])] for g in range(3)]
    va = [P.sb(f"va{g}", [128, NSLOT[g], 4, 8, 65], BF16) for g in range(3)]
    vab = [[Buf() for _ in range(NSLOT[g])] for g in range(3)]
    qT = P.sb("qT", [128, 3, T], BF16)
    qTb = Buf()
    wbuf = [P.sb(f"wbuf{i}", [128, 8, 256], BF16) for i in range(4)]
    wbufb = [Buf() for _ in range(4)]
    wb_sem = [P.sem("wb") for _ in range(4)]
    mk = [P.sb(f"mk{i}", [128, MKW], BF16) for i in range(2)]
    mkbb = [Buf() for _ in range(2)]
    mk_sem = [P.sem("mk") for _ in range(2)]
    NU = 6
    uring = P.sb("uring", [128, NU, 512], BF16)
    uringb = [Buf() for _ in range(NU)]
    BT = P.sb("BT", [128, 4, T], BF16)
    BTb = [Buf() for _ in range(4)]
    AT = P.sb("AT", [128, 4, T], BF16)
    ATb = [Buf() for _ in range(4)]
    E0 = [P.sb(f"E0_{i}", [128, 512], BF16) for i in range(2)]
    E0hb = [[Buf(), Buf()] for _ in range(2)]
    Em = [P.sb(f"Em_{i}", [128, 512], BF16) for i in range(2)]
    Emhb = [[Buf(), Buf()] for _ in range(2)]
    Emb = [None, None]
    s12 = P.sb("s12", [128, T], F32)
    s12b = Buf()
    att = P.sb("att", [128, T], F32)
    attb = Buf()
    th = P.sb("th", [128, T], F32)
    thb = Buf()
    sgs = P.sb("sgs", [128, 4, T], BF16)
    sgsb = [Buf() for _ in range(4)]
    pooledT = Em[0]
    sgp = P.sb("sgp", [128, T], BF16)
    sgpb = Buf()
    den_sem = P.sem("den")
    merged = P.sb("merged", [128, 8, T], BF16)
    mergedb = [Buf() for _ in range(8)]
    gy, gyb = s12, s12b
    res = AT[:, :, :].rearrange("p a b -> p (a b)").bitcast(F32)
    xres0 = BT[:, :, :].rearrange("p a b -> p (a b)").bitcast(F32)
    xres1 = sgs[:, :, :].rearrange("p a b -> p (a b)").bitcast(F32)
    resb_l = ATb
    xres_bufs = [(xres0, BTb), (xres1, sgsb)]
    xres_sems = [P.sem("xres0"), P.sem("xres1")]

    def load_xres(c, ti):
        k = (ti + 1) % 2
        xr, xrb = xres_bufs[k]
        P.dma(POOL, xr[:, :], x_d.ap()[c * T + 128 * ti:c * T + 128 * ti + 128, :], xres_sems[k], writes=xrb)
    xres_sem = P.sem("xres")
    out_sems = [P.sem("out0"), P.sem("out1")]

    P.op(POOL, lambda: nc.gpsimd.memset(att[:, :], 1.0), writes=[attb])

    wstate = {"n": 0}

    def load_block(name):
        i = wstate["n"] % 4
        wstate["n"] += 1
        P.dma(SP, wbuf[i][:, :, :].rearrange("p a b -> p (a b)"), wb_d.ap()[BLK[name]], wb_sem[i], writes=[wbufb[i]])
        return wbuf[i], wbufb[i]

    def hslot(c):
        return c % 3

    def tok_ap(c, g, idx):
        s = hslot(c)
        if g == 0:
            return lambda dc: hT[:, s, dc, 128 * idx:128 * idx + 128]
        return lambda dc: hT[:, s, dc, :].rearrange("p (m r) -> p r m", r=4)[:, idx, :]

    def stageA1a_load(c):
        for t in range(4):
            P.dma(SP, xb[t][:, :], x_d.ap()[c * T + 128 * t:c * T + 128 * t + 128, :], xb_sem[t], writes=[xbb[t]])

    def stageA1a(c, load=True):
        if load:
            stageA1a_load(c)
        for t in range(4):
            P.op(ACT, lambda t=t: nc.scalar.activation(out=xs[t][:, :], in_=xb[t][:, :], func=AF.Square,
                                                       accum_out=stat[:, t:t + 1]),
                 reads=[xbb[t]], writes=[xsb[t], statb])
        P.op(DVE, lambda: nc.vector.tensor_scalar(out=stat[:, 4:8], in0=stat[:, 0:4], scalar1=1.0 / D, scalar2=EPS,
                                                  op0=ALU.mult, op1=ALU.add),
             reads=[statb], writes=[statb])
        P.op(POOL, lambda: nc.gpsimd.tensor_tensor(out=stat[:, 8:12], in0=stat[:, 4:8], in1=negh[:, 0:4], op=ALU.pow),
             reads=[statb, constb], writes=[statb])
        for t in range(4):
            P.op(POOL, lambda t=t: nc.gpsimd.tensor_tensor(out=xs[t][:, :], in0=xb[t][:, :],
                                                           in1=stat[:, 8 + t:9 + t].to_broadcast([128, D]), op=ALU.mult),
                 reads=[xbb[t], statb], writes=[xsb[t]])

    def evac_copy(i, out, in_, reads, writes):
        if i % 2 == 0:
            P.op(ACT, lambda: nc.scalar.copy(out=out, in_=in_), reads=reads, writes=writes)
        else:
            P.op(DVE, lambda: nc.vector.tensor_copy(out=out, in_=in_), reads=reads, writes=writes)

    def kv_stage(c, g):
        s = hslot(c)
        ks = c % NSLOT[g]
        for hb in range(2):
            wbk, wbb = load_block(f"K{g}{hb}")
            for seg in range(2):
                pair = 2 * hb + seg
                bank, bb = next_pj()
                for dc in range(8):
                    P.op(PE, lambda dc=dc, bank=bank, wbk=wbk, seg=seg: nc.tensor.matmul(
                        bank[:, 0:T], lhsT=wbk[:, dc, 128 * seg:128 * seg + 128], rhs=hT[:, s, dc, :],
                        start=(dc == 0), stop=(dc == 7)),
                         reads=[wbb, hTb[s]], writes=[bb], signal=(dc == 7))
                if g == 0:
                    src, dst = bank[:, 0:T], kT[0][:, pair, ks, :]
                else:
                    src = bank[:, 0:T].rearrange("p (m r) -> p r m", r=4)
                    dst = kT[g][:, pair, ks, :].rearrange("p (r m) -> p r m", r=4)
                evac_copy(pair, dst, src, [bb], [kTb[g][ks]])
        for hb in range(2):
            wbk, wbb = load_block(f"V{g}{hb}")
            for idx in range(4):
                bank, bb = next_pj()
                tf = tok_ap(c, g, idx)
                for dc in range(8):
                    P.op(PE, lambda dc=dc, bank=bank, wbk=wbk, tf=tf: nc.tensor.matmul(
                        bank[:, 0:256], lhsT=tf(dc), rhs=wbk[:, dc, :], start=(dc == 0), stop=(dc == 7)),
                         reads=[wbb, hTb[s]], writes=[bb], signal=(dc == 7))
                dst = va[g][:, ks, idx, 4 * hb:4 * hb + 4, 0:64]
                src = bank[:, 0:256].rearrange("p (h d) -> p h d", d=64)
                evac_copy(idx + 1, dst, src, [bb], [vab[g][ks]])
        P.op(POOL, lambda: nc.gpsimd.tensor_copy(out=va[g][:, ks, :, :, 64],
                                                 in_=valid[:, c:c + 1].unsqueeze(2).to_broadcast([128, 4, 8])),
             reads=[constb], writes=[vab[g][ks]])

    def stageA1b(c):
        s = hslot(c)
        for d2 in range(4):
            bank, bb = next_pj()
            bankh = bank[:, :].bitcast(BF16)
            for dd in range(2):
                dc = 2 * d2 + dd
                for t in range(4):
                    P.op(PE, lambda dc=dc, t=t, dd=dd, bankh=bankh: nc.tensor.transpose(
                        bankh[:, dd * 512 + 128 * t:dd * 512 + 128 * t + 128], xs[t][:, 128 * dc:128 * dc + 128], ident[:, :]),
                         reads=[xsb[t], constb], writes=[bb], signal=(dd == 1 and t == 3))
            for dd in range(2):
                dc = 2 * d2 + dd
                P.op(DVE, lambda dc=dc, dd=dd, bankh=bankh: nc.vector.tensor_scalar_mul(
                    out=hT[:, s, dc, :], in0=bankh[:, dd * 512:dd * 512 + 512], scalar1=gain[:, dc:dc + 1]),
                     reads=[bb, constb], writes=[hTb[s]])
        kv_stage(c, 2)

    def u_tile(tt):
        c, ti = tt // 4, tt % 4
        s = hslot(c)
        for hb in range(2):
            wbk, wbb = load_block(f"AI{hb}")
            bank, bb = next_pj()
            for dc in range(8):
                P.op(PE, lambda dc=dc, bank=bank, wbk=wbk: nc.tensor.matmul(
                    bank[:, 0:256], lhsT=hT[:, s, dc, 128 * ti:128 * ti + 128], rhs=wbk[:, dc, :],
                    start=(dc == 0), stop=(dc == 7)),
                     reads=[wbb, hTb[s]], writes=[bb], signal=(dc == 7))
            evac_copy(hb, uring[:, tt % NU, 256 * hb:256 * hb + 256], bank[:, 0:256], [bb], [uringb[tt % NU]])

    def u_tiles(tts):
        groups = {}
        for tt in tts:
            groups.setdefault(tt, None)
        for hb in range(2):
            wbk, wbb = load_block(f"AI{hb}")
            for tt in tts:
                c, ti = tt // 4, tt % 4
                s = hslot(c)
                bank, bb = next_pj()
                for dc in range(8):
                    P.op(PE, lambda dc=dc, bank=bank, wbk=wbk, s=s, ti=ti: nc.tensor.matmul(
                        bank[:, 0:256], lhsT=hT[:, s, dc, 128 * ti:128 * ti + 128], rhs=wbk[:, dc, :],
                        start=(dc == 0), stop=(dc == 7)),
                         reads=[wbb, hTb[s]], writes=[bb], signal=(dc == 7))
                evac_copy(tt + hb, uring[:, tt % NU, 256 * hb:256 * hb + 256], bank[:, 0:256], [bb], [uringb[tt % NU]])

    estate = {"n": 0}
    mstate = {"n": 0}

    def attention_pair(c, pr):
        hbanks = []
        for h in (2 * pr, 2 * pr + 1):
            banks = []
            pb = 64 * (h % 2)
            ob = h % 2
            o0, o0b, o12, o12b = o0s[ob], o0bs[ob], o12s[ob], o12bs[ob]
            mi = h % 2
            mkt, maskb = mk[mi], mkbb[mi]
            if mstate.get(mi) != h:
                P.dma(POOL, mkt[:, :], mk_d.ap()[h], mk_sem[mi], writes=[maskb])
                mstate[mi] = h
            tiles = [(c - 1, 3, 0, 64), (c, 0, 0, 192), (c, 1, 64, 320), (c, 2, 192, 448), (c, 3, 320, 512), (c + 1, 0, 448, 512)]
            first_o0 = True
            for half in range(2):
                rec = {"st": [], "mask": [], "pv": [], "maskb": maskb, "finish": None}
                col = 0
                for ti_, (cc, t_, q0, q1) in enumerate(tiles[3 * half:3 * half + 3]):
                    n = q1 - q0
                    ks = cc % 3
                    rec["st"].append((lambda stbank, col=col, n=n, ks=ks, t_=t_, q0=q0, q1=q1, pb=pb: nc.tensor.matmul(
                        stbank[:, col:col + n], lhsT=kT[0][pb:pb + 64, pr, ks, 128 * t_:128 * t_ + 128],
                        rhs=qT[pb:pb + 64, 0, q0:q1], start=True, stop=True), [kTb[0][ks], qTb]))
                    qq0 = (c * T + q0) - (cc * T + 128 * t_) + 128
                    rec["mask"].append((col, col + n, mkt[:, qq0 - 64:qq0 - 64 + n], None))
                    rec["pv"].append((lambda em, col=col, n=n, ks=ks, t_=t_, q0=q0, q1=q1, st_=first_o0, o0=o0, h=h: nc.tensor.matmul(
                        o0[0:65, q0:q1], lhsT=va[0][:, ks, t_, h, :], rhs=em[:, col:col + n],
                        start=st_, stop=False, skip_group_check=True), [vab[0][ks]], [o0b], col))
                    first_o0 = False
                    col += n
                rec["ncols"] = col
                banks.append(rec)
            first_o12 = True
            for g, dcs in ((1, (-1, 0, 1)), (2, (-2, -1, 0, 1, 2))):
                for di, dc_ in enumerate(dcs):
                    cc = c + dc_
                    ks = cc % NSLOT[g]
                    rec = {"st": [], "mask": [], "pv": [], "maskb": maskb, "finish": None, "ncols": 512}
                    for r4 in range(4):
                        rec["st"].append((lambda stbank, r4=r4, ks=ks, g=g, pb=pb: nc.tensor.matmul(
                            stbank[:, 128 * r4:128 * r4 + 128], lhsT=kT[g][pb:pb + 64, pr, ks, 128 * r4:128 * r4 + 128],
                            rhs=qT[pb:pb + 64, g, 128 * r4:128 * r4 + 128], start=True, stop=True), [kTb[g][ks], qTb]))
                        rec["pv"].append((lambda em, r4=r4, ks=ks, g=g, st_=first_o12, o12=o12, h=h: nc.tensor.matmul(
                            o12[0:65, 128 * r4:128 * r4 + 128], lhsT=va[g][:, ks, r4, h, :], rhs=em[:, 128 * r4:128 * r4 + 128],
                            start=st_, stop=False, skip_group_check=True), [vab[g][ks]], [o12b], 128 * r4))
                        first_o12 = False
                    moff = (256 + 128 - 128 * dc_) if g == 1 else (640 + 128 * di)
                    for hf_ in range(2):
                        rec["mask"].append((256 * hf_, 256 * hf_ + 256, mkt[:, moff:moff + 128].unsqueeze(1).to_broadcast([128, 2, 128]), 128))
                    banks.append(rec)

            def finish(h=h, pb=pb, o0=o0, o0b=o0b, o12=o12, o12b=o12b):
                P.op(ACT, lambda: nc.scalar.copy(out=s12[0:65, :].rearrange("p (m r) -> p r m", r=4),
                                                 in_=o12[0:65, :].rearrange("p (r m) -> p r m", r=4)),
                     reads=[o12b], writes=[s12b])
                if c == DBG_CHUNK:
                    dump(f"s12_h{h}", s12[0:65, :], [s12b])
                P.op(DVE, lambda: nc.vector.tensor_tensor(out=BT[pb:pb + 64, pr, :], in0=o0[0:64, :], in1=s12[0:64, :], op=ALU.add),
                     reads=[o0b, s12b], writes=[BTb[pr]])
                P.op(DVE, lambda: nc.vector.tensor_tensor(out=s12[64:65, :], in0=o0[64:65, :], in1=s12[64:65, :], op=ALU.add),
                     reads=[o0b, s12b], writes=[s12b])
                row = 32 * (h // 4) + (h % 4)
                P.dma(POOL, att[row:row + 1, :], s12[64:65, :], den_sem, reads=[s12b], writes=[attb])
                hn = (h + 2) % 8
                P.dma(POOL, mk[h % 2][:, :], mk_d.ap()[hn], mk_sem[h % 2], writes=[mkbb[h % 2]])
                mstate[h % 2] = hn
            banks[-1]["finish"] = finish
            hbanks.append(banks)

        def emit_st(stage):
            recs = [hbanks[0][stage], hbanks[1][stage]]
            for rec in recs:
                rec["stbank"], rec["stbb"] = next_st()
            nst = len(recs[0]["st"])
            for i in range(nst):
                for rec in recs:
                    fn, reads = rec["st"][i]
                    P.op(PE, lambda fn=fn, rec=rec: fn(rec["stbank"]), reads=reads, writes=[rec["stbb"]], signal=(i == nst - 1))

        def emit_rest(rec):
            k = estate["n"] % 2
            estate["n"] += 1
            stbank, stbb, ncols, maskb = rec["stbank"], rec["stbb"], rec["ncols"], rec["maskb"]
            for hf_ in range(2):
                lo, hi = 256 * hf_, min(256 * hf_ + 256, ncols)
                if hi <= lo:
                    continue
                P.op(ACT, lambda lo=lo, hi=hi: nc.scalar.activation(out=E0[k][:, lo:hi], in_=stbank[:, lo:hi], func=AF.Exp, scale=0.125),
                     reads=[stbb], writes=[E0hb[k][hf_]])
                for (c0, c1, m_ap, shape3) in rec["mask"]:
                    if not (lo <= c0 and c1 <= hi):
                        continue
                    if shape3 is None:
                        o_ap, i_ap = Em[k][:, c0:c1], E0[k][:, c0:c1]
                    else:
                        o_ap = Em[k][:, c0:c1].rearrange("p (a b) -> p a b", b=shape3)
                        i_ap = E0[k][:, c0:c1].rearrange("p (a b) -> p a b", b=shape3)
                    P.op(DVE, lambda o_ap=o_ap, i_ap=i_ap, m_ap=m_ap: nc.vector.tensor_tensor(out=o_ap, in0=i_ap, in1=m_ap, op=ALU.mult),
                         reads=[E0hb[k][hf_], maskb], writes=[Emhb[k][hf_]])
                pvs = [p for p in rec["pv"] if lo <= p[3] < hi]
                for i, (fn, reads, writes, _c0) in enumerate(pvs):
                    P.op(PE, lambda fn=fn: fn(Em[k]), reads=[Emhb[k][hf_]] + reads, writes=writes, signal=(i == len(pvs) - 1))
            if rec["finish"] is not None:
                rec["finish"]()

        nstage = len(hbanks[0])
        emit_st(0)
        for n in range(nstage):
            if n + 1 < nstage:
                emit_st(n + 1)
            emit_rest(hbanks[0][n])
            emit_rest(hbanks[1][n])

    def attention_recip(c):
        P.op(DVE, lambda: nc.vector.reciprocal(out=att[0:36, :], in_=att[0:36, :]), reads=[attb], writes=[attb])

    def attention_normalise(c):
        for pr in range(4):
            base = 32 * (pr // 2)
            a = pr % 2
            bcbank, bcb = next_pj()
            P.op(PE, lambda: nc.tensor.matmul(bcbank[:, 0:T], lhsT=sel2[base:base + 4, 128 * a:128 * a + 128],
                                              rhs=att[base:base + 4, :], start=True, stop=True),
                 reads=[attb, constb], writes=[bcb])
            P.op(DVE, lambda: nc.vector.scalar_tensor_tensor(out=th[:, :], in0=bcbank[:, 0:T], scalar=0.5,
                                                             in1=sgs[:, pr, :], op0=ALU.mult, op1=ALU.mult),
                 reads=[bcb, sgsb[pr]], writes=[thb])
            P.op(POOL, lambda: nc.gpsimd.tensor_tensor(out=BT[:, pr, :], in0=BT[:, pr, :], in1=th[:, :], op=ALU.mult),
                 reads=[thb, BTb[pr]], writes=[BTb[pr]])

    def silu_half(bank, bb, dst, dstb):
        P.op(ACT, lambda: nc.scalar.activation(out=th[:, :], in_=bank[:, 0:T], func=AF.Tanh, scale=0.5), reads=[bb], writes=[thb])
        P.op(DVE, lambda: nc.vector.scalar_tensor_tensor(out=dst, in0=th[:, :], scalar=1.0, in1=bank[:, 0:T],
                                                         op0=ALU.add, op1=ALU.mult),
             reads=[thb, bb], writes=[dstb])

    def proj_fm(wbk, wbb, seg, s):
        bank, bb = next_pj()
        for dc in range(8):
            P.op(PE, lambda dc=dc: nc.tensor.matmul(bank[:, 0:T], lhsT=wbk[:, dc, 128 * seg:128 * seg + 128],
                                                    rhs=hT[:, s, dc, :], start=(dc == 0), stop=(dc == 7)),
                 reads=[wbb, hTb[s]], writes=[bb], signal=(dc == 7))
        return bank, bb

    def stageB(c):
        s = hslot(c)
        rot["wide"] = False
        for pr in range(4):
            wa, wab = load_block(f"QA{pr}")
            wq, wqb = load_block(f"QB{pr}")
            for g in range(3):
                wbk, wbb, seg = (wa, wab, g) if g < 2 else (wq, wqb, 0)
                bank, bb = proj_fm(wbk, wbb, seg, s)
                if g == 0:
                    src, dst = bank[:, 0:T], qT[:, 0, :]
                else:
                    src = bank[:, 0:T].rearrange("p (m r) -> p r m", r=4)
                    dst = qT[:, g, :].rearrange("p (r m) -> p r m", r=4)
                evac_copy(1 if g < 2 else 0, dst, src, [bb], [qTb])
            bank, bb = proj_fm(wq, wqb, 1, s)
            silu_half(bank, bb, sgs[:, pr, :], sgsb[pr])
            if c == DBG_CHUNK:
                dump(f"qT_p{pr}", qT[:, :, :], [qTb])
                dump(f"sg_p{pr}", sgs[:, pr, :], [sgsb[pr]])
                if pr == 0:
                    dump("hT", hT[:, s, :, :], [hTb[s]])
                    for g in range(3):
                        dump(f"kT{g}", kT[g][:, :, :, :], kTb[g])
                        dump(f"va{g}", va[g][:, :, :, :, :].rearrange("p a b c d -> p (a b c d)"), vab[g])
            attention_pair(c, pr)
        rot["wide"] = True
        attention_recip(c)
        pbufs = [(Em[0], Emhb[0]), (Em[1], Emhb[1]), (E0[0], E0hb[0]), (E0[1], E0hb[1])]
        bands = []
        for g in range(4):
            bank, bb = next_pj()
            first = True
            for tp in range(-1, 5):
                tt = 4 * c + tp
                q0, q1 = max(0, 128 * tp - 8), min(T, 128 * tp + 136)
                qq0 = q0 - (128 * tp - 8)
                kind = 1
                if c == MAIN0 and tp == 0:
                    kind = 0
                if c == MAIN1 - 1 and tp == 3:
                    kind = 2
                P.op(PE, lambda tt=tt, q0=q0, q1=q1, qq0=qq0, kind=kind, g=g, first=first, bank=bank: nc.tensor.matmul(
                    bank[:, q0:q1], lhsT=uring[:, tt % NU, 128 * g:128 * g + 128],
                    rhs=band[:, 4 * kind + g, qq0:qq0 + (q1 - q0)], start=first, stop=False, skip_group_check=True),
                     reads=[uringb[tt % NU], constb], writes=[bb], signal=(tp == 4))
                first = False
            bands.append((bank, bb))
        for g in range(4):
            bank, bb = bands[g]
            evac_copy(g, pbufs[g][0][:, :], bank[:, 0:T], [bb], pbufs[g][1])
            if c == DBG_CHUNK:
                dump(f"pooledT_g{g}", pbufs[g][0][:, :], pbufs[g][1])
        mg = []
        for g in range(4):
            mbank, mbb = next_pj()
            P.op(PE, lambda g=g, mbank=mbank: nc.tensor.matmul(mbank[:, 0:T], lhsT=wpool[:, g, :], rhs=pbufs[g][0][:, :], start=True, stop=True),
                 reads=pbufs[g][1] + [constb], writes=[mbb])
            if g % 2 == 0:
                wg, wgb = load_block(f"AG{g // 2}")
            gbank, gbb = proj_fm(wg, wgb, g % 2, s)
            mg.append((mbank, mbb, gbank, gbb))
        for g in range(4):
            mbank, mbb, gbank, gbb = mg[g]
            silu_half(gbank, gbb, sgp[:, :], sgpb)
            P.op(DVE, lambda g=g, mbank=mbank: nc.vector.scalar_tensor_tensor(out=AT[:, g, :], in0=mbank[:, 0:T], scalar=psh[:, g:g + 1],
                                                                             in1=sgp[:, :], op0=ALU.mult, op1=ALU.mult),
                 reads=[mbb, sgpb, constb], writes=[ATb[g]])
        attention_normalise(c)
        if c == DBG_CHUNK:
            dump("BT", BT[:, :, :], BTb)
            dump("uring", uring[:, :, :], uringb)
        if c == DBG_CHUNK:
            dump("AT", AT[:, :, :], ATb)
        load_xres(c, 0)
        for q in range(4):
            wp, wpb_ = load_block(f"PAB{q}")
            wga, wgab = load_block(f"GA{q}")
            wgb_, wgbb = load_block(f"GB{q}")
            for seg in range(2):
                oc = 2 * q + seg
                for br in range(2):
                    ybank, ybb = next_pj()
                    src_t, src_b = (AT, ATb) if br == 0 else (BT, BTb)
                    for kc in range(4):
                        P.op(PE, lambda kc=kc, br=br, src_t=src_t: nc.tensor.matmul(
                            ybank[:, 0:T], lhsT=wp[:, 4 * br + kc, 128 * seg:128 * seg + 128], rhs=src_t[:, kc, :],
                            start=(kc == 0), stop=(kc == 3)),
                             reads=[wpb_, src_b[kc]], writes=[ybb], signal=(kc == 3))
                    gw, gwb = (wga, wgab) if br == 0 else (wgb_, wgbb)
                    gbank, gbb = proj_fm(gw, gwb, seg, s)
                    P.op(ACT, lambda br=br, oc=oc, gbank=gbank: nc.scalar.activation(
                        out=th[:, :], in_=gbank[:, 0:T], func=AF.Tanh, scale=0.5, bias=hbias[:, 8 * br + oc:8 * br + oc + 1]),
                         reads=[gbb, constb], writes=[thb])
                    if br == 0:
                        P.op(DVE, lambda ybank=ybank: nc.vector.scalar_tensor_tensor(
                            out=gy[:, :], in0=th[:, :], scalar=1.0, in1=ybank[:, 0:T], op0=ALU.add, op1=ALU.mult),
                             reads=[thb, ybb], writes=[gyb])
                    else:
                        P.op(DVE, lambda ybank=ybank: nc.vector.scalar_tensor_tensor(
                            out=att[:, :], in0=th[:, :], scalar=1.0, in1=ybank[:, 0:T], op0=ALU.add, op1=ALU.mult),
                             reads=[thb, ybb], writes=[attb])
                        P.op(POOL, lambda oc=oc: nc.gpsimd.tensor_tensor(out=merged[:, oc, :], in0=gy[:, :], in1=att[:, :], op=ALU.add),
                             reads=[gyb, attb], writes=[mergedb[oc]])
        if c == DBG_CHUNK:
            dump("merged", merged[:, :, :], mergedb)
        wo = [load_block(f"WO{q}") for q in range(4)]
        load_xres(c, 1)
        for ti in range(4):
            row0 = (c - MAIN0) * T + 128 * ti
            xr, xrb = xres_bufs[(ti + 1) % 2]
            for half in range(2):
                obank, obb = next_pj()
                first = True
                for qq in range(2):
                    wbk, wbb = wo[2 * half + qq]
                    for oc in range(8):
                        last = (qq == 1 and oc == 7)
                        P.op(PE, lambda oc=oc, qq=qq, wbk=wbk, first=first, obank=obank: nc.tensor.matmul(
                            obank[:, 256 * qq:256 * qq + 256], lhsT=merged[:, oc, 128 * ti:128 * ti + 128], rhs=wbk[:, oc, :],
                            start=first, stop=False, skip_group_check=True),
                             reads=[wbb, mergedb[oc]], writes=[obb], signal=last)
                        first = False
                P.op(DVE, lambda half=half, obank=obank, xr=xr: nc.vector.scalar_tensor_tensor(
                    out=res[:, 512 * half:512 * half + 512], in0=obank[:, 0:512], scalar=0.5,
                    in1=xr[:, 512 * half:512 * half + 512], op0=ALU.mult, op1=ALU.add),
                     reads=[obb] + xrb, writes=resb_l)
            P.op(ACT, lambda: nc.scalar.activation(out=th[:, :].bitcast(BF16), in_=res[:, :], func=AF.Square, accum_out=fstat[:, 0:1]),
                 reads=resb_l, writes=[thb, fstatb])
            P.op(DVE, lambda: nc.vector.tensor_scalar(out=fstat[:, 1:2], in0=fstat[:, 0:1], scalar1=1.0 / D, scalar2=EPS,
                                                      op0=ALU.mult, op1=ALU.add),
                 reads=[fstatb], writes=[fstatb])
            P.op(POOL, lambda: nc.gpsimd.tensor_tensor(out=fstat[:, 2:3], in0=fstat[:, 1:2], in1=negh[:, 0:1], op=ALU.pow),
                 reads=[fstatb, constb], writes=[fstatb])
            P.op(DVE, lambda xr=xr: nc.vector.scalar_tensor_tensor(out=xr[:, :], in0=res[:, :], scalar=fstat[:, 2:3], in1=fgb[:, :],
                                                                   op0=ALU.mult, op1=ALU.mult),
                 reads=resb_l + [fstatb, constb], writes=xrb)
            P.dma(POOL, out_d.ap()[row0:row0 + 128, :], xr[:, :], out_sems[(ti + 1) % 2], reads=xrb)
            if ti + 2 < 4:
                load_xres(c, ti + 2)

    P.op(POOL, lambda: nc.gpsimd.memset(kT[2][:, :, 0:2, :], 0.0), writes=[kTb[2][0], kTb[2][1]])
    P.op(POOL, lambda: nc.gpsimd.memset(va[2][:, 0:2, :, :, :], 0.0), writes=[vab[2][0], vab[2][1]])
    for g in (0, 1):
        P.op(POOL, lambda g=g: nc.gpsimd.memset(kT[g][:, :, 1, :], 0.0), writes=[kTb[g][1]])
        P.op(POOL, lambda g=g: nc.gpsimd.memset(va[g][:, 1, :, :, :], 0.0), writes=[vab[g][1]])
    P.op(POOL, lambda: nc.gpsimd.memset(uring[:, 7 % NU, :], 0.0), writes=[uringb[7 % NU]])
    stageA1a(2); stageA1b(2)
    stageA1a(3); stageA1b(3)
    for g in (0, 1):
        kv_stage(2, g)
    u_tiles([8])
    for i in range(MAIN0, MAIN1):
        if i + 2 < NCH:
            stageA1a_load(i + 2)
        kv_stage(i + 1, 0)
        if i + 2 < NCH:
            stageA1a(i + 2, load=False)
        kv_stage(i + 1, 1)
        u_tiles([4 * i + 1, 4 * i + 2, 4 * i + 3, 4 * i + 4])
        if i + 2 < NCH:
            stageA1b(i + 2)
        stageB(i)
    for osem in out_sems:
        POOL.h.wait_ge(osem.h, osem.val)
        SP.h.wait_ge(osem.h, osem.val)
    if debug and dbg_sem.val:
        SP.h.wait_ge(dbg_sem.h, dbg_sem.val)
    es.close()
    return nc


def _t5_bucket_np(rel):
    nb = 16
    ret = (rel > 0).astype(np.int32) * nb
    n = np.abs(rel)
    max_exact = nb // 2
    nf = np.maximum(n, 1).astype(np.float32)
    large = max_exact + (np.log(nf / np.float32(max_exact)) / np.float32(math.log(1024 / max_exact))
                         * np.float32(nb - max_exact)).astype(np.int32)
    large = np.minimum(large, nb - 1)
    return ret + np.where(n < max_exact, n, large)


def _host_consts(hf):
    sign = 1 if hf == 0 else -1
    ident = np.eye(128, dtype=np.float32).astype(ml_dtypes.bfloat16)
    onehot = np.zeros((33, 7, 512), np.float32)
    idx = np.arange(512)
    for g in range(2):
        j = 128 - idx
        inwin = (idx >= 64) & (idx <= 192)
        b = _t5_bucket_np((sign * j * DIL[g]).astype(np.int32))
        b = np.where(inwin, b, 32)
        onehot[b, g, idx] = 1.0
    for di, dc_ in enumerate((-2, -1, 0, 1, 2)):
        delta = 128 - idx
        j = 32 * dc_ + delta // 4
        ok = (idx >= 1) & (idx <= 255) & (delta % 4 == 0) & (np.abs(j) <= 64)
        b = _t5_bucket_np((sign * j * DIL[2]).astype(np.int32))
        b = np.where(ok, b, 32)
        onehot[b, 2 + di, idx] = 1.0
    band = np.zeros((128, 3, 4, 144), np.float32)
    k = np.arange(128)[:, None]
    qp = (np.arange(144) - 8)[None, :]
    for g, w in enumerate((2, 4, 8, 16)):
        hw = w // 2
        if hf == 0:
            inw = ((k >= qp - hw) & (k <= qp + hw - 1)).astype(np.float32)
            cnt_first = (np.minimum(np.maximum(qp, 0), hw) + hw).astype(np.float32)
        else:
            inw = ((k >= qp - hw + 1) & (k <= qp + hw)).astype(np.float32)
            cnt_first = (hw + np.minimum(hw, np.maximum(qp, 0) + 1)).astype(np.float32)
        cnt_first = np.minimum(cnt_first, w)
        eye = (k == qp).astype(np.float32)
        cnt_mid = np.full((1, 144), float(w), np.float32)
        cnts = [cnt_first, cnt_mid, cnt_mid]
        for kind in range(3):
            band[:, kind, g, :] = inw / cnts[kind] - eye
    band = band.reshape(128, 12 * 144).astype(ml_dtypes.bfloat16)
    valid = np.ones((128, NCH), np.float32)
    valid[:, 0:2] = 0.0
    sel2 = np.zeros((128, 256), np.float32)
    for base in (0, 32):
        for r in range(4):
            for a in range(2):
                for m in range(128):
                    if r == 2 * a + m // 64:
                        sel2[base + r, 128 * a + m] = 1.0
    return ident, onehot.reshape(33, 7 * 512), band, valid, sel2


_CACHE = {}


def kernel(x, norm_gain, w_in, b_gate, rel_bias, w_pool, pool_scale, w_proj_a, w_proj_b, w_out, final_gain):
    x = np.asarray(x, np.float32)
    if "nc" not in _CACHE:
        _CACHE["nc"] = build_program()
    nc = _CACHE["nc"]
    common = {
        "w_in": np.ascontiguousarray(np.asarray(w_in, np.float32)[0]),
        "w_proj_a": np.ascontiguousarray(np.asarray(w_proj_a, np.float32)[0]),
        "w_proj_b": np.ascontiguousarray(np.asarray(w_proj_b, np.float32)[0]),
        "w_out": np.ascontiguousarray(np.asarray(w_out, np.float32)[0]),
        "w_pool": np.ascontiguousarray(np.asarray(w_pool, np.float32)[0]),
        "norm_gain": np.ascontiguousarray(np.asarray(norm_gain, np.float32)[0]),
        "final_gain": np.ascontiguousarray(np.asarray(final_gain, np.float32)),
        "b_gate": np.ascontiguousarray(np.asarray(b_gate, np.float32)[0]),
        "pool_scale": np.ascontiguousarray(np.asarray(pool_scale, np.float32)[0]),
        "rel_bias": np.ascontiguousarray(np.asarray(rel_bias, np.float32)),
    }
    in_maps = []
    for core in range(NCORES):
        b, hf = core // 2, core % 2
        xe = np.zeros((EXT, D), np.float32)
        if hf == 0:
            xe[1024:EXT] = x[b, 0:EXT - 1024]
        else:
            xe[1024:EXT] = x[b, SEQ - (EXT - 1024):SEQ][::-1]
        ident, onehot, band, valid, sel2 = _host_consts(hf)
        m = dict(common)
        m.update({"x": xe, "ident": ident, "onehot": onehot, "band": band, "valid": valid, "sel2": sel2})
        in_maps.append(m)
    res = run_bass_kernel_spmd(nc, in_maps, core_ids=list(range(NCORES)))
    out = np.empty((4, SEQ, D), np.float32)
    for core in range(NCORES):
        b, hf = core // 2, core % 2
        o = np.asarray(res.results[core]["out"], np.float32)
        out[b, hf * 4096:(hf + 1) * 4096] = o if hf == 0 else o[::-1]
    return out
```
